# Optimizing a Trainium2 kernel written in Bass

```python
import jax
import jax.numpy as jnp
from jax import lax
import numpy as np

D_MODEL = 4096
BATCH = 16
SEQ = 256
DEPTH = 1
DEC_BATCH = 8
DEC_SEQ = 4096
PAST_LEN = 512

GRID_W = 64
MIX_WIDTH = D_MODEL
M_WIDTH = MIX_WIDTH // 2
F_WIDTH = MIX_WIDTH - M_WIDTH
M_HEADS = 4
M_DV = M_WIDTH // M_HEADS
M_DQK = M_DV // 2
F_GROUPS = 8
F_CG = F_WIDTH // F_GROUPS
N_GATES = 4 * M_HEADS
N_IN = 2 * M_HEADS * M_DQK + 3 * M_WIDTH + N_GATES + 2 * F_WIDTH
CHUNK = 64
EPS = 1e-6

kernel_name = 'hybrid_mlstm_fnet_prefix_diffusion_step'


def _rmsnorm(x, w):
    xf = x.astype(jnp.float32)
    y = xf * lax.rsqrt(jnp.mean(xf * xf, axis=-1, keepdims=True) + EPS)
    return (y * w.astype(jnp.float32)).astype(x.dtype)


def _split_in(proj):
    sizes = (M_HEADS * M_DQK, M_HEADS * M_DQK, M_WIDTH, M_WIDTH, M_WIDTH, N_GATES, F_WIDTH, F_WIDTH)
    outs = []
    start = 0
    for s in sizes:
        outs.append(proj[..., start:start + s])
        start += s
    return outs


def _modulation(cvec, w_ada, b_ada):
    mod = jax.nn.silu(cvec) @ w_ada + b_ada
    shift, scale, gate = jnp.split(mod, 3, axis=-1)
    return shift, scale, gate


def _mlstm_chunkwise(q, k, v, ig, lf, C0, n0, m0):
    bsz, nh, T, _ = q.shape
    dv = v.shape[-1]
    nc = T // CHUNK

    def to_chunks(a):
        a = a.reshape((bsz, nh, nc, CHUNK) + a.shape[3:])
        return jnp.moveaxis(a, 2, 0)

    causal = jnp.tril(jnp.ones((CHUNK, CHUNK), dtype=bool))

    def step(carry, inp):
        C, n, m = carry
        qb, kb, vb, ib, fb = inp
        b = jnp.cumsum(fb, axis=-1)
        a = b + m[..., None]
        logw = jnp.where(causal, b[..., :, None] - b[..., None, :] + ib[..., None, :], -jnp.inf)
        mt = jnp.maximum(a, jnp.max(logw, axis=-1))
        s = jnp.einsum('bhtd,bhsd->bhts', qb, kb) * jnp.exp(logw - mt[..., None])
        inter = jnp.exp(a - mt)
        num = jnp.einsum('bhts,bhse->bhte', s, vb) + inter[..., None] * jnp.einsum('bhtd,bhde->bhte', qb, C)
        den = jnp.sum(s, axis=-1) + inter * jnp.einsum('bhtd,bhd->bht', qb, n)
        h = num / jnp.maximum(jnp.abs(den), jnp.exp(-mt))[..., None]
        bl = b[..., -1]
        g = bl[..., None] - b + ib
        m_new = jnp.maximum(bl + m, jnp.max(g, axis=-1))
        decay = jnp.exp(bl + m - m_new)
        kw = kb * jnp.exp(g - m_new[..., None])[..., None]
        C_new = decay[..., None, None] * C + jnp.einsum('bhsd,bhse->bhde', kw, vb)
        n_new = decay[..., None] * n + jnp.sum(kw, axis=2)
        return (C_new, n_new, m_new), h

    carry0 = (C0.astype(jnp.float32), n0.astype(jnp.float32), m0.astype(jnp.float32))
    xs = (to_chunks(q), to_chunks(k), to_chunks(v), to_chunks(ig), to_chunks(lf))
    (C, n, m), hc = lax.scan(step, carry0, xs)
    h = jnp.moveaxis(hc, 0, 2).reshape(bsz, nh, T, dv)
    return h, (C, n, m)


def _mixer_layer(x, shift, scale, gate, norm_w, w_in, b_gates, hnorm_w, w_four, w_out,
                 init_fwd, init_bwd, grid_rows):
    bsz, T, _ = x.shape
    f32 = jnp.float32
    h = (_rmsnorm(x, norm_w) * (1 + scale[:, None, :]) + shift[:, None, :]).astype(x.dtype)
    proj = h @ w_in
    q, k, v, o, z_m, gt, u, z_f = _split_in(proj)

    def heads(a, d):
        return a.reshape(bsz, T, M_HEADS, d).transpose(0, 2, 1, 3).astype(f32)
    q = heads(q, M_DQK)
    k = heads(k, M_DQK) * (M_DQK ** -0.5)
    v = heads(v, M_DV)
    gt = (gt.astype(f32) + b_gates.astype(f32)).reshape(bsz, T, 4, M_HEADS).transpose(2, 0, 3, 1)
    ig_f, lf_f = gt[0], jax.nn.log_sigmoid(gt[1])
    ig_b, lf_b = gt[2], jax.nn.log_sigmoid(gt[3])
    h_f, st_f = _mlstm_chunkwise(q, k, v, ig_f, lf_f, *init_fwd)

    def rev(a):
        return jnp.flip(a, axis=2)
    h_b, st_b = _mlstm_chunkwise(rev(q), rev(k), rev(v), rev(ig_b), rev(lf_b), *init_bwd)
    hm = h_f + rev(h_b)
    hm = hm * lax.rsqrt(jnp.mean(hm * hm, axis=-1, keepdims=True) + EPS)
    hm = hm.transpose(0, 2, 1, 3).reshape(bsz, T, M_WIDTH)
    m_out = (hm * hnorm_w.astype(f32) * jax.nn.sigmoid(o.astype(f32))).astype(x.dtype) * jax.nn.silu(z_m)

    uf = u.astype(f32)
    if grid_rows is None:
        uf = uf.reshape(bsz, T, F_GROUPS, F_CG)
        mixed = jnp.real(jnp.fft.fftn(uf, axes=(1, 3), norm='ortho'))
    else:
        uf = uf.reshape(bsz, grid_rows, GRID_W, F_GROUPS, F_CG)
        mixed = jnp.real(jnp.fft.fftn(uf, axes=(1, 2, 4), norm='ortho')).reshape(bsz, T, F_GROUPS, F_CG)
    f_out = jnp.einsum('btgc,gcd->btgd', mixed.astype(x.dtype), w_four).reshape(bsz, T, F_WIDTH)
    f_out = f_out * jax.nn.silu(z_f)

    out = jnp.concatenate([m_out, f_out], axis=-1) @ w_out
    y = (x + gate[:, None, :] * out).astype(x.dtype)
    return y, st_f, st_b


def setup_inputs(seed: int = 0) -> dict:
    key = jax.random.key(seed)
    ks = jax.random.split(key, 16)
    nrm = jax.random.normal
    f32 = jnp.float32
    x_prompt = nrm(ks[0], (BATCH, SEQ, D_MODEL), f32)
    x_sample = nrm(ks[1], (DEC_BATCH, DEC_SEQ, D_MODEL), f32)
    c = nrm(ks[2], (DEC_BATCH, D_MODEL), f32)
    state_C = 0.05 * nrm(ks[3], (DEC_BATCH, DEPTH, 2, M_HEADS, M_DQK, M_DV), f32)
    state_n = 0.5 * nrm(ks[4], (DEC_BATCH, DEPTH, 2, M_HEADS, M_DQK), f32)
    state_m = 0.5 * nrm(ks[5], (DEC_BATCH, DEPTH, 2, M_HEADS), f32)
    c_ctx = nrm(ks[6], (D_MODEL,), f32)
    w_ada = 0.5 * (D_MODEL ** -0.5) * nrm(ks[7], (DEPTH, D_MODEL, 3 * D_MODEL), f32)
    b_ada = 0.02 * nrm(ks[8], (DEPTH, 3 * D_MODEL), f32)
    norm_w = 1.0 + 0.02 * nrm(ks[9], (DEPTH, D_MODEL), f32)
    w_in = (D_MODEL ** -0.5) * nrm(ks[10], (DEPTH, D_MODEL, N_IN), f32)
    f_bias = jnp.linspace(3.0, 6.0, M_HEADS, dtype=f32)
    i_bias = jnp.zeros((M_HEADS,), f32)
    base = jnp.concatenate([i_bias, f_bias, i_bias, f_bias])
    b_gates = base[None, :] + 0.1 * nrm(ks[11], (DEPTH, N_GATES), f32)
    hnorm_w = 1.0 + 0.02 * nrm(ks[12], (DEPTH, M_WIDTH), f32)
    w_four = (F_CG ** -0.5) * nrm(ks[13], (DEPTH, F_GROUPS, F_CG, F_CG), f32)
    w_out = (MIX_WIDTH ** -0.5) * nrm(ks[14], (DEPTH, MIX_WIDTH, D_MODEL), f32)
    final_norm_w = 1.0 + 0.02 * nrm(ks[15], (D_MODEL,), f32)
    return {'x_prompt': x_prompt, 'x_sample': x_sample, 'c': c,
            'state_C': state_C, 'state_n': state_n, 'state_m': state_m,
            'c_ctx': c_ctx, 'w_ada': w_ada, 'b_ada': b_ada, 'norm_w': norm_w,
            'w_in': w_in, 'b_gates': b_gates, 'hnorm_w': hnorm_w, 'w_four': w_four,
            'w_out': w_out, 'final_norm_w': final_norm_w}


def reference(x_prompt, x_sample, c, state_C, state_n, state_m, c_ctx, w_ada, b_ada, norm_w,
              w_in, b_gates, hnorm_w, w_four, w_out, final_norm_w):
    f32 = jnp.float32
    bp = x_prompt.shape[0]
    rows = x_sample.shape[1] // GRID_W
    zero_state = (jnp.zeros((bp, M_HEADS, M_DQK, M_DV), f32),
                  jnp.zeros((bp, M_HEADS, M_DQK), f32),
                  jnp.zeros((bp, M_HEADS), f32))
    xp = x_prompt
    xs = x_sample
    Cs, ns, ms = [], [], []
    for l in range(DEPTH):
        sh_p, sc_p, g_p = _modulation(c_ctx[None, :], w_ada[l], b_ada[l])
        xp, st_f, st_b = _mixer_layer(xp, sh_p, sc_p, g_p, norm_w[l], w_in[l], b_gates[l], hnorm_w[l],
                                      w_four[l], w_out[l], zero_state, zero_state, None)
        Cs.append(jnp.stack([st_f[0], st_b[0]], axis=1))
        ns.append(jnp.stack([st_f[1], st_b[1]], axis=1))
        ms.append(jnp.stack([st_f[2], st_b[2]], axis=1))
        sh_s, sc_s, g_s = _modulation(c, w_ada[l], b_ada[l])
        init_f = (state_C[:, l, 0], state_n[:, l, 0], state_m[:, l, 0])
        init_b = (state_C[:, l, 1], state_n[:, l, 1], state_m[:, l, 1])
        xs, _, _ = _mixer_layer(xs, sh_s, sc_s, g_s, norm_w[l], w_in[l], b_gates[l], hnorm_w[l],
                                w_four[l], w_out[l], init_f, init_b, rows)
    y_prompt = _rmsnorm(xp, final_norm_w)
    y_sample = _rmsnorm(xs, final_norm_w)
    new_C = jnp.stack(Cs, axis=1)
    new_n = jnp.stack(ns, axis=1)
    new_m = jnp.stack(ms, axis=1)
    return (y_prompt, y_sample, new_C, new_n, new_m)
```

```python
import numpy as np
from contextlib import ExitStack
import concourse.bass as bass
import concourse.mybir as mybir
from concourse.bass_utils import run_bass_kernel_spmd

F32 = mybir.dt.float32
BF16 = mybir.dt.bfloat16
AF = mybir.ActivationFunctionType
ALU = mybir.AluOpType
AX = mybir.AxisListType

D = 4096
NIN = 12304
TS = 4096
TP = 256
NTOK = TS + 2 * TP
NT = NTOK // 128
KC = D // 128
EPS = 1e-6
H = 4
DK = 256
DV = 512
C_Q, C_K, C_V, C_O, C_ZM, C_G, C_U, C_ZF = 0, 1024, 2048, 4096, 6144, 8192, 8208, 10256


class Res:
    __slots__ = ("w", "wx", "r", "name")

    def __init__(self, name=""):
        self.w = {}
        self.wx = {}
        self.r = {}
        self.name = name


def _merge(dst, src):
    for k, v in src.items():
        if dst.get(k, 0) < v:
            dst[k] = v


class Sched:
    def __init__(self, nc, es, kslots=None):
        self.nc = nc
        self.eng = {"pe": nc.tensor, "dve": nc.vector, "act": nc.scalar,
                    "pool": nc.gpsimd, "sp": nc.sync}
        self.semh = {}
        self.cnt = {}
        self.waited = {e: {} for e in self.eng}
        for e in ("pe", "dve", "act", "pool"):
            self.semh[e] = es.enter_context(nc.semaphore("s_" + e))
            self.cnt[e] = 0
        self.kslots = kslots or {"sp": 8, "pool": 6, "act": 4}
        self.dma_i = {q: 0 for q in self.kslots}
        for q, k in self.kslots.items():
            for s in range(k):
                key = "d_%s_%d" % (q, s)
                self.semh[key] = es.enter_context(nc.semaphore(key))
                self.cnt[key] = 0
        self.nwaits = 0
        self.nops = 0

    def _wait(self, e, ev):
        w = self.waited[e]
        for k, v in ev.items():
            if w.get(k, 0) < v:
                self.eng[e].wait_ge(self.semh[k], v)
                w[k] = v
                self.nwaits += 1

    def _deps(self, reads, writes, swrites):
        ev = {}
        for r in reads:
            _merge(ev, r.w)
        for w_ in writes:
            _merge(ev, w_.w)
            _merge(ev, w_.r)
        for w_ in swrites:
            _merge(ev, w_.r)
            _merge(ev, w_.wx)
        return ev

    def _record(self, key, val, reads, writes, swrites):
        for r in reads:
            if r.r.get(key, 0) < val:
                r.r[key] = val
        for w_ in writes:
            w_.w = {key: val}
            w_.wx = {key: val}
            w_.r = {}
        for w_ in swrites:
            if w_.w.get(key, 0) < val:
                w_.w[key] = val

    def op(self, e, fn, reads=(), writes=(), swrites=()):
        ev = self._deps(reads, writes, swrites)
        self._wait(e, ev)
        ins = fn(self.eng[e])
        self.cnt[e] += 1
        ins.then_inc(self.semh[e], 1)
        self._record(e, self.cnt[e], reads, writes, swrites)
        self.nops += 1
        return ins

    def dma(self, q, out, in_, reads=(), writes=(), swrites=(), **kw):
        k = self.kslots[q]
        slot = self.dma_i[q] % k
        self.dma_i[q] += 1
        key = "d_%s_%d" % (q, slot)
        ev = self._deps(reads, writes, swrites)
        if self.cnt[key] > 0:
            if ev.get(key, 0) < self.cnt[key]:
                ev[key] = self.cnt[key]
        self._wait(q, ev)
        ins = self.eng[q].dma_start(out=out, in_=in_, **kw)
        self.cnt[key] += 16
        ins.then_inc(self.semh[key], 16)
        self._record(key, self.cnt[key], reads, writes, swrites)
        self.nops += 1
        return ins

    def barrier(self, engines=("pe", "dve", "act", "pool", "sp")):
        ev = dict(self.cnt)
        ev = {k: v for k, v in ev.items() if v > 0}
        for e in engines:
            self._wait(e, ev)


def build_nc(debug_out=(), stop_after=None):
    nc = bass.Bass("TRN2", target_bir_lowering=False)

    def din(name, shape, dt=F32):
        return nc.dram_tensor(name, list(shape), dt, kind="ExternalInput").ap()

    def dout(name, shape, dt=F32):
        return nc.dram_tensor(name, list(shape), dt, kind="ExternalOutput").ap()

    def dscr(name, shape, dt):
        kind = "ExternalOutput" if name in debug_out else "Internal"
        return nc.dram_tensor(name, list(shape), dt, kind=kind).ap()

    xs_d = din("xs", [TS, D])
    xp_d = din("xp", [2 * TP, D])
    cvec_d = din("cvec", [128, KC, 2])
    wada_d = din("w_ada", [D, 3 * D])
    bada_d = din("b_ada", [1, 3 * D])
    normw_d = din("norm_w", [128, KC])
    win_d = din("w_in", [D, NIN])
    bg_d = din("b_gates", [16, 1])
    hnwT_d = din("hnorm_wT", [128, 16])
    wfour_d = din("w_four", [8, 256, 256])
    wout_d = din("w_out", [D, D])
    fnw_d = din("final_norm_w", [1, D])
    sC_d = din("state_C", [2, H, DK, DV])
    sn_d = din("state_n", [2, H, DK])
    sm_d = din("state_m", [2, H])
    ident_d = din("ident", [128, 128])
    sel_d = din("sel", [4, 4, 128])
    mask_d = din("maskb", [2, 128, 128])
    bd_d = din("bd64", [128, 4, 128])
    rp_d = din("rp256", [128, 2, 512])
    cs_d = din("cs256", [128, 2, 2, 256])
    ys_d = dout("ys", [TS, D])
    yp_d = dout("yp", [2 * TP, D])
    nC_d = dout("new_C", [2, 2, H, DK, DV])
    nn_d = dout("new_n", [2, 2, H, DK])
    nm_d = dout("new_m", [2, 2, H])
    modrow_d = dscr("modrow", [2, 3 * D], F32)
    qT_d = dscr("qT", [1024, NTOK], BF16)
    kT_d = dscr("kT", [1024, NTOK], BF16)
    ktok_d = dscr("ktok", [NTOK, 1024], BF16)
    v_d = dscr("v", [NTOK, 2048], BF16)
    o_d = dscr("o", [NTOK, 2048], BF16)
    zm_d = dscr("zm", [NTOK, 2048], BF16)
    u_d = dscr("u", [NTOK, 2048], BF16)
    zfT_d = dscr("zfT", [2048, NTOK], BF16)
    gT_d = dscr("gT", [16, NTOK], F32)

    def xrows(t0, n):
        if t0 < TS:
            return xs_d[t0:t0 + n, :]
        return xp_d[t0 - TS:t0 - TS + n, :]

    with ExitStack() as es:
        S = Sched(nc, es)

        def sb(name, shape, dt):
            return es.enter_context(nc.sbuf_tensor(name, list(shape), dt))

        ident_f = sb("ident_f", [128, 128], F32)
        ident_b = sb("ident_b", [128, 128], BF16)
        gfeat = sb("gfeat", [128, 2, KC], F32)
        sfeat = sb("sfeat", [128, 2, KC], F32)
        R_ident = Res("ident")
        R_gs = Res("gs")
        S.dma("sp", ident_f[:], ident_d, writes=[R_ident])
        S.op("dve", lambda e: e.tensor_copy(out=ident_b[:], in_=ident_f[:]),
             reads=[R_ident], writes=[R_ident])

        with ExitStack() as p0:
            def sb0(name, shape, dt):
                return p0.enter_context(nc.sbuf_tensor(name, list(shape), dt))
            cv = sb0("cv", [128, KC, 2], F32)
            sc = sb0("sc", [128, KC, 2], BF16)
            nw = sb0("nw", [128, KC], F32)
            bada = sb0("bada", [2, 3 * D], F32)
            modrow = sb0("modrow_sb", [2, 3 * D], F32)
            wa = [sb0("wa%d" % i, [128, KC, 512], BF16) for i in range(2)]
            ps_m = [p0.enter_context(nc.psum_tensor("ps_m%d" % i, [128, 512], F32)) for i in range(2)]
            ps_t = p0.enter_context(nc.psum_tensor("ps_t", [128, 2, KC, 2], F32))
            R_cv, R_sc, R_nw, R_bada = Res(), Res(), Res(), Res()
            R_wa = [Res(), Res()]
            R_psm = [Res(), Res()]
            R_pst = Res()
            R_modrow = Res()
            R_modrow_d = Res()
            S.dma("sp", cv[:], cvec_d, writes=[R_cv])
            S.dma("sp", nw[:], normw_d, writes=[R_nw])
            S.dma("sp", bada[0:1, :], bada_d, swrites=[R_bada])
            S.dma("sp", bada[1:2, :], bada_d, swrites=[R_bada])
            S.op("act", lambda e: e.activation(out=sc[:], in_=cv[:], func=AF.Silu),
                 reads=[R_cv], writes=[R_sc])
            wada_v = wada_d.rearrange("(kc p) n -> p kc n", p=128)
            NB0 = 3 * D // 512
            for nb in range(NB0):
                b = nb % 2
                for q4 in range(4):
                    S.dma("pool", wa[b][:, q4 * 8:(q4 + 1) * 8, :], wada_v[:, q4 * 8:(q4 + 1) * 8, nb * 512:(nb + 1) * 512],
                          swrites=[R_wa[b]])

                def mm(e, b=b):
                    ins = None
                    for kc in range(KC):
                        ins = e.matmul(ps_m[b][0:2, :], lhsT=sc[:, kc, :], rhs=wa[b][:, kc, :],
                                       start=(kc == 0), stop=(kc == KC - 1))
                    return ins
                S.op("pe", mm, reads=[R_sc, R_wa[b]], writes=[R_psm[b]])
                S.op("dve", lambda e, b=b, nb=nb: e.tensor_tensor(
                    out=modrow[:, nb * 512:(nb + 1) * 512], in0=ps_m[b][0:2, :],
                    in1=bada[:, nb * 512:(nb + 1) * 512], op=ALU.add),
                    reads=[R_psm[b], R_bada], swrites=[R_modrow])
            S.dma("sp", modrow_d, modrow[:], reads=[R_modrow], writes=[R_modrow_d])

            def tr(e):
                ins = None
                for j in range(2):
                    for kc in range(KC):
                        ins = e.transpose(out=ps_t[:, j, kc, :],
                                          in_=modrow[0:2, j * D + kc * 128:j * D + (kc + 1) * 128],
                                          identity=ident_f[0:2, 0:2])
                return ins
            S.op("pe", tr, reads=[R_modrow, R_ident], writes=[R_pst])
            for v in range(2):
                S.op("dve", lambda e, v=v: e.tensor_copy(out=sfeat[:, v, :], in_=ps_t[:, 0, :, v]),
                     reads=[R_pst], swrites=[R_gs])
                S.op("dve", lambda e, v=v: e.scalar_tensor_tensor(
                    out=gfeat[:, v, :], in0=ps_t[:, 1, :, v], scalar=1.0, in1=nw[:],
                    op0=ALU.add, op1=ALU.mult),
                    reads=[R_pst, R_nw], swrites=[R_gs])
            S.barrier()
        if stop_after == 0:
            S.barrier(("sp",))
            return nc

        import os
        TB = 384 if os.environ.get('K_DBG_SMALL') else 1152
        NTB = NTOK // TB
        TPB = TB // 128
        with ExitStack() as p1:
            def sb1(name, shape, dt):
                return p1.enter_context(nc.sbuf_tensor(name, list(shape), dt))
            hT = sb1("hT", [128, KC, TB], BF16)
            wb = [sb1("wb%d" % i, [128, KC, 512], BF16) for i in range(2)]
            xbuf = [sb1("xbuf%d" % i, [128, D], F32) for i in range(2)]
            xsb = sb1("xsb", [128, D], BF16)
            ss = sb1("ss", [128, 2], F32)
            rstd = sb1("rstd", [128, 2], F32)
            stg = [sb1("stg%d" % i, [128, 512], BF16) for i in range(4)]
            stg32 = sb1("stg32", [16, 384], F32)
            stgK = [sb1("stgK%d" % i, [128, 3, 128], BF16) for i in range(2)]
            R_stgK = [Res(), Res()]
            kq = [0]
            bg = sb1("bg", [16, 1], F32)
            ps = [p1.enter_context(nc.psum_tensor("ps%d" % i, [128, 512], F32)) for i in range(6)]
            pst = [p1.enter_context(nc.psum_tensor("pst%d" % i, [128, 8, 128], BF16)) for i in range(2)]
            R_ps = [Res() for _ in range(6)]
            R_pst = [Res() for _ in range(2)]
            R_hT = [Res() for _ in range(TPB)]
            R_wb = [Res(), Res()]
            R_x = [Res(), Res()]
            R_xsb, R_ss, R_rstd = Res(), Res(), Res()
            R_stg = [Res() for _ in range(4)]
            R_stg32 = Res()
            R_bg = Res()
            R_scr = Res("proj scratch")
            S.dma("sp", bg[:], bg_d, writes=[R_bg])
            win_v = win_d.rearrange("(kc p) n -> p kc n", p=128)

            ablocks = []
            for (dst, c0, n) in ((v_d, C_V, 2048), (o_d, C_O, 2048),
                                 (zm_d, C_ZM, 2048), (u_d, C_U, 2048)):
                for j in range(n // 512):
                    ablocks.append(("A", dst, j * 512, c0 + j * 512, 512))
            bblocks = []
            for (dst, c0, n) in ((qT_d, C_Q, 1024), (kT_d, C_K, 1024), (zfT_d, C_ZF, 2048)):
                for j in range(n // 512):
                    bblocks.append(("B", dst, j * 512, c0 + j * 512, 512))
            bblocks.append(("G", gT_d, 0, C_G, 16))
            blocks = ablocks + bblocks
            if os.environ.get('K_DBG_SMALL'):
                blocks = ablocks[:2] + bblocks[:1] + bblocks[-1:]
                NTB = int(os.environ['K_DBG_SMALL'])
            evq = [0]
            psq = [0]
            stq = [0]

            def evac_engine():
                evq[0] += 1
                return "act" if evq[0] % 2 else "dve"

            winb_d = dscr("winb", [len(blocks), 128, KC, 512], BF16)
            R_winb = [Res() for _ in blocks]

            def load_w(i, blk, tb):
                b = i % 2
                kind, dst, dc0, wc0, n = blk
                if tb == 0 or n < 512:
                    for q4 in range(4):
                        S.dma("pool", wb[b][:, q4 * 8:(q4 + 1) * 8, 0:n], win_v[:, q4 * 8:(q4 + 1) * 8, wc0:wc0 + n],
                              swrites=[R_wb[b]])
                    if n == 512:
                        S.dma("pool", winb_d[i], wb[b][:], reads=[R_wb[b]], writes=[R_winb[i]])
                else:
                    for q4 in range(4):
                        S.dma("pool", wb[b][:, q4 * 8:(q4 + 1) * 8, :], winb_d[i][:, q4 * 8:(q4 + 1) * 8, :],
                              reads=[R_winb[i]], swrites=[R_wb[b]])

            for tb in range(NTB):
                tok0 = tb * TB
                for tt in range(TPB):
                    if os.environ.get('K_DBG_CUT') == '3':
                        break
                    t0 = tok0 + tt * 128
                    v = 0 if t0 < TS else 1
                    xb = (tb * TPB + tt) % 2
                    S.dma("sp", xbuf[xb][:], xrows(t0, 128), writes=[R_x[xb]])
                    S.op("act", lambda e, xb=xb, xc=xb: e.activation(
                        out=xsb[:], in_=xbuf[xb][:], func=AF.Square, accum_out=ss[:, xc:xc + 1]),
                        reads=[R_x[xb]], writes=[R_xsb, R_ss])
                    PL = int(os.environ.get('K_DBG_PREP', '9'))
                    if PL < 2:
                        continue
                    S.op("dve", lambda e, xc=xb: e.tensor_scalar(
                        out=rstd[:, xc:xc + 1], in0=ss[:, xc:xc + 1], scalar1=1.0 / D, scalar2=EPS,
                        op0=ALU.mult, op1=ALU.add), reads=[R_ss], writes=[R_rstd])
                    S.op("act", lambda e, xc=xb: e.activation(
                        out=rstd[:, xc:xc + 1], in_=rstd[:, xc:xc + 1], func=AF.Sqrt),
                        reads=[R_rstd], writes=[R_rstd])
                    S.op("dve", lambda e, xc=xb: e.reciprocal(
                        out=rstd[:, xc:xc + 1], in_=rstd[:, xc:xc + 1]),
                        reads=[R_rstd], writes=[R_rstd])
                    if PL < 3:
                        continue
                    S.op("dve", lambda e, xb=xb, xc=xb: e.tensor_scalar(
                        out=xsb[:], in0=xbuf[xb][:], scalar1=rstd[:, xc:xc + 1], scalar2=None,
                        op0=ALU.mult), reads=[R_x[xb], R_rstd], writes=[R_xsb])
                    if PL < 4:
                        continue
                    for grp in range(4):
                        pb = grp % 2

                        def trp(e, grp=grp, pb=pb):
                            ins = None
                            for j in range(8):
                                kc = grp * 8 + j
                                ins = e.transpose(out=pst[pb][:, j, :], in_=xsb[:, kc * 128:(kc + 1) * 128],
                                                  identity=ident_b[:])
                            return ins
                        S.op("pe", trp, reads=[R_xsb, R_ident], writes=[R_pst[pb]])
                        if PL < 5:
                            continue
                        for j in range(8):
                            kc = grp * 8 + j
                            eng = "dve"
                            if eng == "act":
                                S.op("act", lambda e, j=j, kc=kc, pb=pb, tt=tt, v=v: e.activation(
                                    out=hT[:, kc, tt * 128:(tt + 1) * 128], in_=pst[pb][:, j, :],
                                    func=AF.Identity, scale=gfeat[:, v, kc:kc + 1], bias=sfeat[:, v, kc:kc + 1]),
                                    reads=[R_pst[pb], R_gs], swrites=[R_hT[tt]])
                            else:
                                S.op("dve", lambda e, j=j, kc=kc, pb=pb, tt=tt, v=v: e.tensor_scalar(
                                    out=hT[:, kc, tt * 128:(tt + 1) * 128], in0=pst[pb][:, j, :],
                                    scalar1=gfeat[:, v, kc:kc + 1], scalar2=sfeat[:, v, kc:kc + 1],
                                    op0=ALU.mult, op1=ALU.add),
                                    reads=[R_pst[pb], R_gs], swrites=[R_hT[tt]])
                if os.environ.get('K_DBG_CUT') == '2':
                    continue
                load_w(0, blocks[0], tb)
                for bi, blk in enumerate(blocks):
                    if bi + 1 < len(blocks):
                        load_w(bi + 1, blocks[bi + 1], tb)
                    b = bi % 2
                    kind, dst, dc0, wc0, n = blk
                    if kind == "A":
                        for tt in range(TPB):
                            pi = psq[0] % 6
                            psq[0] += 1

                            def mm(e, b=b, tt=tt, pi=pi):
                                ins = None
                                for kc in range(KC):
                                    ins = e.matmul(ps[pi][:, :], lhsT=hT[:, kc, tt * 128:(tt + 1) * 128],
                                                   rhs=wb[b][:, kc, :], start=(kc == 0), stop=(kc == KC - 1))
                                return ins
                            S.op("pe", mm, reads=[R_hT[tt], R_wb[b]], writes=[R_ps[pi]])
                            si = stq[0] % 4
                            stq[0] += 1
                            eng = evac_engine()
                            if eng == "act":
                                S.op("act", lambda e, si=si, pi=pi: e.activation(
                                    out=stg[si][:], in_=ps[pi][:], func=AF.Copy),
                                    reads=[R_ps[pi]], writes=[R_stg[si]])
                            else:
                                S.op("dve", lambda e, si=si, pi=pi: e.tensor_copy(
                                    out=stg[si][:], in_=ps[pi][:]),
                                    reads=[R_ps[pi]], writes=[R_stg[si]])
                            t0 = tok0 + tt * 128
                            S.dma("sp", dst[t0:t0 + 128, dc0:dc0 + 512], stg[si][:],
                                  reads=[R_stg[si]], swrites=[R_scr])
                    else:
                        nch = (n + 127) // 128
                        for ch in range(nch):
                            m = 128
                            for tg in range(TB // 384):
                                pi = psq[0] % 6
                                psq[0] += 1

                                def mm(e, b=b, ch=ch, m=m, tg=tg, pi=pi):
                                    ins = None
                                    for kc in range(KC):
                                        ins = e.matmul(ps[pi][0:m, 0:384], lhsT=wb[b][:, kc, ch * 128:ch * 128 + m],
                                                       rhs=hT[:, kc, tg * 384:(tg + 1) * 384],
                                                       start=(kc == 0), stop=(kc == KC - 1))
                                    return ins
                                S.op("pe", mm, reads=[R_hT[3 * tg], R_hT[3 * tg + 1], R_hT[3 * tg + 2], R_wb[b]],
                                     writes=[R_ps[pi]])
                                t0 = tok0 + tg * 384
                                if kind == "G":
                                    S.op("act", lambda e, pi=pi: e.activation(
                                        out=stg32[:], in_=ps[pi][0:16, 0:384], func=AF.Identity,
                                        bias=bg[:, 0:1], scale=1.0),
                                        reads=[R_ps[pi], R_bg], writes=[R_stg32])
                                    S.dma("sp", dst[:, t0:t0 + 384], stg32[:], reads=[R_stg32], swrites=[R_scr])
                                    continue
                                si = stq[0] % 4
                                stq[0] += 1
                                eng = evac_engine()
                                if eng == "act":
                                    S.op("act", lambda e, si=si, pi=pi: e.activation(
                                        out=stg[si][:, 0:384], in_=ps[pi][:, 0:384], func=AF.Copy),
                                        reads=[R_ps[pi]], writes=[R_stg[si]])
                                else:
                                    S.op("dve", lambda e, si=si, pi=pi: e.tensor_copy(
                                        out=stg[si][:, 0:384], in_=ps[pi][:, 0:384]),
                                        reads=[R_ps[pi]], writes=[R_stg[si]])
                                r0 = dc0 + ch * 128
                                S.dma("sp", dst[r0:r0 + 128, t0:t0 + 384], stg[si][:, 0:384],
                                      reads=[R_stg[si]], swrites=[R_scr])
                                if dst is kT_d:
                                    kq[0] += 1
                                    pb = kq[0] % 2

                                    def trk(e, si=si, pb=pb):
                                        ins = None
                                        for j in range(3):
                                            ins = e.transpose(out=pst[pb][:, j, :], in_=stg[si][:, j * 128:(j + 1) * 128],
                                                              identity=ident_b[:])
                                        return ins
                                    S.op("pe", trk, reads=[R_stg[si], R_ident], writes=[R_pst[pb]])
                                    S.op("dve", lambda e, pb=pb: e.tensor_copy(out=stgK[pb][:], in_=pst[pb][:, 0:3, :]),
                                         reads=[R_pst[pb]], writes=[R_stgK[pb]])
                                    S.dma("sp", ktok_d[t0:t0 + 384, r0:r0 + 128].rearrange("(j p) f -> p j f", p=128),
                                          stgK[pb][:], reads=[R_stgK[pb]], swrites=[R_scr])
            S.barrier()
        if stop_after == 1:
            S.barrier(("sp",))
            return nc


        seqs = [(0, TS), (TS, TP), (TS + TP, TP)]
        p23 = ExitStack()

        def sb23(name, shape, dt):
            return p23.enter_context(nc.sbuf_tensor(name, list(shape), dt))
        ncm = [sb23("ncm%d" % d, [128, NTOK], F32) for d in range(2)]
        ucol = sb23("ucol", [128, NT, 2, 4], F32)
        emtcol = sb23("emtcol", [128, NT, 2, 4], F32)
        R_ncm = [Res(), Res()]
        R_ucol, R_emtcol = Res(), Res()
        R_nm = Res()
        with ExitStack() as p2:
            def sb2(name, shape, dt):
                return p2.enter_context(nc.sbuf_tensor(name, list(shape), dt))
            gi = sb2("gi", [4, NTOK], F32)
            gf = sb2("gf", [4, NTOK], F32)
            Bp = sb2("Bp", [4, NTOK], F32)
            cmr = sb2("cmr", [4, NTOK], F32)
            ones4 = sb2("ones4", [4, NTOK], F32)
            m0 = sb2("m0", [4, 2, 2], F32)
            nmt = sb2("nmt", [4, 2, 2], F32)
            R_nmt = Res()
            ps_u_t = p2.enter_context(nc.psum_tensor("ps_u", [128, 512], F32))
            ps_e_t = p2.enter_context(nc.psum_tensor("ps_e", [128, 512], F32))
            ps_u = ps_u_t[:, 0:NT * 4].rearrange("p (t h) -> p t h", h=4)
            ps_e = ps_e_t[:, 0:NT * 4].rearrange("p (t h) -> p t h", h=4)
            R_gi, R_gf, R_Bp, R_cmr, R_ones, R_m0 = Res(), Res(), Res(), Res(), Res(), Res()
            R_psu, R_pse = Res(), Res()
            S.op("dve", lambda e: e.memset(ones4[:], 1.0), writes=[R_ones])
            for d_ in range(2):
                S.op("dve", lambda e, d_=d_: e.memset(ncm[d_][:], 0.0), writes=[R_ncm[d_]])
            S.op("dve", lambda e: e.memset(m0[:], 0.0), writes=[R_m0])
            for d in range(2):
                S.dma("sp", m0[:, d, 0:1], sm_d[d:d + 1, :].rearrange("o h -> h o"), reads=[], swrites=[R_m0])
            for d in range(2):
                S.dma("sp", gi[:], gT_d[8 * d:8 * d + 4, :], reads=[R_scr], writes=[R_gi])
                S.dma("sp", gf[:], gT_d[8 * d + 4:8 * d + 8, :], reads=[R_scr], writes=[R_gf])
                S.op("act", lambda e: e.activation(out=gf[:], in_=gf[:], func=AF.Exp, scale=-1.0),
                     reads=[R_gf], writes=[R_gf])
                S.op("dve", lambda e: e.tensor_scalar_add(out=gf[:], in0=gf[:], scalar1=1.0),
                     reads=[R_gf], writes=[R_gf])
                S.op("act", lambda e: e.activation(out=gf[:], in_=gf[:], func=AF.Ln),
                     reads=[R_gf], writes=[R_gf])

                def dirv(ap_):
                    return ap_ if d == 0 else ap_[:, ::-1]
                for (t0, T) in seqs:
                    S.op("dve", lambda e, t0=t0, T=T: e.tensor_tensor_scan(
                        out=dirv(Bp[:, t0:t0 + T]), data0=dirv(ones4[:, t0:t0 + T]), data1=dirv(gf[:, t0:t0 + T]),
                        initial=0.0, op0=ALU.mult, op1=ALU.add),
                        reads=[R_gf, R_ones], swrites=[R_Bp])
                S.op("dve", lambda e: e.tensor_tensor(out=gi[:], in0=gi[:], in1=Bp[:], op=ALU.add),
                     reads=[R_Bp, R_gi], writes=[R_gi])
                for si, (t0, T) in enumerate(seqs):
                    mi = 0 if si == 0 else 1
                    S.op("dve", lambda e, t0=t0, T=T, mi=mi: e.tensor_tensor_scan(
                        out=dirv(cmr[:, t0:t0 + T]), data0=dirv(gi[:, t0:t0 + T]), data1=dirv(gi[:, t0:t0 + T]),
                        initial=m0[:, d, mi:mi + 1], op0=ALU.max, op1=ALU.max),
                        reads=[R_gi, R_m0], swrites=[R_cmr])
                S.op("dve", lambda e: e.tensor_scalar_mul(out=ncm[d][0:4, :], in0=cmr[:], scalar1=-1.0),
                     reads=[R_cmr], writes=[R_ncm[d]])
                S.op("dve", lambda e: e.tensor_tensor(out=Bp[:], in0=cmr[:], in1=Bp[:], op=ALU.subtract),
                     reads=[R_cmr, R_Bp], writes=[R_Bp])
                for si in (1, 2):
                    t0, T = seqs[si]
                    idx = t0 + T - 1 if d == 0 else t0
                    S.op("dve", lambda e, si=si, idx=idx: e.tensor_copy(out=nmt[:, si - 1, d:d + 1], in_=Bp[:, idx:idx + 1]),
                         reads=[R_Bp], swrites=[R_nmt])
                S.op("act", lambda e: e.activation(out=cmr[:], in_=Bp[:], func=AF.Exp, scale=-1.0),
                     reads=[R_Bp], writes=[R_cmr])

                def tru(e):
                    ins = None
                    for tt in range(NT):
                        ins = e.transpose(out=ps_u[:, tt, :], in_=gi[0:4, tt * 128:(tt + 1) * 128],
                                          identity=ident_f[0:4, 0:4])
                    return ins
                S.op("pe", tru, reads=[R_gi, R_ident], writes=[R_psu])

                def tre(e):
                    ins = None
                    for tt in range(NT):
                        ins = e.transpose(out=ps_e[:, tt, :], in_=cmr[0:4, tt * 128:(tt + 1) * 128],
                                          identity=ident_f[0:4, 0:4])
                    return ins
                S.op("pe", tre, reads=[R_cmr, R_ident], writes=[R_pse])
                S.op("dve", lambda e: e.tensor_copy(out=ucol[:, :, d, :], in_=ps_u),
                     reads=[R_psu], swrites=[R_ucol])
                S.op("dve", lambda e: e.tensor_copy(out=emtcol[:, :, d, :], in_=ps_e),
                     reads=[R_pse], swrites=[R_emtcol])
            with nc.allow_non_contiguous_dma(reason="tiny new_m output"):
                S.dma("sp", nm_d.rearrange("s d h -> h s d"), nmt[:], reads=[R_nmt], swrites=[R_nm])
            S.barrier()
        if stop_after == 2:
            S.barrier(("sp",))
            return nc

        with ExitStack() as p3:
            def sb3(name, shape, dt):
                return p3.enter_context(nc.sbuf_tensor(name, list(shape), dt))
            sel = sb3("sel_sb", [128, 4, 128], F32)
            maskf = sb3("maskf", [128, 2, 128], F32)
            maskb = sb3("maskb_sb", [128, 2, 128], BF16)
            ones_b = sb3("ones_b", [128, 1], BF16)
            hnwT = sb3("hnwT_sb", [128, 16], F32)
            qTc = [sb3("qTc%d" % i, [128, 8, 128], BF16) for i in range(2)]
            kTc = [sb3("kTc%d" % i, [128, 8, 128], BF16) for i in range(2)]
            ktc = [sb3("ktc%d" % i, [128, 1024], BF16) for i in range(2)]
            vc = [sb3("vc%d" % i, [128, 2048], BF16) for i in range(2)]
            oc = [sb3("oc%d" % i, [128, 2048], BF16) for i in range(2)]
            zc = [sb3("zc%d" % i, [128, 2048], BF16) for i in range(2)]
            hfc = [sb3("hfc%d" % i, [128, 2048], F32) for i in range(2)]
            gmul = sb3("gmul", [128, 2048], F32)
            gsil = sb3("gsil", [128, 2048], F32)
            hfo = [sb3("hfo%d" % i, [128, 2048], F32) for i in range(2)]
            Cf = sb3("Cf", [128, H, 2, DV], F32)
            Cb = sb3("Cb", [128, H, 2, DV], BF16)
            nf = sb3("nf", [128, H, 2], F32)
            nb = sb3("nb", [128, H, 2], BF16)
            cmprev = sb3("cmprev", [128, H], F32)
            DT = [sb3("DT%d" % i, [128, 128], F32) for i in range(2)]
            inter = [sb3("inter%d" % i, [128, 128], F32) for i in range(2)]
            sTm = [sb3("sTm%d" % i, [128, 128], BF16) for i in range(2)]
            qs = [sb3("qs%d" % i, [128, 2, 128], BF16) for i in range(2)]
            kw = [sb3("kw%d" % i, [128, 256], BF16) for i in range(2)]
            kws = [sb3("kws%d" % i, [128, 1], F32) for i in range(2)]
            rd = [sb3("rd%d" % i, [128, 1], F32) for i in range(2)]
            hm = [sb3("hm%d" % i, [128, 512], F32) for i in range(2)]
            sq = [sb3("sq%d" % i, [128, 1], F32) for i in range(2)]
            mtok = [sb3("mtok%d" % i, [128, 512], BF16) for i in range(2)]
            mTs = [sb3("mTs%d" % i, [128, 16, 128], BF16) for i in range(2)]
            psX = [p3.enter_context(nc.psum_tensor("psX%d" % i, [128, 512], F32)) for i in range(2)]
            psD = [p3.enter_context(nc.psum_tensor("psD%d" % i, [128, 512], F32)) for i in range(2)]
            psF = p3.enter_context(nc.psum_tensor("psF", [128, 2, 512], F32))
            psEG = p3.enter_context(nc.psum_tensor("psEG", [128, 512], F32))
            psT = p3.enter_context(nc.psum_tensor("psT", [128, 4, 128], BF16))
            R_sel, R_mask, R_onesb, R_hnw = Res(), Res(), Res(), Res()
            R_q = [Res(), Res()]; R_k = [Res(), Res()]; R_kt = [Res(), Res()]; R_v = [Res(), Res()]
            R_o = [Res(), Res()]; R_z = [Res(), Res()]; R_hfc = [Res(), Res()]
            R_gmul, R_gsil = Res(), Res()
            R_hfo = [Res(), Res()]
            R_Cf = [Res() for _ in range(H)]; R_Cb = [Res() for _ in range(H)]
            R_nf = [Res() for _ in range(H)]; R_nb = [Res() for _ in range(H)]
            R_cmp = [Res() for _ in range(H)]
            R_DT = [Res(), Res()]; R_inter = [Res(), Res()]; R_sTm = [Res(), Res()]; R_qs = [Res(), Res()]
            R_kw = [Res(), Res()]; R_kws = [Res(), Res()]; R_rd = [Res(), Res()]; R_hm = [Res(), Res()]
            R_sq = [Res(), Res()]; R_mtok = [Res(), Res()]; R_mTs = [Res(), Res()]
            R_A = [Res(), Res()]; R_B = R_A; R_C = R_A
            R_EG = Res(); R_E = [R_EG, R_EG]; R_G = R_E
            R_D = [Res(), Res()]; R_F = Res(); R_T = Res()
            R_hfd = Res("hf dram"); R_mixT = Res("mixT dram"); R_state = Res("state out")
            hf_d = dscr("hf", [NTOK, 2048], F32)
            mixT_d = dscr("mixT", [D, NTOK], BF16)
            S.op("dve", lambda e: e.memset(sel[:], 0.0), writes=[R_sel])
            S.dma("sp", sel[0:4, :, :], sel_d, writes=[R_sel])
            S.dma("sp", maskf[:], mask_d.rearrange("d s t -> s d t"), writes=[R_mask])
            S.op("dve", lambda e: e.tensor_copy(out=maskb[:], in_=maskf[:]), reads=[R_mask], writes=[R_mask])
            S.op("dve", lambda e: e.memset(ones_b[:], 1.0), writes=[R_onesb])
            S.dma("sp", hnwT[:], hnwT_d, writes=[R_hnw])
            qT_v = qT_d.rearrange("(j p) t -> p j t", p=128)
            kT_v = kT_d.rearrange("(j p) t -> p j t", p=128)
            mixT_v = mixT_d.rearrange("(j p) t -> p j t", p=128)
            gmulb = [gmul, p3.enter_context(nc.sbuf_tensor("gmul1", [128, 2048], F32))]
            nrow = sb3("nrow", [8, 128], F32)
            R_nrow = Res()
            R_gm = [R_gmul, Res()]
            hix = [0]

            def issue_loads(t0, cb, d):
                S.dma("sp", qTc[cb][:], qT_v[:, :, t0:t0 + 128], reads=[R_scr], writes=[R_q[cb]])
                S.dma("sp", kTc[cb][:], kT_v[:, :, t0:t0 + 128], reads=[R_scr], writes=[R_k[cb]])
                S.dma("sp", ktc[cb][:], ktok_d[t0:t0 + 128, :], reads=[R_scr], writes=[R_kt[cb]])
                S.dma("sp", vc[cb][:], v_d[t0:t0 + 128, :], reads=[R_scr], writes=[R_v[cb]])
                if d == 1:
                    S.dma("sp", oc[cb][:], o_d[t0:t0 + 128, :], reads=[R_scr], writes=[R_o[cb]])
                    S.dma("sp", zc[cb][:], zm_d[t0:t0 + 128, :], reads=[R_scr], writes=[R_z[cb]])
                    S.dma("sp", hfc[cb][:], hf_d[t0:t0 + 128, :], reads=[R_hfd], writes=[R_hfc[cb]])

            def gate_prep(cb):
                S.op("act", lambda e: e.activation(out=gmulb[cb][:], in_=oc[cb][:], func=AF.Sigmoid),
                     reads=[R_o[cb]], writes=[R_gm[cb]])
                S.op("act", lambda e: e.activation(out=gsil[:], in_=zc[cb][:], func=AF.Silu),
                     reads=[R_z[cb]], writes=[R_gsil])
                S.op("dve", lambda e: e.tensor_tensor(out=gmulb[cb][:], in0=gmulb[cb][:], in1=gsil[:], op=ALU.mult),
                     reads=[R_gsil, R_gm[cb]], writes=[R_gm[cb]])

            def stage1(cx):
                d, t0, tt, cb, h, sl, eidx = cx["d"], cx["t0"], cx["tt"], cx["cb"], cx["h"], cx["sl"], cx["eidx"]
                A = psX[sl][:, 0:128]
                Bm = psX[sl][:, 128:256]
                Cs = psX[sl][:, 256:384]
                ncm_c = ncm[d][:, t0:t0 + 128]
                S.op("pe", lambda e: e.matmul(A, lhsT=sel[:, h, :], rhs=ncm_c, start=True, stop=True),
                     reads=[R_sel, R_ncm[d]], swrites=[R_A[sl]])

                def mmB(e):
                    e.matmul(Bm, lhsT=sel[:, h, :], rhs=ncm_c, start=True, stop=False)
                    return e.matmul(Bm, lhsT=ident_b[:], rhs=maskb[:, d, :], start=False, stop=True)
                S.op("pe", mmB, reads=[R_sel, R_ncm[d], R_ident, R_mask], swrites=[R_B[sl]])

                def mmC(e):
                    e.matmul(Cs, lhsT=kTc[cb][:, 2 * h, :], rhs=qTc[cb][:, 2 * h, :], start=True, stop=False)
                    return e.matmul(Cs, lhsT=kTc[cb][:, 2 * h + 1, :], rhs=qTc[cb][:, 2 * h + 1, :],
                                    start=False, stop=True)
                S.op("pe", mmC, reads=[R_q[cb], R_k[cb]], swrites=[R_C[sl]])
                uc = ucol[:, tt, d, h:h + 1]
                S.op("act", lambda e: e.activation(out=DT[sl][:], in_=Bm, func=AF.Exp, bias=uc, scale=1.0),
                     reads=[R_B[sl], R_ucol], writes=[R_DT[sl]])
                S.op("act", lambda e: e.activation(out=inter[sl][:], in_=A, func=AF.Exp, bias=cmprev[:, h:h + 1], scale=1.0),
                     reads=[R_A[sl], R_cmp[h]], writes=[R_inter[sl]])
                S.op("act", lambda e: e.activation(out=kws[sl][:], in_=A[:, eidx:eidx + 1], func=AF.Exp, bias=uc, scale=1.0),
                     reads=[R_A[sl], R_ucol], writes=[R_kws[sl]])
                S.op("dve", lambda e: e.tensor_scalar_mul(out=cmprev[:, h:h + 1], in0=A[:, eidx:eidx + 1], scalar1=-1.0),
                     reads=[R_A[sl]], writes=[R_cmp[h]])
                S.op("dve", lambda e: e.scalar_tensor_tensor(
                    out=sTm[sl][:], in0=Cs, scalar=0.0625, in1=DT[sl][:], op0=ALU.mult, op1=ALU.mult),
                    reads=[R_C[sl], R_DT[sl]], writes=[R_sTm[sl]])
                S.op("dve", lambda e: e.tensor_tensor(
                    out=qs[sl][:], in0=qTc[cb][:, 2 * h:2 * h + 2, :],
                    in1=inter[sl][:].rearrange("p (o t) -> p o t", o=1).broadcast_to([128, 2, 128]), op=ALU.mult),
                    reads=[R_q[cb], R_inter[sl]], writes=[R_qs[sl]])
                S.op("dve", lambda e: e.tensor_scalar(
                    out=kw[sl][:], in0=ktc[cb][:, h * 256:(h + 1) * 256], scalar1=kws[sl][:, 0:1],
                    scalar2=0.0625, op0=ALU.mult, op1=ALU.mult),
                    reads=[R_kt[cb], R_kws[sl]], writes=[R_kw[sl]])

            def stage2(cx):
                d, t0, tt, cb, h, sl, eidx = cx["d"], cx["t0"], cx["tt"], cx["cb"], cx["h"], cx["sl"], cx["eidx"]
                Ed = psEG[:, 8 * sl:8 * sl + 1]
                Gn = psEG[:, 8 * sl + 2:8 * sl + 4]

                def mmD(e):
                    e.matmul(psD[sl][:], lhsT=sTm[sl][:], rhs=vc[cb][:, h * 512:(h + 1) * 512], start=True, stop=False)
                    e.matmul(psD[sl][:], lhsT=qs[sl][:, 0, :], rhs=Cb[:, h, 0, :], start=False, stop=False)
                    return e.matmul(psD[sl][:], lhsT=qs[sl][:, 1, :], rhs=Cb[:, h, 1, :], start=False, stop=True)
                S.op("pe", mmD, reads=[R_sTm[sl], R_v[cb], R_qs[sl], R_Cb[h]], writes=[R_D[sl]])

                def mmE(e):
                    e.matmul(Ed, lhsT=sTm[sl][:], rhs=ones_b[:], start=True, stop=False)
                    e.matmul(Ed, lhsT=qs[sl][:, 0, :], rhs=nb[:, h, 0:1], start=False, stop=False)
                    return e.matmul(Ed, lhsT=qs[sl][:, 1, :], rhs=nb[:, h, 1:2], start=False, stop=True)
                S.op("pe", mmE, reads=[R_sTm[sl], R_onesb, R_qs[sl], R_nb[h]], swrites=[R_E[sl]])

                def mmF(e):
                    e.matmul(psF[:, 0, :], lhsT=kw[sl][:, 0:128], rhs=vc[cb][:, h * 512:(h + 1) * 512], start=True, stop=True)
                    return e.matmul(psF[:, 1, :], lhsT=kw[sl][:, 128:256], rhs=vc[cb][:, h * 512:(h + 1) * 512],
                                    start=True, stop=True)
                S.op("pe", mmF, reads=[R_kw[sl], R_v[cb]], writes=[R_F])

                def mmG(e):
                    e.matmul(Gn[:, 0:1], lhsT=kw[sl][:, 0:128], rhs=ones_b[:], start=True, stop=True)
                    return e.matmul(Gn[:, 1:2], lhsT=kw[sl][:, 128:256], rhs=ones_b[:], start=True, stop=True)
                S.op("pe", mmG, reads=[R_kw[sl], R_onesb], swrites=[R_G[sl]])
                dec = inter[sl][:, eidx:eidx + 1]
                S.op("dve", lambda e: e.tensor_scalar_mul(out=rd[sl][:], in0=Ed, scalar1=-1.0),
                     reads=[R_E[sl]], writes=[R_rd[sl]])
                S.op("dve", lambda e: e.scalar_tensor_tensor(
                    out=rd[sl][:], in0=Ed, scalar=emtcol[:, tt, d, h:h + 1], in1=rd[sl][:], op0=ALU.max, op1=ALU.max),
                    reads=[R_E[sl], R_emtcol, R_rd[sl]], writes=[R_rd[sl]])
                S.op("dve", lambda e: e.scalar_tensor_tensor(
                    out=nf[:, h, :], in0=nf[:, h, :], scalar=dec, in1=Gn, op0=ALU.mult, op1=ALU.add),
                    reads=[R_G[sl], R_inter[sl], R_nf[h]], writes=[R_nf[h]])
                S.op("dve", lambda e: e.reciprocal(out=rd[sl][:], in_=rd[sl][:]),
                     reads=[R_rd[sl]], writes=[R_rd[sl]])
                S.op("dve", lambda e: e.tensor_copy(out=nb[:, h, :], in_=nf[:, h, :]),
                     reads=[R_nf[h]], writes=[R_nb[h]])
                S.op("dve", lambda e: e.scalar_tensor_tensor(
                    out=Cf[:, h, :, :], in0=Cf[:, h, :, :], scalar=dec, in1=psF[:, :, :], op0=ALU.mult, op1=ALU.add),
                    reads=[R_F, R_inter[sl], R_Cf[h]], writes=[R_Cf[h]])

            def stage2b(cx):
                d, t0, tt, cb, h, sl, eidx = cx["d"], cx["t0"], cx["tt"], cx["cb"], cx["h"], cx["sl"], cx["eidx"]
                S.op("act", lambda e: e.activation(out=Cb[:, h, :, :], in_=Cf[:, h, :, :], func=AF.Copy),
                     reads=[R_Cf[h]], writes=[R_Cb[h]])
                if d == 0:
                    S.op("act", lambda e: e.activation(
                        out=hfo[cb][:, h * 512:(h + 1) * 512], in_=psD[sl][:], func=AF.Copy, scale=rd[sl][:, 0:1]),
                        reads=[R_D[sl], R_rd[sl]], swrites=[R_hfo[cb]])
                    return
                S.op("dve", lambda e: e.scalar_tensor_tensor(
                    out=hm[sl][:], in0=psD[sl][:], scalar=rd[sl][:, 0:1],
                    in1=hfc[cb][:, h * 512:(h + 1) * 512], op0=ALU.mult, op1=ALU.add),
                    reads=[R_D[sl], R_rd[sl], R_hfc[cb]], writes=[R_hm[sl]])
                S.op("act", lambda e: e.activation(out=mtok[sl][:], in_=hm[sl][:], func=AF.Square, accum_out=sq[sl][:]),
                     reads=[R_hm[sl]], writes=[R_mtok[sl], R_sq[sl]])
                S.op("dve", lambda e: e.tensor_scalar(
                    out=sq[sl][:], in0=sq[sl][:], scalar1=1.0 / DV, scalar2=EPS, op0=ALU.mult, op1=ALU.add),
                    reads=[R_sq[sl]], writes=[R_sq[sl]])
                S.op("act", lambda e: e.activation(out=sq[sl][:], in_=sq[sl][:], func=AF.Ln),
                     reads=[R_sq[sl]], writes=[R_sq[sl]])
                S.op("act", lambda e: e.activation(out=sq[sl][:], in_=sq[sl][:], func=AF.Exp, scale=-0.5),
                     reads=[R_sq[sl]], writes=[R_sq[sl]])
                S.op("dve", lambda e: e.scalar_tensor_tensor(
                    out=mtok[sl][:], in0=hm[sl][:], scalar=sq[sl][:, 0:1],
                    in1=gmulb[cb][:, h * 512:(h + 1) * 512], op0=ALU.mult, op1=ALU.mult),
                    reads=[R_hm[sl], R_sq[sl], R_gm[cb]], writes=[R_mtok[sl]])

            def stage2c(cx):
                cb, h, sl = cx["cb"], cx["h"], cx["sl"]

                def trm(e):
                    ins = None
                    for j in range(4):
                        ins = e.transpose(out=psT[:, j, :], in_=mtok[sl][:, j * 128:(j + 1) * 128], identity=ident_b[:])
                    return ins
                S.op("pe", trm, reads=[R_mtok[sl], R_ident], writes=[R_T])
                S.op("dve", lambda e: e.tensor_tensor(
                    out=mTs[cb][:, 4 * h:4 * h + 4, :], in0=psT[:],
                    in1=hnwT[:, 4 * h:4 * h + 4].rearrange("p (j o) -> p j o", o=1).broadcast_to([128, 4, 128]),
                    op=ALU.mult),
                    reads=[R_T, R_hnw], swrites=[R_mTs[cb]])

            cbq = [0]
            for si, (seq0, T) in enumerate(seqs):
                nchunk = T // 128
                for d in range(2):
                    if si == 0:
                        S.dma("sp", Cf[:], sC_d[d].rearrange("h (kc p) e -> p h kc e", p=128), writes=R_Cf)
                        with nc.allow_non_contiguous_dma(reason="tiny state vectors"):
                            S.dma("sp", nf[:], sn_d[d].rearrange("h (kc p) -> p h kc", p=128), writes=R_nf)
                        S.dma("sp", cmprev[:], sm_d[d:d + 1, :].broadcast_to([128, H]), writes=R_cmp)
                    else:
                        S.op("dve", lambda e: e.memset(Cf[:], 0.0), writes=R_Cf)
                        S.op("dve", lambda e: e.memset(nf[:], 0.0), writes=R_nf)
                        S.op("dve", lambda e: e.memset(cmprev[:], 0.0), writes=R_cmp)
                    S.op("pool", lambda e: e.tensor_copy(out=Cb[:], in_=Cf[:]), reads=R_Cf, writes=R_Cb)
                    S.op("pool", lambda e: e.tensor_copy(out=nb[:], in_=nf[:]), reads=R_nf, writes=R_nb)
                    order = list(range(nchunk)) if d == 0 else list(range(nchunk - 1, -1, -1))
                    eidx = 127 if d == 0 else 0
                    ctxs = []
                    for c in order:
                        cb = cbq[0] % 2
                        cbq[0] += 1
                        for h in range(H):
                            t0 = seq0 + c * 128
                            ctxs.append({"d": d, "t0": t0, "tt": t0 // 128, "cb": cb, "h": h,
                                         "sl": hix[0] % 2, "eidx": eidx})
                            hix[0] += 1
                    issue_loads(ctxs[0]["t0"], ctxs[0]["cb"], d)
                    if d == 1:
                        gate_prep(ctxs[0]["cb"])
                    stage1(ctxs[0])
                    def finish(cx):
                        stage2b(cx)
                        if cx["h"] == 3 and d == 0:
                            t0, cb = cx["t0"], cx["cb"]
                            S.dma("pool", hf_d[t0:t0 + 128, :], hfo[cb][:], reads=[R_hfo[cb]], swrites=[R_hfd])

                    def finish2(cx):
                        if d == 0:
                            return
                        stage2c(cx)
                        if cx["h"] == 3:
                            t0, cb = cx["t0"], cx["cb"]
                            for h4 in range(4):
                                S.dma("pool", mixT_v[:, 4 * h4:4 * h4 + 4, t0:t0 + 128], mTs[cb][:, 4 * h4:4 * h4 + 4, :],
                                      reads=[R_mTs[cb]], swrites=[R_mixT])

                    for i, cx in enumerate(ctxs):
                        nxt = ctxs[i + 4] if i + 4 < len(ctxs) else None
                        if i + 1 < len(ctxs):
                            stage1(ctxs[i + 1])
                        stage2(cx)
                        if i >= 1:
                            finish(ctxs[i - 1])
                        if i >= 2:
                            finish2(ctxs[i - 2])
                        if cx["h"] == 0 and nxt is not None:
                            issue_loads(nxt["t0"], nxt["cb"], d)
                        if cx["h"] == 1 and d == 1 and nxt is not None:
                            gate_prep(nxt["cb"])
                    finish(ctxs[-1])
                    if len(ctxs) >= 2:
                        finish2(ctxs[-2])
                    finish2(ctxs[-1])
                    if si > 0:
                        S.dma("sp", nC_d[si - 1, d].rearrange("h (kc p) e -> p h kc e", p=128), Cf[:],
                              reads=R_Cf, swrites=[R_state])
                        S.op("pe", lambda e: e.transpose(out=psX[0][0:8, 0:128], in_=nf[:].rearrange("p h k -> p (h k)"),
                                                         identity=ident_f[:]),
                             reads=R_nf + [R_ident], writes=[R_A[0]])
                        S.op("dve", lambda e: e.tensor_copy(out=nrow[:], in_=psX[0][0:8, 0:128]),
                             reads=[R_A[0]], writes=[R_nrow])
                        S.dma("sp", nn_d[si - 1, d].rearrange("h (kc p) -> (h kc) p", p=128), nrow[:],
                              reads=[R_nrow], swrites=[R_state])
            S.barrier()
        p23.close()
        if stop_after == 3:
            S.barrier(("sp",))
            return nc


        Ef_d = dscr("Efft", [2, TS, 2048], BF16)
        with ExitStack() as p4:
            def sb4(name, shape, dt):
                return p4.enter_context(nc.sbuf_tensor(name, list(shape), dt))
            bd_f = sb4("bd_f", [128, 4, 128], F32)
            bd_b = sb4("bd_b", [128, 4, 128], BF16)
            rp_f = sb4("rp_f", [128, 2, 512], F32)
            rp_b = sb4("rp_b", [128, 2, 512], BF16)
            cs_f = sb4("cs_f", [128, 2, 2, 256], F32)
            cs_b = sb4("cs_b", [128, 2, 2, 256], BF16)
            w4 = sb4("w4", [128, 8, 2, 256], BF16)
            CWt = sb4("CWt", [128, 8, 2, 2, 256], BF16)
            U = [sb4("U%d" % i, [128, 2048], BF16) for i in range(2)]
            E1 = [[sb4("E1_%d_%d" % (i, r), [128, 2048], BF16) for r in range(2)] for i in range(2)]
            Zg = sb4("Zg", [128, 32, 2, 256], BF16)
            PT = sb4("PT", [128, 2, 2, TS], BF16)
            zf = [sb4("zf%d" % i, [128, TS], BF16) for i in range(2)]
            sz = sb4("sz", [128, TS], F32)
            fo = [sb4("fo%d" % i, [128, 512], BF16) for i in range(2)]
            Zp = sb4("Zp", [128, 2, 2048], BF16)
            PTp = sb4("PTp", [128, 16, 2, 256], BF16)
            ps4 = [p4.enter_context(nc.psum_tensor("ps4_%d" % i, [128, 512], F32)) for i in range(6)]
            R_c4 = Res(); R_w4 = Res(); R_CW = Res()
            R_U = [Res(), Res()]; R_E1 = [[Res(), Res()], [Res(), Res()]]
            R_Zg, R_PT, R_sz, R_Zp, R_PTp = Res(), Res(), Res(), Res(), Res()
            R_zf = [Res(), Res()]; R_fo = [Res(), Res()]
            R_ps4 = [Res() for _ in range(6)]
            R_Ef = Res("Ef dram")
            p4q = [0]

            def nps():
                p4q[0] += 1
                return p4q[0] % 6
            S.dma("sp", bd_f[:], bd_d, writes=[R_c4])
            S.dma("sp", rp_f[:], rp_d, swrites=[R_c4])
            S.dma("sp", cs_f[:], cs_d, swrites=[R_c4])
            S.op("dve", lambda e: e.tensor_copy(out=bd_b[:], in_=bd_f[:]), reads=[R_c4], swrites=[R_c4])
            S.op("dve", lambda e: e.tensor_copy(out=rp_b[:], in_=rp_f[:]), reads=[R_c4], swrites=[R_c4])
            S.op("dve", lambda e: e.tensor_copy(out=cs_b[:], in_=cs_f[:]), reads=[R_c4], swrites=[R_c4])
            w4_v = wfour_d.rearrange("g (j p) d -> p g j d", p=128)
            for g4 in range(4):
                S.dma("pool", w4[:, 2 * g4:2 * g4 + 2, :, :], w4_v[:, 2 * g4:2 * g4 + 2, :, :], swrites=[R_w4])
            for g in range(8):
                for cch in range(2):
                    for ri in range(2):
                        pi = nps()

                        def mmcw(e, g=g, cch=cch, ri=ri, pi=pi):
                            e.matmul(ps4[pi][:, 0:256], lhsT=cs_b[:, 0, ri, cch * 128:(cch + 1) * 128],
                                     rhs=w4[:, g, 0, :], start=True, stop=False)
                            return e.matmul(ps4[pi][:, 0:256], lhsT=cs_b[:, 1, ri, cch * 128:(cch + 1) * 128],
                                            rhs=w4[:, g, 1, :], start=False, stop=True)
                        S.op("pe", mmcw, reads=[R_c4, R_w4], writes=[R_ps4[pi]])
                        S.op("dve", lambda e, g=g, cch=cch, ri=ri, pi=pi: e.tensor_copy(
                            out=CWt[:, g, cch, ri, :], in_=ps4[pi][:, 0:256]),
                            reads=[R_ps4[pi]], swrites=[R_CW])
            for j in range(32):
                ub = j % 2
                S.dma("sp", U[ub][0:64, :], u_d[2 * j:TS:64, :], reads=[R_scr], writes=[R_U[ub]])
                S.dma("sp", U[ub][64:128, :], u_d[2 * j + 1:TS:64, :], reads=[R_scr], swrites=[R_U[ub]])
                for ri in range(2):
                    for chb in range(4):
                        pi = nps()
                        S.op("pe", lambda e, ub=ub, ri=ri, chb=chb, pi=pi: e.matmul(
                            ps4[pi][:], lhsT=bd_b[:, ri, :], rhs=U[ub][:, chb * 512:(chb + 1) * 512],
                            start=True, stop=True),
                            reads=[R_c4, R_U[ub]], writes=[R_ps4[pi]])
                        if chb % 2:
                            S.op("act", lambda e, ub=ub, ri=ri, chb=chb, pi=pi: e.activation(
                                out=E1[ub][ri][:, chb * 512:(chb + 1) * 512], in_=ps4[pi][:], func=AF.Copy),
                                reads=[R_ps4[pi]], swrites=[R_E1[ub][ri]])
                        else:
                            S.op("dve", lambda e, ub=ub, ri=ri, chb=chb, pi=pi: e.tensor_copy(
                                out=E1[ub][ri][:, chb * 512:(chb + 1) * 512], in_=ps4[pi][:]),
                                reads=[R_ps4[pi]], swrites=[R_E1[ub][ri]])
                    S.dma("sp", Ef_d[ri, 2 * j:TS:64, :], E1[ub][ri][0:64, :], reads=[R_E1[ub][ri]], swrites=[R_Ef])
                    S.dma("sp", Ef_d[ri, 2 * j + 1:TS:64, :], E1[ub][ri][64:128, :], reads=[R_E1[ub][ri]], swrites=[R_Ef])
            def stage3(g, ntok, tok0, PTsrc):
                nb_ = max(1, ntok // 512)
                nn = min(512, ntok)
                for dc in range(2):
                    zb = (2 * g + dc) % 2
                    r0 = g * 256 + dc * 128
                    S.dma("sp", zf[zb][:, 0:ntok], zfT_d[r0:r0 + 128, tok0:tok0 + ntok], reads=[R_scr], writes=[R_zf[zb]])
                    S.op("act", lambda e, zb=zb: e.activation(out=sz[:, 0:ntok], in_=zf[zb][:, 0:ntok], func=AF.Silu),
                         reads=[R_zf[zb]], writes=[R_sz])
                    for tb in range(nb_):
                        pi = nps()

                        def mm3(e, pi=pi, tb=tb, dc=dc):
                            ins = None
                            k = 0
                            for cc in range(2):
                                for ri in range(2):
                                    ins = e.matmul(ps4[pi][:, 0:nn], lhsT=CWt[:, g, cc, ri, dc * 128:(dc + 1) * 128],
                                                   rhs=PTsrc(cc, ri, tb * 512, nn), start=(k == 0), stop=(k == 3))
                                    k += 1
                            return ins
                        S.op("pe", mm3, reads=[R_CW, R_PT, R_PTp], writes=[R_ps4[pi]])
                        fb = p4q[0] % 2
                        S.op("dve", lambda e, pi=pi, fb=fb, tb=tb: e.tensor_tensor(
                            out=fo[fb][:, 0:nn], in0=ps4[pi][:, 0:nn], in1=sz[:, tb * 512:tb * 512 + nn], op=ALU.mult),
                            reads=[R_ps4[pi], R_sz], writes=[R_fo[fb]])
                        S.dma("sp", mixT_d[2048 + r0:2048 + r0 + 128, tok0 + tb * 512:tok0 + tb * 512 + nn],
                              fo[fb][:, 0:nn], reads=[R_fo[fb]], swrites=[R_mixT])

            Ef_v = [Ef_d[ri].rearrange("(i p) c -> p i c", p=128) for ri in range(2)]
            for g in range(8):
                first = True
                for ri in range(2):
                    for i4 in range(4):
                        S.dma("sp", Zg[:, i4 * 8:(i4 + 1) * 8, ri, :], Ef_v[ri][:, i4 * 8:(i4 + 1) * 8, g * 256:(g + 1) * 256],
                              reads=[R_Ef], writes=[R_Zg] if first else [], swrites=[] if first else [R_Zg])
                        first = False
                for i in range(32):
                    for cc in range(2):
                        pi = nps()

                        def mm2(e, i=i, cc=cc, pi=pi):
                            e.matmul(ps4[pi][:, 0:256], lhsT=Zg[:, i, 0, cc * 128:(cc + 1) * 128],
                                     rhs=bd_b[:, 0:2, :], start=True, stop=False)
                            return e.matmul(ps4[pi][:, 0:256], lhsT=Zg[:, i, 1, cc * 128:(cc + 1) * 128],
                                            rhs=bd_b[:, 2:4, :], start=False, stop=True)
                        S.op("pe", mm2, reads=[R_c4, R_Zg], writes=[R_ps4[pi]])
                        eng = "act" if (i + cc) % 2 else "dve"
                        if eng == "act":
                            S.op("act", lambda e, i=i, cc=cc, pi=pi: e.activation(
                                out=PT[:, cc, :, i * 128:(i + 1) * 128],
                                in_=ps4[pi][:, 0:256].rearrange("p (r t) -> p r t", r=2), func=AF.Copy),
                                reads=[R_ps4[pi]], swrites=[R_PT])
                        else:
                            S.op("dve", lambda e, i=i, cc=cc, pi=pi: e.tensor_copy(
                                out=PT[:, cc, :, i * 128:(i + 1) * 128],
                                in_=ps4[pi][:, 0:256].rearrange("p (r t) -> p r t", r=2)),
                                reads=[R_ps4[pi]], swrites=[R_PT])
                stage3(g, TS, 0, lambda cc, ri, t0_, n_: PT[:, cc, ri, t0_:t0_ + n_])
                S.op("dve", lambda e: e.memset(PT[:, 0, 0, 0:1], 0.0), writes=[R_PT])
            for si in (1, 2):
                tok0 = seqs[si][0]
                S.dma("sp", Zp[:], u_d[tok0:tok0 + 256, :].rearrange("(i p) c -> p i c", p=128), reads=[R_scr], writes=[R_Zp])
                for cc in range(16):
                    pi = nps()

                    def mmp(e, cc=cc, pi=pi):
                        e.matmul(ps4[pi][:], lhsT=Zp[:, 0, cc * 128:(cc + 1) * 128], rhs=rp_b[:, 0, :], start=True, stop=False)
                        return e.matmul(ps4[pi][:], lhsT=Zp[:, 1, cc * 128:(cc + 1) * 128], rhs=rp_b[:, 1, :], start=False, stop=True)
                    S.op("pe", mmp, reads=[R_c4, R_Zp], writes=[R_ps4[pi]])
                    S.op("dve", lambda e, cc=cc, pi=pi: e.tensor_copy(
                        out=PTp[:, cc, :, :], in_=ps4[pi][:].rearrange("p (r t) -> p r t", r=2)),
                        reads=[R_ps4[pi]], swrites=[R_PTp])
                for g in range(8):
                    stage3(g, 256, tok0, lambda cc, ri, t0_, n_, g=g: PTp[:, 2 * g + cc, ri, 0:256])
                S.op("dve", lambda e: e.memset(PTp[:, 0, 0, 0:1], 0.0), writes=[R_PTp])
            S.barrier()
        if stop_after == 4:
            S.barrier(("sp",))
            return nc

        with ExitStack() as p5:
            def sb5(name, shape, dt):
                return p5.enter_context(nc.sbuf_tensor(name, list(shape), dt))
            mixTb = sb5("mixTb", [128, KC, 512], BF16)
            yblk = sb5("yblk", [128, 4, D], F32)
            wob = [sb5("wob%d" % i, [128, KC, 512], BF16) for i in range(2)]
            gate_rep = sb5("gate_rep", [128, D], F32)
            fnw_rep = sb5("fnw_rep", [128, D], F32)
            tmp5 = [sb5("tmp5_%d" % i, [128, 512], F32) for i in range(2)]
            junk5 = sb5("junk5", [128, D], BF16)
            ss5 = sb5("ss5", [128, 4], F32)
            psO = [p5.enter_context(nc.psum_tensor("psO%d" % i, [128, 512], F32)) for i in range(4)]
            R_mixTb, R_gate, R_fnw, R_junk5, R_ss5 = Res(), Res(), Res(), Res(), Res()
            R_y = [Res() for _ in range(4)]; R_yx = [Res() for _ in range(4)]
            R_wob = [Res(), Res()]; R_tmp5 = [Res(), Res()]; R_psO = [Res() for _ in range(4)]
            R_yout = Res("y out")
            wout_v = wout_d.rearrange("(kc p) n -> p kc n", p=128)
            mixT_v5 = mixT_d.rearrange("(kc p) t -> p kc t", p=128)
            S.dma("sp", fnw_rep[:], fnw_d.broadcast_to([128, D]), writes=[R_fnw])
            wq = [0]
            o5 = [0]

            woutb_d = dscr("woutb", [8, 128, KC, 512], BF16)
            R_woutb = [Res() for _ in range(8)]

            def load_wo(cb, tb):
                b = wq[0] % 2
                wq[0] += 1
                if tb == 0:
                    for q4 in range(4):
                        S.dma("pool", wob[b][:, q4 * 8:(q4 + 1) * 8, :], wout_v[:, q4 * 8:(q4 + 1) * 8, cb * 512:(cb + 1) * 512],
                              swrites=[R_wob[b]])
                    S.dma("pool", woutb_d[cb], wob[b][:], reads=[R_wob[b]], writes=[R_woutb[cb]])
                else:
                    for q4 in range(4):
                        S.dma("pool", wob[b][:, q4 * 8:(q4 + 1) * 8, :], woutb_d[cb][:, q4 * 8:(q4 + 1) * 8, :],
                              reads=[R_woutb[cb]], swrites=[R_wob[b]])
                return b
            NB5 = NTOK // 512
            for tb in range(NB5):
                tok0 = tb * 512
                v = 0 if tok0 < TS else 1
                if tb == 0 or tb == TS // 512:
                    S.dma("sp", gate_rep[:], modrow_d[v:v + 1, 2 * D:3 * D].broadcast_to([128, D]),
                          reads=[R_modrow_d], writes=[R_gate])
                for tt in range(4):
                    S.dma("sp", yblk[:, tt, :], xrows(tok0 + tt * 128, 128), writes=[R_y[tt], R_yx[tt]])
                S.dma("sp", mixTb[:], mixT_v5[:, :, tok0:tok0 + 512], reads=[R_mixT], writes=[R_mixTb])
                nxt = load_wo(0, tb)
                for cb in range(8):
                    b = nxt
                    if cb + 1 < 8:
                        nxt = load_wo(cb + 1, tb)
                    for tt in range(4):
                        o5[0] += 1
                        pi = o5[0] % 4
                        tb_ = o5[0] % 2

                        def mmo(e, b=b, tt=tt, pi=pi):
                            ins = None
                            for kc in range(KC):
                                ins = e.matmul(psO[pi][:, :], lhsT=mixTb[:, kc, tt * 128:(tt + 1) * 128],
                                               rhs=wob[b][:, kc, :], start=(kc == 0), stop=(kc == KC - 1))
                            return ins
                        S.op("pe", mmo, reads=[R_mixTb, R_wob[b]], writes=[R_psO[pi]])
                        S.op("dve", lambda e, pi=pi, tb_=tb_, cb=cb: e.tensor_tensor(
                            out=tmp5[tb_][:], in0=psO[pi][:, :], in1=gate_rep[:, cb * 512:(cb + 1) * 512], op=ALU.mult),
                            reads=[R_psO[pi], R_gate], writes=[R_tmp5[tb_]])
                        S.op("pool", lambda e, tb_=tb_, tt=tt, cb=cb: e.tensor_tensor(
                            out=yblk[:, tt, cb * 512:(cb + 1) * 512], in0=yblk[:, tt, cb * 512:(cb + 1) * 512],
                            in1=tmp5[tb_][:], op=ALU.add),
                            reads=[R_tmp5[tb_], R_yx[tt]], swrites=[R_y[tt]])
                for tt in range(4):
                    S.op("act", lambda e, tt=tt: e.activation(out=junk5[:], in_=yblk[:, tt, :], func=AF.Square,
                                                              accum_out=ss5[:, tt:tt + 1]),
                         reads=[R_y[tt]], writes=[R_junk5, R_ss5])
                    S.op("dve", lambda e, tt=tt: e.tensor_scalar(
                        out=ss5[:, tt:tt + 1], in0=ss5[:, tt:tt + 1], scalar1=1.0 / D, scalar2=EPS,
                        op0=ALU.mult, op1=ALU.add), reads=[R_ss5], writes=[R_ss5])
                    S.op("act", lambda e, tt=tt: e.activation(out=ss5[:, tt:tt + 1], in_=ss5[:, tt:tt + 1], func=AF.Sqrt),
                         reads=[R_ss5], writes=[R_ss5])
                    S.op("dve", lambda e, tt=tt: e.reciprocal(out=ss5[:, tt:tt + 1], in_=ss5[:, tt:tt + 1]),
                         reads=[R_ss5], writes=[R_ss5])
                    S.op("dve", lambda e, tt=tt: e.scalar_tensor_tensor(
                        out=yblk[:, tt, :], in0=yblk[:, tt, :], scalar=ss5[:, tt:tt + 1], in1=fnw_rep[:],
                        op0=ALU.mult, op1=ALU.mult),
                        reads=[R_ss5, R_fnw, R_y[tt]], writes=[R_y[tt]])
                    t0 = tok0 + tt * 128
                    dst = ys_d[t0:t0 + 128, :] if t0 < TS else yp_d[t0 - TS:t0 - TS + 128, :]
                    S.dma("sp", dst, yblk[:, tt, :], reads=[R_y[tt]], swrites=[R_yout])
            S.barrier()

        S.barrier(("sp",))
    return nc


def make_consts():
    f = np.float32
    c = {}
    c["ident"] = np.eye(128, dtype=f)
    sel = np.zeros((4, 4, 128), f)
    for h in range(4):
        sel[h, h, :] = 1.0
    c["sel"] = sel
    s_idx = np.arange(128)[:, None]
    t_idx = np.arange(128)[None, :]
    mk = np.zeros((2, 128, 128), f)
    mk[0][s_idx > t_idx] = -30000.0
    mk[1][s_idx < t_idx] = -30000.0
    c["maskb"] = mk
    a = np.arange(64, dtype=np.float64)
    th = 2 * np.pi * np.outer(a, a) / 64.0
    C64, S64 = np.cos(th) / 8.0, np.sin(th) / 8.0
    z = np.zeros((64, 64))
    BDC = np.block([[C64, z], [z, C64]])
    BDS = np.block([[S64, z], [z, S64]])
    c["bd64"] = np.ascontiguousarray(np.stack([BDC, -BDS, BDS, BDC], axis=1).astype(f))
    t = np.arange(256, dtype=np.float64)
    th = 2 * np.pi * np.outer(t, t) / 256.0
    Cp, Sp = np.cos(th) / 16.0, np.sin(th) / 16.0
    rp = np.concatenate([Cp, -Sp], axis=1).reshape(2, 128, 512).transpose(1, 0, 2)
    c["rp256"] = np.ascontiguousarray(rp.astype(f))
    cs = np.stack([Cp, Sp], axis=1).reshape(2, 128, 2, 256).transpose(1, 0, 2, 3)
    c["cs256"] = np.ascontiguousarray(cs.astype(f))
    return c


_NC_CACHE = {}


def _core_inputs(inp, b, consts):
    f = np.float32
    cvec = np.stack([np.asarray(inp["c"])[b].reshape(32, 128).T,
                     np.asarray(inp["c_ctx"]).reshape(32, 128).T], axis=-1)
    m = {
        "xs": np.ascontiguousarray(inp["x_sample"][b], dtype=f),
        "xp": np.ascontiguousarray(np.asarray(inp["x_prompt"])[2 * b:2 * b + 2].reshape(512, D), dtype=f),
        "cvec": np.ascontiguousarray(cvec, dtype=f),
        "w_ada": np.asarray(inp["w_ada"])[0], "b_ada": np.asarray(inp["b_ada"])[0].reshape(1, -1),
        "norm_w": np.ascontiguousarray(np.asarray(inp["norm_w"])[0].reshape(32, 128).T),
        "w_in": np.asarray(inp["w_in"])[0], "b_gates": np.asarray(inp["b_gates"])[0].reshape(16, 1),
        "hnorm_wT": np.ascontiguousarray(np.asarray(inp["hnorm_w"])[0].reshape(16, 128).T), "w_four": np.asarray(inp["w_four"])[0],
        "w_out": np.asarray(inp["w_out"])[0], "final_norm_w": np.asarray(inp["final_norm_w"]).reshape(1, -1),
        "state_C": np.ascontiguousarray(np.asarray(inp["state_C"])[b, 0]),
        "state_n": np.ascontiguousarray(np.asarray(inp["state_n"])[b, 0]),
        "state_m": np.ascontiguousarray(np.asarray(inp["state_m"])[b, 0]),
    }
    m.update(consts)
    return m


def kernel(**inputs):
    if "nc" not in _NC_CACHE:
        _NC_CACHE["nc"] = build_nc()
    nc = _NC_CACHE["nc"]
    consts = make_consts()
    in_maps = [_core_inputs(inputs, b, consts) for b in range(8)]
    res = run_bass_kernel_spmd(nc, in_maps, core_ids=list(range(8)))
    r = res.results
    y_sample = np.stack([r[b]["ys"] for b in range(8)], axis=0).astype(np.float32)
    y_prompt = np.concatenate([r[b]["yp"].reshape(2, TP, D) for b in range(8)], axis=0).astype(np.float32)
    new_C = np.concatenate([r[b]["new_C"] for b in range(8)], axis=0)[:, None].astype(np.float32)
    new_n = np.concatenate([r[b]["new_n"] for b in range(8)], axis=0)[:, None].astype(np.float32)
    new_m = np.concatenate([r[b]["new_m"] for b in range(8)], axis=0)[:, None].astype(np.float32)
    return (y_prompt, y_sample, new_C, new_n, new_m)
```

```python
import numpy as np
from contextlib import ExitStack
import concourse.bass as bass
import concourse.mybir as mybir
from concourse.bass_utils import run_bass_kernel_spmd

F32 = mybir.dt.float32
BF16 = mybir.dt.bfloat16
AF = mybir.ActivationFunctionType
ALU = mybir.AluOpType
AX = mybir.AxisListType

D = 4096
NIN = 12304
TS = 4096
TP = 256
NTOK = TS + 2 * TP
NT = NTOK // 128
KC = D // 128
EPS = 1e-6
H = 4
DK = 256
DV = 512
C_Q, C_K, C_V, C_O, C_ZM, C_G, C_U, C_ZF = 0, 1024, 2048, 4096, 6144, 8192, 8208, 10256


class Res:
    __slots__ = ("w", "wx", "r", "name")

    def __init__(self, name=""):
        self.w = {}
        self.wx = {}
        self.r = {}
        self.name = name


def _merge(dst, src):
    for k, v in src.items():
        if dst.get(k, 0) < v:
            dst[k] = v


class Sched:
    def __init__(self, nc, es, kslots=None):
        self.nc = nc
        self.eng = {"pe": nc.tensor, "dve": nc.vector, "act": nc.scalar,
                    "pool": nc.gpsimd, "sp": nc.sync}
        self.semh = {}
        self.cnt = {}
        self.waited = {e: {} for e in self.eng}
        for e in ("pe", "dve", "act", "pool"):
            self.semh[e] = es.enter_context(nc.semaphore("s_" + e))
            self.cnt[e] = 0
        self.kslots = kslots or {"sp": 8, "pool": 6, "act": 4}
        self.dma_i = {q: 0 for q in self.kslots}
        for q, k in self.kslots.items():
            for s in range(k):
                key = "d_%s_%d" % (q, s)
                self.semh[key] = es.enter_context(nc.semaphore(key))
                self.cnt[key] = 0
        self.nwaits = 0
        self.nops = 0

    def _wait(self, e, ev):
        w = self.waited[e]
        for k, v in ev.items():
            if w.get(k, 0) < v:
                self.eng[e].wait_ge(self.semh[k], v)
                w[k] = v
                self.nwaits += 1

    def _deps(self, reads, writes, swrites):
        ev = {}
        for r in reads:
            _merge(ev, r.w)
        for w_ in writes:
            _merge(ev, w_.w)
            _merge(ev, w_.r)
        for w_ in swrites:
            _merge(ev, w_.r)
            _merge(ev, w_.wx)
        return ev

    def _record(self, key, val, reads, writes, swrites):
        for r in reads:
            if r.r.get(key, 0) < val:
                r.r[key] = val
        for w_ in writes:
            w_.w = {key: val}
            w_.wx = {key: val}
            w_.r = {}
        for w_ in swrites:
            if w_.w.get(key, 0) < val:
                w_.w[key] = val

    def op(self, e, fn, reads=(), writes=(), swrites=()):
        ev = self._deps(reads, writes, swrites)
        self._wait(e, ev)
        ins = fn(self.eng[e])
        self.cnt[e] += 1
        ins.then_inc(self.semh[e], 1)
        self._record(e, self.cnt[e], reads, writes, swrites)
        self.nops += 1
        return ins

    def dma(self, q, out, in_, reads=(), writes=(), swrites=(), **kw):
        k = self.kslots[q]
        slot = self.dma_i[q] % k
        self.dma_i[q] += 1
        key = "d_%s_%d" % (q, slot)
        ev = self._deps(reads, writes, swrites)
        if self.cnt[key] > 0:
            if ev.get(key, 0) < self.cnt[key]:
                ev[key] = self.cnt[key]
        self._wait(q, ev)
        ins = self.eng[q].dma_start(out=out, in_=in_, **kw)
        self.cnt[key] += 16
        ins.then_inc(self.semh[key], 16)
        self._record(key, self.cnt[key], reads, writes, swrites)
        self.nops += 1
        return ins

    def barrier(self, engines=("pe", "dve", "act", "pool", "sp")):
        ev = dict(self.cnt)
        ev = {k: v for k, v in ev.items() if v > 0}
        for e in engines:
            self._wait(e, ev)


def build_nc(debug_out=(), stop_after=None):
    nc = bass.Bass("TRN2", target_bir_lowering=False)

    def din(name, shape, dt=F32):
        return nc.dram_tensor(name, list(shape), dt, kind="ExternalInput").ap()

    def dout(name, shape, dt=F32):
        return nc.dram_tensor(name, list(shape), dt, kind="ExternalOutput").ap()

    def dscr(name, shape, dt):
        kind = "ExternalOutput" if name in debug_out else "Internal"
        return nc.dram_tensor(name, list(shape), dt, kind=kind).ap()

    xs_d = din("xs", [TS, D])
    xp_d = din("xp", [2 * TP, D])
    cvec_d = din("cvec", [128, KC, 2])
    wada_d = din("w_ada", [D, 3 * D])
    bada_d = din("b_ada", [1, 3 * D])
    normw_d = din("norm_w", [128, KC])
    win_d = din("w_in", [D, NIN])
    bg_d = din("b_gates", [16, 1])
    hnwT_d = din("hnorm_wT", [128, 16])
    wfour_d = din("w_four", [8, 256, 256])
    wout_d = din("w_out", [D, D])
    fnw_d = din("final_norm_w", [1, D])
    sC_d = din("state_C", [2, H, DK, DV])
    sn_d = din("state_n", [2, H, DK])
    sm_d = din("state_m", [2, H])
    ident_d = din("ident", [128, 128])
    sel_d = din("sel", [4, 4, 128])
    mask_d = din("maskb", [2, 128, 128])
    bd_d = din("bd64", [128, 4, 128])
    rp_d = din("rp256", [128, 2, 512])
    cs_d = din("cs256", [128, 2, 2, 256])
    ys_d = dout("ys", [TS, D])
    yp_d = dout("yp", [2 * TP, D])
    nC_d = dout("new_C", [2, 2, H, DK, DV])
    nn_d = dout("new_n", [2, 2, H, DK])
    nm_d = dout("new_m", [2, 2, H])
    modrow_d = dscr("modrow", [2, 3 * D], F32)
    qT_d = dscr("qT", [1024, NTOK], BF16)
    kT_d = dscr("kT", [1024, NTOK], BF16)
    ktok_d = dscr("ktok", [NTOK, 1024], BF16)
    v_d = dscr("v", [NTOK, 2048], BF16)
    o_d = dscr("o", [NTOK, 2048], BF16)
    zm_d = dscr("zm", [NTOK, 2048], BF16)
    u_d = dscr("u", [NTOK, 2048], BF16)
    zfT_d = dscr("zfT", [2048, NTOK], BF16)
    gT_d = dscr("gT", [16, NTOK], F32)

    def xrows(t0, n):
        if t0 < TS:
            return xs_d[t0:t0 + n, :]
        return xp_d[t0 - TS:t0 - TS + n, :]

    with ExitStack() as es:
        S = Sched(nc, es)

        def sb(name, shape, dt):
            return es.enter_context(nc.sbuf_tensor(name, list(shape), dt))

        ident_f = sb("ident_f", [128, 128], F32)
        ident_b = sb("ident_b", [128, 128], BF16)
        gfeat = sb("gfeat", [128, 2, KC], F32)
        sfeat = sb("sfeat", [128, 2, KC], F32)
        R_ident = Res("ident")
        R_gs = Res("gs")
        S.dma("sp", ident_f[:], ident_d, writes=[R_ident])
        S.op("dve", lambda e: e.tensor_copy(out=ident_b[:], in_=ident_f[:]),
             reads=[R_ident], writes=[R_ident])

        with ExitStack() as p0:
            def sb0(name, shape, dt):
                return p0.enter_context(nc.sbuf_tensor(name, list(shape), dt))
            cv = sb0("cv", [128, KC, 2], F32)
            sc = sb0("sc", [128, KC, 2], BF16)
            nw = sb0("nw", [128, KC], F32)
            bada = sb0("bada", [2, 3 * D], F32)
            modrow = sb0("modrow_sb", [2, 3 * D], F32)
            wa = [sb0("wa%d" % i, [128, KC, 512], BF16) for i in range(2)]
            ps_m = [p0.enter_context(nc.psum_tensor("ps_m%d" % i, [128, 512], F32)) for i in range(2)]
            ps_t = p0.enter_context(nc.psum_tensor("ps_t", [128, 2, KC, 2], F32))
            R_cv, R_sc, R_nw, R_bada = Res(), Res(), Res(), Res()
            R_wa = [Res(), Res()]
            R_psm = [Res(), Res()]
            R_pst = Res()
            R_modrow = Res()
            R_modrow_d = Res()
            S.dma("sp", cv[:], cvec_d, writes=[R_cv])
            S.dma("sp", nw[:], normw_d, writes=[R_nw])
            S.dma("sp", bada[0:1, :], bada_d, swrites=[R_bada])
            S.dma("sp", bada[1:2, :], bada_d, swrites=[R_bada])
            S.op("act", lambda e: e.activation(out=sc[:], in_=cv[:], func=AF.Silu),
                 reads=[R_cv], writes=[R_sc])
            wada_v = wada_d.rearrange("(kc p) n -> p kc n", p=128)
            NB0 = 3 * D // 512
            for nb in range(NB0):
                b = nb % 2
                for q4 in range(4):
                    S.dma("pool", wa[b][:, q4 * 8:(q4 + 1) * 8, :], wada_v[:, q4 * 8:(q4 + 1) * 8, nb * 512:(nb + 1) * 512],
                          swrites=[R_wa[b]])

                def mm(e, b=b):
                    ins = None
                    for kc in range(KC):
                        ins = e.matmul(ps_m[b][0:2, :], lhsT=sc[:, kc, :], rhs=wa[b][:, kc, :],
                                       start=(kc == 0), stop=(kc == KC - 1))
                    return ins
                S.op("pe", mm, reads=[R_sc, R_wa[b]], writes=[R_psm[b]])
                S.op("dve", lambda e, b=b, nb=nb: e.tensor_tensor(
                    out=modrow[:, nb * 512:(nb + 1) * 512], in0=ps_m[b][0:2, :],
                    in1=bada[:, nb * 512:(nb + 1) * 512], op=ALU.add),
                    reads=[R_psm[b], R_bada], swrites=[R_modrow])
            S.dma("sp", modrow_d, modrow[:], reads=[R_modrow], writes=[R_modrow_d])

            def tr(e):
                ins = None
                for j in range(2):
                    for kc in range(KC):
                        ins = e.transpose(out=ps_t[:, j, kc, :],
                                          in_=modrow[0:2, j * D + kc * 128:j * D + (kc + 1) * 128],
                                          identity=ident_f[0:2, 0:2])
                return ins
            S.op("pe", tr, reads=[R_modrow, R_ident], writes=[R_pst])
            for v in range(2):
                S.op("dve", lambda e, v=v: e.tensor_copy(out=sfeat[:, v, :], in_=ps_t[:, 0, :, v]),
                     reads=[R_pst], swrites=[R_gs])
                S.op("dve", lambda e, v=v: e.scalar_tensor_tensor(
                    out=gfeat[:, v, :], in0=ps_t[:, 1, :, v], scalar=1.0, in1=nw[:],
                    op0=ALU.add, op1=ALU.mult),
                    reads=[R_pst, R_nw], swrites=[R_gs])
            S.barrier()
        if stop_after == 0:
            S.barrier(("sp",))
            return nc

        import os
        TB = 384 if os.environ.get('K_DBG_SMALL') else 1152
        NTB = NTOK // TB
        TPB = TB // 128
        with ExitStack() as p1:
            def sb1(name, shape, dt):
                return p1.enter_context(nc.sbuf_tensor(name, list(shape), dt))
            hT = sb1("hT", [128, KC, TB], BF16)
            wb = [sb1("wb%d" % i, [128, KC, 512], BF16) for i in range(2)]
            xbuf = [sb1("xbuf%d" % i, [128, D], F32) for i in range(2)]
            xsb = sb1("xsb", [128, D], BF16)
            ss = sb1("ss", [128, 2], F32)
            rstd = sb1("rstd", [128, 2], F32)
            stg = [sb1("stg%d" % i, [128, 512], BF16) for i in range(4)]
            stg32 = sb1("stg32", [16, 384], F32)
            stgK = [sb1("stgK%d" % i, [128, 3, 128], BF16) for i in range(2)]
            R_stgK = [Res(), Res()]
            kq = [0]
            bg = sb1("bg", [16, 1], F32)
            ps = [p1.enter_context(nc.psum_tensor("ps%d" % i, [128, 512], F32)) for i in range(6)]
            pst = [p1.enter_context(nc.psum_tensor("pst%d" % i, [128, 8, 128], BF16)) for i in range(2)]
            R_ps = [Res() for _ in range(6)]
            R_pst = [Res() for _ in range(2)]
            R_hT = [Res() for _ in range(TPB)]
            R_wb = [Res(), Res()]
            R_x = [Res(), Res()]
            R_xsb, R_ss, R_rstd = Res(), Res(), Res()
            R_stg = [Res() for _ in range(4)]
            R_stg32 = Res()
            R_bg = Res()
            R_scr = Res("proj scratch")
            S.dma("sp", bg[:], bg_d, writes=[R_bg])
            win_v = win_d.rearrange("(kc p) n -> p kc n", p=128)

            ablocks = []
            for (dst, c0, n) in ((v_d, C_V, 2048), (o_d, C_O, 2048),
                                 (zm_d, C_ZM, 2048), (u_d, C_U, 2048)):
                for j in range(n // 512):
                    ablocks.append(("A", dst, j * 512, c0 + j * 512, 512))
            bblocks = []
            for (dst, c0, n) in ((qT_d, C_Q, 1024), (kT_d, C_K, 1024), (zfT_d, C_ZF, 2048)):
                for j in range(n // 512):
                    bblocks.append(("B", dst, j * 512, c0 + j * 512, 512))
            bblocks.append(("G", gT_d, 0, C_G, 16))
            blocks = ablocks + bblocks
            if os.environ.get('K_DBG_SMALL'):
                blocks = ablocks[:2] + bblocks[:1] + bblocks[-1:]
                NTB = int(os.environ['K_DBG_SMALL'])
            evq = [0]
            psq = [0]
            stq = [0]

            def evac_engine():
                evq[0] += 1
                return "act" if evq[0] % 2 else "dve"

            winb_d = dscr("winb", [len(blocks), 128, KC, 512], BF16)
            R_winb = [Res() for _ in blocks]

            def load_w(i, blk, tb):
                b = i % 2
                kind, dst, dc0, wc0, n = blk
                if tb == 0 or n < 512:
                    for q4 in range(4):
                        S.dma("pool", wb[b][:, q4 * 8:(q4 + 1) * 8, 0:n], win_v[:, q4 * 8:(q4 + 1) * 8, wc0:wc0 + n],
                              swrites=[R_wb[b]])
                    if n == 512:
                        S.dma("pool", winb_d[i], wb[b][:], reads=[R_wb[b]], writes=[R_winb[i]])
                else:
                    for q4 in range(4):
                        S.dma("pool", wb[b][:, q4 * 8:(q4 + 1) * 8, :], winb_d[i][:, q4 * 8:(q4 + 1) * 8, :],
                              reads=[R_winb[i]], swrites=[R_wb[b]])

            for tb in range(NTB):
                tok0 = tb * TB
                for tt in range(TPB):
                    if os.environ.get('K_DBG_CUT') == '3':
                        break
                    t0 = tok0 + tt * 128
                    v = 0 if t0 < TS else 1
                    xb = (tb * TPB + tt) % 2
                    S.dma("sp", xbuf[xb][:], xrows(t0, 128), writes=[R_x[xb]])
                    S.op("act", lambda e, xb=xb, xc=xb: e.activation(
                        out=xsb[:], in_=xbuf[xb][:], func=AF.Square, accum_out=ss[:, xc:xc + 1]),
                        reads=[R_x[xb]], writes=[R_xsb, R_ss])
                    PL = int(os.environ.get('K_DBG_PREP', '9'))
                    if PL < 2:
                        continue
                    S.op("dve", lambda e, xc=xb: e.tensor_scalar(
                        out=rstd[:, xc:xc + 1], in0=ss[:, xc:xc + 1], scalar1=1.0 / D, scalar2=EPS,
                        op0=ALU.mult, op1=ALU.add), reads=[R_ss], writes=[R_rstd])
                    S.op("act", lambda e, xc=xb: e.activation(
                        out=rstd[:, xc:xc + 1], in_=rstd[:, xc:xc + 1], func=AF.Sqrt),
                        reads=[R_rstd], writes=[R_rstd])
                    S.op("dve", lambda e, xc=xb: e.reciprocal(
                        out=rstd[:, xc:xc + 1], in_=rstd[:, xc:xc + 1]),
                        reads=[R_rstd], writes=[R_rstd])
                    if PL < 3:
                        continue
                    S.op("dve", lambda e, xb=xb, xc=xb: e.tensor_scalar(
                        out=xsb[:], in0=xbuf[xb][:], scalar1=rstd[:, xc:xc + 1], scalar2=None,
                        op0=ALU.mult), reads=[R_x[xb], R_rstd], writes=[R_xsb])
                    if PL < 4:
                        continue
                    for grp in range(4):
                        pb = grp % 2

                        def trp(e, grp=grp, pb=pb):
                            ins = None
                            for j in range(8):
                                kc = grp * 8 + j
                                ins = e.transpose(out=pst[pb][:, j, :], in_=xsb[:, kc * 128:(kc + 1) * 128],
                                                  identity=ident_b[:])
                            return ins
                        S.op("pe", trp, reads=[R_xsb, R_ident], writes=[R_pst[pb]])
                        if PL < 5:
                            continue
                        for j in range(8):
                            kc = grp * 8 + j
                            eng = "dve"
                            if eng == "act":
                                S.op("act", lambda e, j=j, kc=kc, pb=pb, tt=tt, v=v: e.activation(
                                    out=hT[:, kc, tt * 128:(tt + 1) * 128], in_=pst[pb][:, j, :],
                                    func=AF.Identity, scale=gfeat[:, v, kc:kc + 1], bias=sfeat[:, v, kc:kc + 1]),
                                    reads=[R_pst[pb], R_gs], swrites=[R_hT[tt]])
                            else:
                                S.op("dve", lambda e, j=j, kc=kc, pb=pb, tt=tt, v=v: e.tensor_scalar(
                                    out=hT[:, kc, tt * 128:(tt + 1) * 128], in0=pst[pb][:, j, :],
                                    scalar1=gfeat[:, v, kc:kc + 1], scalar2=sfeat[:, v, kc:kc + 1],
                                    op0=ALU.mult, op1=ALU.add),
                                    reads=[R_pst[pb], R_gs], swrites=[R_hT[tt]])
                if os.environ.get('K_DBG_CUT') == '2':
                    continue
                load_w(0, blocks[0], tb)
                for bi, blk in enumerate(blocks):
                    if bi + 1 < len(blocks):
                        load_w(bi + 1, blocks[bi + 1], tb)
                    b = bi % 2
                    kind, dst, dc0, wc0, n = blk
                    if kind == "A":
                        for tt in range(TPB):
                            pi = psq[0] % 6
                            psq[0] += 1

                            def mm(e, b=b, tt=tt, pi=pi):
                                ins = None
                                for kc in range(KC):
                                    ins = e.matmul(ps[pi][:, :], lhsT=hT[:, kc, tt * 128:(tt + 1) * 128],
                                                   rhs=wb[b][:, kc, :], start=(kc == 0), stop=(kc == KC - 1))
                                return ins
                            S.op("pe", mm, reads=[R_hT[tt], R_wb[b]], writes=[R_ps[pi]])
                            si = stq[0] % 4
                            stq[0] += 1
                            eng = evac_engine()
                            if eng == "act":
                                S.op("act", lambda e, si=si, pi=pi: e.activation(
                                    out=stg[si][:], in_=ps[pi][:], func=AF.Copy),
                                    reads=[R_ps[pi]], writes=[R_stg[si]])
                            else:
                                S.op("dve", lambda e, si=si, pi=pi: e.tensor_copy(
                                    out=stg[si][:], in_=ps[pi][:]),
                                    reads=[R_ps[pi]], writes=[R_stg[si]])
                            t0 = tok0 + tt * 128
                            S.dma("sp", dst[t0:t0 + 128, dc0:dc0 + 512], stg[si][:],
                                  reads=[R_stg[si]], swrites=[R_scr])
                    else:
                        nch = (n + 127) // 128
                        for ch in range(nch):
                            m = 128
                            for tg in range(TB // 384):
                                pi = psq[0] % 6
                                psq[0] += 1

                                def mm(e, b=b, ch=ch, m=m, tg=tg, pi=pi):
                                    ins = None
                                    for kc in range(KC):
                                        ins = e.matmul(ps[pi][0:m, 0:384], lhsT=wb[b][:, kc, ch * 128:ch * 128 + m],
                                                       rhs=hT[:, kc, tg * 384:(tg + 1) * 384],
                                                       start=(kc == 0), stop=(kc == KC - 1))
                                    return ins
                                S.op("pe", mm, reads=[R_hT[3 * tg], R_hT[3 * tg + 1], R_hT[3 * tg + 2], R_wb[b]],
                                     writes=[R_ps[pi]])
                                t0 = tok0 + tg * 384
                                if kind == "G":
                                    S.op("act", lambda e, pi=pi: e.activation(
                                        out=stg32[:], in_=ps[pi][0:16, 0:384], func=AF.Identity,
                                        bias=bg[:, 0:1], scale=1.0),
                                        reads=[R_ps[pi], R_bg], writes=[R_stg32])
                                    S.dma("sp", dst[:, t0:t0 + 384], stg32[:], reads=[R_stg32], swrites=[R_scr])
                                    continue
                                si = stq[0] % 4
                                stq[0] += 1
                                eng = evac_engine()
                                if eng == "act":
                                    S.op("act", lambda e, si=si, pi=pi: e.activation(
                                        out=stg[si][:, 0:384], in_=ps[pi][:, 0:384], func=AF.Copy),
                                        reads=[R_ps[pi]], writes=[R_stg[si]])
                                else:
                                    S.op("dve", lambda e, si=si, pi=pi: e.tensor_copy(
                                        out=stg[si][:, 0:384], in_=ps[pi][:, 0:384]),
                                        reads=[R_ps[pi]], writes=[R_stg[si]])
                                r0 = dc0 + ch * 128
                                S.dma("sp", dst[r0:r0 + 128, t0:t0 + 384], stg[si][:, 0:384],
                                      reads=[R_stg[si]], swrites=[R_scr])
                                if dst is kT_d:
                                    kq[0] += 1
                                    pb = kq[0] % 2

                                    def trk(e, si=si, pb=pb):
                                        ins = None
                                        for j in range(3):
                                            ins = e.transpose(out=pst[pb][:, j, :], in_=stg[si][:, j * 128:(j + 1) * 128],
                                                              identity=ident_b[:])
                                        return ins
                                    S.op("pe", trk, reads=[R_stg[si], R_ident], writes=[R_pst[pb]])
                                    S.op("dve", lambda e, pb=pb: e.tensor_copy(out=stgK[pb][:], in_=pst[pb][:, 0:3, :]),
                                         reads=[R_pst[pb]], writes=[R_stgK[pb]])
                                    S.dma("sp", ktok_d[t0:t0 + 384, r0:r0 + 128].rearrange("(j p) f -> p j f", p=128),
                                          stgK[pb][:], reads=[R_stgK[pb]], swrites=[R_scr])
            S.barrier()
        if stop_after == 1:
            S.barrier(("sp",))
            return nc


        seqs = [(0, TS), (TS, TP), (TS + TP, TP)]
        p23 = ExitStack()

        def sb23(name, shape, dt):
            return p23.enter_context(nc.sbuf_tensor(name, list(shape), dt))
        ncm = [sb23("ncm%d" % d, [128, NTOK], F32) for d in range(2)]
        ucol = sb23("ucol", [128, NT, 2, 4], F32)
        emtcol = sb23("emtcol", [128, NT, 2, 4], F32)
        R_ncm = [Res(), Res()]
        R_ucol, R_emtcol = Res(), Res()
        R_nm = Res()
        with ExitStack() as p2:
            def sb2(name, shape, dt):
                return p2.enter_context(nc.sbuf_tensor(name, list(shape), dt))
            gi = sb2("gi", [4, NTOK], F32)
            gf = sb2("gf", [4, NTOK], F32)
            Bp = sb2("Bp", [4, NTOK], F32)
            cmr = sb2("cmr", [4, NTOK], F32)
            ones4 = sb2("ones4", [4, NTOK], F32)
            m0 = sb2("m0", [4, 2, 2], F32)
            nmt = sb2("nmt", [4, 2, 2], F32)
            R_nmt = Res()
            ps_u_t = p2.enter_context(nc.psum_tensor("ps_u", [128, 512], F32))
            ps_e_t = p2.enter_context(nc.psum_tensor("ps_e", [128, 512], F32))
            ps_u = ps_u_t[:, 0:NT * 4].rearrange("p (t h) -> p t h", h=4)
            ps_e = ps_e_t[:, 0:NT * 4].rearrange("p (t h) -> p t h", h=4)
            R_gi, R_gf, R_Bp, R_cmr, R_ones, R_m0 = Res(), Res(), Res(), Res(), Res(), Res()
            R_psu, R_pse = Res(), Res()
            S.op("dve", lambda e: e.memset(ones4[:], 1.0), writes=[R_ones])
            for d_ in range(2):
                S.op("dve", lambda e, d_=d_: e.memset(ncm[d_][:], 0.0), writes=[R_ncm[d_]])
            S.op("dve", lambda e: e.memset(m0[:], 0.0), writes=[R_m0])
            for d in range(2):
                S.dma("sp", m0[:, d, 0:1], sm_d[d:d + 1, :].rearrange("o h -> h o"), reads=[], swrites=[R_m0])
            for d in range(2):
                S.dma("sp", gi[:], gT_d[8 * d:8 * d + 4, :], reads=[R_scr], writes=[R_gi])
                S.dma("sp", gf[:], gT_d[8 * d + 4:8 * d + 8, :], reads=[R_scr], writes=[R_gf])
                S.op("act", lambda e: e.activation(out=gf[:], in_=gf[:], func=AF.Exp, scale=-1.0),
                     reads=[R_gf], writes=[R_gf])
                S.op("dve", lambda e: e.tensor_scalar_add(out=gf[:], in0=gf[:], scalar1=1.0),
                     reads=[R_gf], writes=[R_gf])
                S.op("act", lambda e: e.activation(out=gf[:], in_=gf[:], func=AF.Ln),
                     reads=[R_gf], writes=[R_gf])

                def dirv(ap_):
                    return ap_ if d == 0 else ap_[:, ::-1]
                for (t0, T) in seqs:
                    S.op("dve", lambda e, t0=t0, T=T: e.tensor_tensor_scan(
                        out=dirv(Bp[:, t0:t0 + T]), data0=dirv(ones4[:, t0:t0 + T]), data1=dirv(gf[:, t0:t0 + T]),
                        initial=0.0, op0=ALU.mult, op1=ALU.add),
                        reads=[R_gf, R_ones], swrites=[R_Bp])
                S.op("dve", lambda e: e.tensor_tensor(out=gi[:], in0=gi[:], in1=Bp[:], op=ALU.add),
                     reads=[R_Bp, R_gi], writes=[R_gi])
                for si, (t0, T) in enumerate(seqs):
                    mi = 0 if si == 0 else 1
                    S.op("dve", lambda e, t0=t0, T=T, mi=mi: e.tensor_tensor_scan(
                        out=dirv(cmr[:, t0:t0 + T]), data0=dirv(gi[:, t0:t0 + T]), data1=dirv(gi[:, t0:t0 + T]),
                        initial=m0[:, d, mi:mi + 1], op0=ALU.max, op1=ALU.max),
                        reads=[R_gi, R_m0], swrites=[R_cmr])
                S.op("dve", lambda e: e.tensor_scalar_mul(out=ncm[d][0:4, :], in0=cmr[:], scalar1=-1.0),
                     reads=[R_cmr], writes=[R_ncm[d]])
                S.op("dve", lambda e: e.tensor_tensor(out=Bp[:], in0=cmr[:], in1=Bp[:], op=ALU.subtract),
                     reads=[R_cmr, R_Bp], writes=[R_Bp])
                for si in (1, 2):
                    t0, T = seqs[si]
                    idx = t0 + T - 1 if d == 0 else t0
                    S.op("dve", lambda e, si=si, idx=idx: e.tensor_copy(out=nmt[:, si - 1, d:d + 1], in_=Bp[:, idx:idx + 1]),
                         reads=[R_Bp], swrites=[R_nmt])
                S.op("act", lambda e: e.activation(out=cmr[:], in_=Bp[:], func=AF.Exp, scale=-1.0),
                     reads=[R_Bp], writes=[R_cmr])

                def tru(e):
                    ins = None
                    for tt in range(NT):
                        ins = e.transpose(out=ps_u[:, tt, :], in_=gi[0:4, tt * 128:(tt + 1) * 128],
                                          identity=ident_f[0:4, 0:4])
                    return ins
                S.op("pe", tru, reads=[R_gi, R_ident], writes=[R_psu])

                def tre(e):
                    ins = None
                    for tt in range(NT):
                        ins = e.transpose(out=ps_e[:, tt, :], in_=cmr[0:4, tt * 128:(tt + 1) * 128],
                                          identity=ident_f[0:4, 0:4])
                    return ins
                S.op("pe", tre, reads=[R_cmr, R_ident], writes=[R_pse])
                S.op("dve", lambda e: e.tensor_copy(out=ucol[:, :, d, :], in_=ps_u),
                     reads=[R_psu], swrites=[R_ucol])
                S.op("dve", lambda e: e.tensor_copy(out=emtcol[:, :, d, :], in_=ps_e),
                     reads=[R_pse], swrites=[R_emtcol])
            with nc.allow_non_contiguous_dma(reason="tiny new_m output"):
                S.dma("sp", nm_d.rearrange("s d h -> h s d"), nmt[:], reads=[R_nmt], swrites=[R_nm])
            S.barrier()
        if stop_after == 2:
            S.barrier(("sp",))
            return nc

        with ExitStack() as p3:
            def sb3(name, shape, dt):
                return p3.enter_context(nc.sbuf_tensor(name, list(shape), dt))
            sel = sb3("sel_sb", [128, 4, 128], F32)
            maskf = sb3("maskf", [128, 2, 128], F32)
            maskb = sb3("maskb_sb", [128, 2, 128], BF16)
            ones_b = sb3("ones_b", [128, 1], BF16)
            hnwT = sb3("hnwT_sb", [128, 16], F32)
            qTc = [sb3("qTc%d" % i, [128, 8, 128], BF16) for i in range(2)]
            kTc = [sb3("kTc%d" % i, [128, 8, 128], BF16) for i in range(2)]
            ktc = [sb3("ktc%d" % i, [128, 1024], BF16) for i in range(2)]
            vc = [sb3("vc%d" % i, [128, 2048], BF16) for i in range(2)]
            oc = [sb3("oc%d" % i, [128, 2048], BF16) for i in range(2)]
            zc = [sb3("zc%d" % i, [128, 2048], BF16) for i in range(2)]
            hfc = [sb3("hfc%d" % i, [128, 2048], F32) for i in range(2)]
            gmul = sb3("gmul", [128, 2048], F32)
            gsil = sb3("gsil", [128, 2048], F32)
            hfo = [sb3("hfo%d" % i, [128, 2048], F32) for i in range(2)]
            Cf = sb3("Cf", [128, H, 2, DV], F32)
            Cb = sb3("Cb", [128, H, 2, DV], BF16)
            nf = sb3("nf", [128, H, 2], F32)
            nb = sb3("nb", [128, H, 2], BF16)
            cmprev = sb3("cmprev", [128, H], F32)
            DT = [sb3("DT%d" % i, [128, 128], F32) for i in range(2)]
            inter = [sb3("inter%d" % i, [128, 128], F32) for i in range(2)]
            sTm = [sb3("sTm%d" % i, [128, 128], BF16) for i in range(2)]
            qs = [sb3("qs%d" % i, [128, 2, 128], BF16) for i in range(2)]
            kw = [sb3("kw%d" % i, [128, 256], BF16) for i in range(2)]
            kws = [sb3("kws%d" % i, [128, 1], F32) for i in range(2)]
            rd = [sb3("rd%d" % i, [128, 1], F32) for i in range(2)]
            hm = [sb3("hm%d" % i, [128, 512], F32) for i in range(2)]
            sq = [sb3("sq%d" % i, [128, 1], F32) for i in range(2)]
            mtok = [sb3("mtok%d" % i, [128, 512], BF16) for i in range(2)]
            mTs = [sb3("mTs%d" % i, [128, 16, 128], BF16) for i in range(2)]
            psX = [p3.enter_context(nc.psum_tensor("psX%d" % i, [128, 512], F32)) for i in range(2)]
            psD = [p3.enter_context(nc.psum_tensor("psD%d" % i, [128, 512], F32)) for i in range(2)]
            psF = p3.enter_context(nc.psum_tensor("psF", [128, 2, 512], F32))
            psEG = p3.enter_context(nc.psum_tensor("psEG", [128, 512], F32))
            psT = p3.enter_context(nc.psum_tensor("psT", [128, 4, 128], BF16))
            R_sel, R_mask, R_onesb, R_hnw = Res(), Res(), Res(), Res()
            R_q = [Res(), Res()]; R_k = [Res(), Res()]; R_kt = [Res(), Res()]; R_v = [Res(), Res()]
            R_o = [Res(), Res()]; R_z = [Res(), Res()]; R_hfc = [Res(), Res()]
            R_gmul, R_gsil = Res(), Res()
            R_hfo = [Res(), Res()]
            R_Cf = [Res() for _ in range(H)]; R_Cb = [Res() for _ in range(H)]
            R_nf = [Res() for _ in range(H)]; R_nb = [Res() for _ in range(H)]
            R_cmp = [Res() for _ in range(H)]
            R_DT = [Res(), Res()]; R_inter = [Res(), Res()]; R_sTm = [Res(), Res()]; R_qs = [Res(), Res()]
            R_kw = [Res(), Res()]; R_kws = [Res(), Res()]; R_rd = [Res(), Res()]; R_hm = [Res(), Res()]
            R_sq = [Res(), Res()]; R_mtok = [Res(), Res()]; R_mTs = [Res(), Res()]
            R_A = [Res(), Res()]; R_B = R_A; R_C = R_A
            R_EG = Res(); R_E = [R_EG, R_EG]; R_G = R_E
            R_D = [Res(), Res()]; R_F = Res(); R_T = Res()
            R_hfd = Res("hf dram"); R_mixT = Res("mixT dram"); R_state = Res("state out")
            hf_d = dscr("hf", [NTOK, 2048], F32)
            mixT_d = dscr("mixT", [D, NTOK], BF16)
            S.op("dve", lambda e: e.memset(sel[:], 0.0), writes=[R_sel])
            S.dma("sp", sel[0:4, :, :], sel_d, writes=[R_sel])
            S.dma("sp", maskf[:], mask_d.rearrange("d s t -> s d t"), writes=[R_mask])
            S.op("dve", lambda e: e.tensor_copy(out=maskb[:], in_=maskf[:]), reads=[R_mask], writes=[R_mask])
            S.op("dve", lambda e: e.memset(ones_b[:], 1.0), writes=[R_onesb])
            S.dma("sp", hnwT[:], hnwT_d, writes=[R_hnw])
            qT_v = qT_d.rearrange("(j p) t -> p j t", p=128)
            kT_v = kT_d.rearrange("(j p) t -> p j t", p=128)
            mixT_v = mixT_d.rearrange("(j p) t -> p j t", p=128)
            gmulb = [gmul, p3.enter_context(nc.sbuf_tensor("gmul1", [128, 2048], F32))]
            nrow = sb3("nrow", [8, 128], F32)
            R_nrow = Res()
            R_gm = [R_gmul, Res()]
            hix = [0]

            def issue_loads(t0, cb, d):
                S.dma("sp", qTc[cb][:], qT_v[:, :, t0:t0 + 128], reads=[R_scr], writes=[R_q[cb]])
                S.dma("sp", kTc[cb][:], kT_v[:, :, t0:t0 + 128], reads=[R_scr], writes=[R_k[cb]])
                S.dma("sp", ktc[cb][:], ktok_d[t0:t0 + 128, :], reads=[R_scr], writes=[R_kt[cb]])
                S.dma("sp", vc[cb][:], v_d[t0:t0 + 128, :], reads=[R_scr], writes=[R_v[cb]])
                if d == 1:
                    S.dma("sp", oc[cb][:], o_d[t0:t0 + 128, :], reads=[R_scr], writes=[R_o[cb]])
                    S.dma("sp", zc[cb][:], zm_d[t0:t0 + 128, :], reads=[R_scr], writes=[R_z[cb]])
                    S.dma("sp", hfc[cb][:], hf_d[t0:t0 + 128, :], reads=[R_hfd], writes=[R_hfc[cb]])

            def gate_prep(cb):
                S.op("act", lambda e: e.activation(out=gmulb[cb][:], in_=oc[cb][:], func=AF.Sigmoid),
                     reads=[R_o[cb]], writes=[R_gm[cb]])
                S.op("act", lambda e: e.activation(out=gsil[:], in_=zc[cb][:], func=AF.Silu),
                     reads=[R_z[cb]], writes=[R_gsil])
                S.op("dve", lambda e: e.tensor_tensor(out=gmulb[cb][:], in0=gmulb[cb][:], in1=gsil[:], op=ALU.mult),
                     reads=[R_gsil, R_gm[cb]], writes=[R_gm[cb]])

            def stage1(cx):
                d, t0, tt, cb, h, sl, eidx = cx["d"], cx["t0"], cx["tt"], cx["cb"], cx["h"], cx["sl"], cx["eidx"]
                A = psX[sl][:, 0:128]
                Bm = psX[sl][:, 128:256]
                Cs = psX[sl][:, 256:384]
                ncm_c = ncm[d][:, t0:t0 + 128]
                S.op("pe", lambda e: e.matmul(A, lhsT=sel[:, h, :], rhs=ncm_c, start=True, stop=True),
                     reads=[R_sel, R_ncm[d]], swrites=[R_A[sl]])

                def mmB(e):
                    e.matmul(Bm, lhsT=sel[:, h, :], rhs=ncm_c, start=True, stop=False)
                    return e.matmul(Bm, lhsT=ident_b[:], rhs=maskb[:, d, :], start=False, stop=True)
                S.op("pe", mmB, reads=[R_sel, R_ncm[d], R_ident, R_mask], swrites=[R_B[sl]])

                def mmC(e):
                    e.matmul(Cs, lhsT=kTc[cb][:, 2 * h, :], rhs=qTc[cb][:, 2 * h, :], start=True, stop=False)
                    return e.matmul(Cs, lhsT=kTc[cb][:, 2 * h + 1, :], rhs=qTc[cb][:, 2 * h + 1, :],
                                    start=False, stop=True)
                S.op("pe", mmC, reads=[R_q[cb], R_k[cb]], swrites=[R_C[sl]])
                uc = ucol[:, tt, d, h:h + 1]
                S.op("act", lambda e: e.activation(out=DT[sl][:], in_=Bm, func=AF.Exp, bias=uc, scale=1.0),
                     reads=[R_B[sl], R_ucol], writes=[R_DT[sl]])
                S.op("act", lambda e: e.activation(out=inter[sl][:], in_=A, func=AF.Exp, bias=cmprev[:, h:h + 1], scale=1.0),
                     reads=[R_A[sl], R_cmp[h]], writes=[R_inter[sl]])
                S.op("act", lambda e: e.activation(out=kws[sl][:], in_=A[:, eidx:eidx + 1], func=AF.Exp, bias=uc, scale=1.0),
                     reads=[R_A[sl], R_ucol], writes=[R_kws[sl]])
                S.op("dve", lambda e: e.tensor_scalar_mul(out=cmprev[:, h:h + 1], in0=A[:, eidx:eidx + 1], scalar1=-1.0),
                     reads=[R_A[sl]], writes=[R_cmp[h]])
                S.op("dve", lambda e: e.scalar_tensor_tensor(
                    out=sTm[sl][:], in0=Cs, scalar=0.0625, in1=DT[sl][:], op0=ALU.mult, op1=ALU.mult),
                    reads=[R_C[sl], R_DT[sl]], writes=[R_sTm[sl]])
                S.op("dve", lambda e: e.tensor_tensor(
                    out=qs[sl][:], in0=qTc[cb][:, 2 * h:2 * h + 2, :],
                    in1=inter[sl][:].rearrange("p (o t) -> p o t", o=1).broadcast_to([128, 2, 128]), op=ALU.mult),
                    reads=[R_q[cb], R_inter[sl]], writes=[R_qs[sl]])
                S.op("dve", lambda e: e.tensor_scalar(
                    out=kw[sl][:], in0=ktc[cb][:, h * 256:(h + 1) * 256], scalar1=kws[sl][:, 0:1],
                    scalar2=0.0625, op0=ALU.mult, op1=ALU.mult),
                    reads=[R_kt[cb], R_kws[sl]], writes=[R_kw[sl]])

            def stage2(cx):
                d, t0, tt, cb, h, sl, eidx = cx["d"], cx["t0"], cx["tt"], cx["cb"], cx["h"], cx["sl"], cx["eidx"]
                Ed = psEG[:, 8 * sl:8 * sl + 1]
                Gn = psEG[:, 8 * sl + 2:8 * sl + 4]

                def mmD(e):
                    e.matmul(psD[sl][:], lhsT=sTm[sl][:], rhs=vc[cb][:, h * 512:(h + 1) * 512], start=True, stop=False)
                    e.matmul(psD[sl][:], lhsT=qs[sl][:, 0, :], rhs=Cb[:, h, 0, :], start=False, stop=False)
                    return e.matmul(psD[sl][:], lhsT=qs[sl][:, 1, :], rhs=Cb[:, h, 1, :], start=False, stop=True)
                S.op("pe", mmD, reads=[R_sTm[sl], R_v[cb], R_qs[sl], R_Cb[h]], writes=[R_D[sl]])

                def mmE(e):
                    e.matmul(Ed, lhsT=sTm[sl][:], rhs=ones_b[:], start=True, stop=False)
                    e.matmul(Ed, lhsT=qs[sl][:, 0, :], rhs=nb[:, h, 0:1], start=False, stop=False)
                    return e.matmul(Ed, lhsT=qs[sl][:, 1, :], rhs=nb[:, h, 1:2], start=False, stop=True)
                S.op("pe", mmE, reads=[R_sTm[sl], R_onesb, R_qs[sl], R_nb[h]], swrites=[R_E[sl]])

                def mmF(e):
                    e.matmul(psF[:, 0, :], lhsT=kw[sl][:, 0:128], rhs=vc[cb][:, h * 512:(h + 1) * 512], start=True, stop=True)
                    return e.matmul(psF[:, 1, :], lhsT=kw[sl][:, 128:256], rhs=vc[cb][:, h * 512:(h + 1) * 512],
                                    start=True, stop=True)
                S.op("pe", mmF, reads=[R_kw[sl], R_v[cb]], writes=[R_F])

                def mmG(e):
                    e.matmul(Gn[:, 0:1], lhsT=kw[sl][:, 0:128], rhs=ones_b[:], start=True, stop=True)
                    return e.matmul(Gn[:, 1:2], lhsT=kw[sl][:, 128:256], rhs=ones_b[:], start=True, stop=True)
                S.op("pe", mmG, reads=[R_kw[sl], R_onesb], swrites=[R_G[sl]])
                dec = inter[sl][:, eidx:eidx + 1]
                S.op("dve", lambda e: e.tensor_scalar_mul(out=rd[sl][:], in0=Ed, scalar1=-1.0),
                     reads=[R_E[sl]], writes=[R_rd[sl]])
                S.op("dve", lambda e: e.scalar_tensor_tensor(
                    out=rd[sl][:], in0=Ed, scalar=emtcol[:, tt, d, h:h + 1], in1=rd[sl][:], op0=ALU.max, op1=ALU.max),
                    reads=[R_E[sl], R_emtcol, R_rd[sl]], writes=[R_rd[sl]])
                S.op("dve", lambda e: e.scalar_tensor_tensor(
                    out=nf[:, h, :], in0=nf[:, h, :], scalar=dec, in1=Gn, op0=ALU.mult, op1=ALU.add),
                    reads=[R_G[sl], R_inter[sl], R_nf[h]], writes=[R_nf[h]])
                S.op("dve", lambda e: e.reciprocal(out=rd[sl][:], in_=rd[sl][:]),
                     reads=[R_rd[sl]], writes=[R_rd[sl]])
                S.op("dve", lambda e: e.tensor_copy(out=nb[:, h, :], in_=nf[:, h, :]),
                     reads=[R_nf[h]], writes=[R_nb[h]])
                S.op("dve", lambda e: e.scalar_tensor_tensor(
                    out=Cf[:, h, :, :], in0=Cf[:, h, :, :], scalar=dec, in1=psF[:, :, :], op0=ALU.mult, op1=ALU.add),
                    reads=[R_F, R_inter[sl], R_Cf[h]], writes=[R_Cf[h]])

            def stage2b(cx):
                d, t0, tt, cb, h, sl, eidx = cx["d"], cx["t0"], cx["tt"], cx["cb"], cx["h"], cx["sl"], cx["eidx"]
                S.op("act", lambda e: e.activation(out=Cb[:, h, :, :], in_=Cf[:, h, :, :], func=AF.Copy),
                     reads=[R_Cf[h]], writes=[R_Cb[h]])
                if d == 0:
                    S.op("act", lambda e: e.activation(
                        out=hfo[cb][:, h * 512:(h + 1) * 512], in_=psD[sl][:], func=AF.Copy, scale=rd[sl][:, 0:1]),
                        reads=[R_D[sl], R_rd[sl]], swrites=[R_hfo[cb]])
                    return
                S.op("dve", lambda e: e.scalar_tensor_tensor(
                    out=hm[sl][:], in0=psD[sl][:], scalar=rd[sl][:, 0:1],
                    in1=hfc[cb][:, h * 512:(h + 1) * 512], op0=ALU.mult, op1=ALU.add),
                    reads=[R_D[sl], R_rd[sl], R_hfc[cb]], writes=[R_hm[sl]])
                S.op("act", lambda e: e.activation(out=mtok[sl][:], in_=hm[sl][:], func=AF.Square, accum_out=sq[sl][:]),
                     reads=[R_hm[sl]], writes=[R_mtok[sl], R_sq[sl]])
                S.op("dve", lambda e: e.tensor_scalar(
                    out=sq[sl][:], in0=sq[sl][:], scalar1=1.0 / DV, scalar2=EPS, op0=ALU.mult, op1=ALU.add),
                    reads=[R_sq[sl]], writes=[R_sq[sl]])
                S.op("act", lambda e: e.activation(out=sq[sl][:], in_=sq[sl][:], func=AF.Ln),
                     reads=[R_sq[sl]], writes=[R_sq[sl]])
                S.op("act", lambda e: e.activation(out=sq[sl][:], in_=sq[sl][:], func=AF.Exp, scale=-0.5),
                     reads=[R_sq[sl]], writes=[R_sq[sl]])
                S.op("dve", lambda e: e.scalar_tensor_tensor(
                    out=mtok[sl][:], in0=hm[sl][:], scalar=sq[sl][:, 0:1],
                    in1=gmulb[cb][:, h * 512:(h + 1) * 512], op0=ALU.mult, op1=ALU.mult),
                    reads=[R_hm[sl], R_sq[sl], R_gm[cb]], writes=[R_mtok[sl]])

            def stage2c(cx):
                cb, h, sl = cx["cb"], cx["h"], cx["sl"]

                def trm(e):
                    ins = None
                    for j in range(4):
                        ins = e.transpose(out=psT[:, j, :], in_=mtok[sl][:, j * 128:(j + 1) * 128], identity=ident_b[:])
                    return ins
                S.op("pe", trm, reads=[R_mtok[sl], R_ident], writes=[R_T])
                S.op("dve", lambda e: e.tensor_tensor(
                    out=mTs[cb][:, 4 * h:4 * h + 4, :], in0=psT[:],
                    in1=hnwT[:, 4 * h:4 * h + 4].rearrange("p (j o) -> p j o", o=1).broadcast_to([128, 4, 128]),
                    op=ALU.mult),
                    reads=[R_T, R_hnw], swrites=[R_mTs[cb]])

            cbq = [0]
            for si, (seq0, T) in enumerate(seqs):
                nchunk = T // 128
                for d in range(2):
                    if si == 0:
                        S.dma("sp", Cf[:], sC_d[d].rearrange("h (kc p) e -> p h kc e", p=128), writes=R_Cf)
                        with nc.allow_non_contiguous_dma(reason="tiny state vectors"):
                            S.dma("sp", nf[:], sn_d[d].rearrange("h (kc p) -> p h kc", p=128), writes=R_nf)
                        S.dma("sp", cmprev[:], sm_d[d:d + 1, :].broadcast_to([128, H]), writes=R_cmp)
                    else:
                        S.op("dve", lambda e: e.memset(Cf[:], 0.0), writes=R_Cf)
                        S.op("dve", lambda e: e.memset(nf[:], 0.0), writes=R_nf)
                        S.op("dve", lambda e: e.memset(cmprev[:], 0.0), writes=R_cmp)
                    S.op("pool", lambda e: e.tensor_copy(out=Cb[:], in_=Cf[:]), reads=R_Cf, writes=R_Cb)
                    S.op("pool", lambda e: e.tensor_copy(out=nb[:], in_=nf[:]), reads=R_nf, writes=R_nb)
                    order = list(range(nchunk)) if d == 0 else list(range(nchunk - 1, -1, -1))
                    eidx = 127 if d == 0 else 0
                    ctxs = []
                    for c in order:
                        cb = cbq[0] % 2
                        cbq[0] += 1
                        for h in range(H):
                            t0 = seq0 + c * 128
                            ctxs.append({"d": d, "t0": t0, "tt": t0 // 128, "cb": cb, "h": h,
                                         "sl": hix[0] % 2, "eidx": eidx})
                            hix[0] += 1
                    issue_loads(ctxs[0]["t0"], ctxs[0]["cb"], d)
                    if d == 1:
                        gate_prep(ctxs[0]["cb"])
                    stage1(ctxs[0])
                    def finish(cx):
                        stage2b(cx)
                        if cx["h"] == 3 and d == 0:
                            t0, cb = cx["t0"], cx["cb"]
                            S.dma("pool", hf_d[t0:t0 + 128, :], hfo[cb][:], reads=[R_hfo[cb]], swrites=[R_hfd])

                    def finish2(cx):
                        if d == 0:
                            return
                        stage2c(cx)
                        if cx["h"] == 3:
                            t0, cb = cx["t0"], cx["cb"]
                            for h4 in range(4):
                                S.dma("pool", mixT_v[:, 4 * h4:4 * h4 + 4, t0:t0 + 128], mTs[cb][:, 4 * h4:4 * h4 + 4, :],
                                      reads=[R_mTs[cb]], swrites=[R_mixT])

                    for i, cx in enumerate(ctxs):
                        nxt = ctxs[i + 4] if i + 4 < len(ctxs) else None
                        if i + 1 < len(ctxs):
                            stage1(ctxs[i + 1])
                        stage2(cx)
                        if i >= 1:
                            finish(ctxs[i - 1])
                        if i >= 2:
                            finish2(ctxs[i - 2])
                        if cx["h"] == 0 and nxt is not None:
                            issue_loads(nxt["t0"], nxt["cb"], d)
                        if cx["h"] == 1 and d == 1 and nxt is not None:
                            gate_prep(nxt["cb"])
                    finish(ctxs[-1])
                    if len(ctxs) >= 2:
                        finish2(ctxs[-2])
                    finish2(ctxs[-1])
                    if si > 0:
                        S.dma("sp", nC_d[si - 1, d].rearrange("h (kc p) e -> p h kc e", p=128), Cf[:],
                              reads=R_Cf, swrites=[R_state])
                        S.op("pe", lambda e: e.transpose(out=psX[0][0:8, 0:128], in_=nf[:].rearrange("p h k -> p (h k)"),
                                                         identity=ident_f[:]),
                             reads=R_nf + [R_ident], writes=[R_A[0]])
                        S.op("dve", lambda e: e.tensor_copy(out=nrow[:], in_=psX[0][0:8, 0:128]),
                             reads=[R_A[0]], writes=[R_nrow])
                        S.dma("sp", nn_d[si - 1, d].rearrange("h (kc p) -> (h kc) p", p=128), nrow[:],
                              reads=[R_nrow], swrites=[R_state])
            S.barrier()
        p23.close()
        if stop_after == 3:
            S.barrier(("sp",))
            return nc


        Ef_d = dscr("Efft", [2, TS, 2048], BF16)
        with ExitStack() as p4:
            def sb4(name, shape, dt):
                return p4.enter_context(nc.sbuf_tensor(name, list(shape), dt))
            bd_f = sb4("bd_f", [128, 4, 128], F32)
            bd_b = sb4("bd_b", [128, 4, 128], BF16)
            rp_f = sb4("rp_f", [128, 2, 512], F32)
            rp_b = sb4("rp_b", [128, 2, 512], BF16)
            cs_f = sb4("cs_f", [128, 2, 2, 256], F32)
            cs_b = sb4("cs_b", [128, 2, 2, 256], BF16)
            w4 = sb4("w4", [128, 8, 2, 256], BF16)
            CWt = sb4("CWt", [128, 8, 2, 2, 256], BF16)
            U = [sb4("U%d" % i, [128, 2048], BF16) for i in range(2)]
            E1 = [[sb4("E1_%d_%d" % (i, r), [128, 2048], BF16) for r in range(2)] for i in range(2)]
            zf = [sb4("zf%d" % i, [128, TS], BF16) for i in range(2)]
            sz = sb4("sz", [128, TS], F32)
            fo = [sb4("fo%d" % i, [128, 512], BF16) for i in range(2)]
            p4s = ExitStack()

            def sb4s(name, shape, dt):
                return p4s.enter_context(nc.sbuf_tensor(name, list(shape), dt))
            Zgb = [sb4s("Zg%d" % i, [128, 32, 2, 256], BF16) for i in range(2)]
            PT = sb4s("PT", [128, 2, 2, TS], BF16)
            ps4 = [p4.enter_context(nc.psum_tensor("ps4_%d" % i, [128, 512], F32)) for i in range(6)]
            R_c4 = Res(); R_w4 = Res(); R_CW = Res()
            R_U = [Res(), Res()]; R_E1 = [[Res(), Res()], [Res(), Res()]]
            R_Zgb = [Res(), Res()]
            R_PT, R_sz, R_Zp, R_PTp = Res(), Res(), Res(), Res()
            R_zf = [Res(), Res()]; R_fo = [Res(), Res()]
            R_ps4 = [Res() for _ in range(6)]
            R_Ef = Res("Ef dram")
            p4q = [0]

            def nps():
                p4q[0] += 1
                return p4q[0] % 6
            S.dma("sp", bd_f[:], bd_d, writes=[R_c4])
            S.dma("sp", rp_f[:], rp_d, swrites=[R_c4])
            S.dma("sp", cs_f[:], cs_d, swrites=[R_c4])
            S.op("dve", lambda e: e.tensor_copy(out=bd_b[:], in_=bd_f[:]), reads=[R_c4], swrites=[R_c4])
            S.op("dve", lambda e: e.tensor_copy(out=rp_b[:], in_=rp_f[:]), reads=[R_c4], swrites=[R_c4])
            S.op("dve", lambda e: e.tensor_copy(out=cs_b[:], in_=cs_f[:]), reads=[R_c4], swrites=[R_c4])
            w4_v = wfour_d.rearrange("g (j p) d -> p g j d", p=128)
            for g4 in range(4):
                S.dma("pool", w4[:, 2 * g4:2 * g4 + 2, :, :], w4_v[:, 2 * g4:2 * g4 + 2, :, :], swrites=[R_w4])
            for g in range(8):
                for cch in range(2):
                    for ri in range(2):
                        pi = nps()

                        def mmcw(e, g=g, cch=cch, ri=ri, pi=pi):
                            e.matmul(ps4[pi][:, 0:256], lhsT=cs_b[:, 0, ri, cch * 128:(cch + 1) * 128],
                                     rhs=w4[:, g, 0, :], start=True, stop=False)
                            return e.matmul(ps4[pi][:, 0:256], lhsT=cs_b[:, 1, ri, cch * 128:(cch + 1) * 128],
                                            rhs=w4[:, g, 1, :], start=False, stop=True)
                        S.op("pe", mmcw, reads=[R_c4, R_w4], writes=[R_ps4[pi]])
                        S.op("dve", lambda e, g=g, cch=cch, ri=ri, pi=pi: e.tensor_copy(
                            out=CWt[:, g, cch, ri, :], in_=ps4[pi][:, 0:256]),
                            reads=[R_ps4[pi]], swrites=[R_CW])
            for j in range(32):
                ub = j % 2
                S.dma("sp", U[ub][0:64, :], u_d[2 * j:TS:64, :], reads=[R_scr], writes=[R_U[ub]])
                S.dma("sp", U[ub][64:128, :], u_d[2 * j + 1:TS:64, :], reads=[R_scr], swrites=[R_U[ub]])
                for ri in range(2):
                    for chb in range(4):
                        pi = nps()
                        S.op("pe", lambda e, ub=ub, ri=ri, chb=chb, pi=pi: e.matmul(
                            ps4[pi][:], lhsT=bd_b[:, ri, :], rhs=U[ub][:, chb * 512:(chb + 1) * 512],
                            start=True, stop=True),
                            reads=[R_c4, R_U[ub]], writes=[R_ps4[pi]])
                        if chb % 2:
                            S.op("act", lambda e, ub=ub, ri=ri, chb=chb, pi=pi: e.activation(
                                out=E1[ub][ri][:, chb * 512:(chb + 1) * 512], in_=ps4[pi][:], func=AF.Copy),
                                reads=[R_ps4[pi]], swrites=[R_E1[ub][ri]])
                        else:
                            S.op("dve", lambda e, ub=ub, ri=ri, chb=chb, pi=pi: e.tensor_copy(
                                out=E1[ub][ri][:, chb * 512:(chb + 1) * 512], in_=ps4[pi][:]),
                                reads=[R_ps4[pi]], swrites=[R_E1[ub][ri]])
                    S.dma("sp", Ef_d[ri, 2 * j:TS:64, :], E1[ub][ri][0:64, :], reads=[R_E1[ub][ri]], swrites=[R_Ef])
                    S.dma("sp", Ef_d[ri, 2 * j + 1:TS:64, :], E1[ub][ri][64:128, :], reads=[R_E1[ub][ri]], swrites=[R_Ef])
            def stage3(g, ntok, tok0, PTsrc):
                nb_ = max(1, ntok // 512)
                nn = min(512, ntok)
                for dc in range(2):
                    zb = (2 * g + dc) % 2
                    r0 = g * 256 + dc * 128
                    S.dma("sp", zf[zb][:, 0:ntok], zfT_d[r0:r0 + 128, tok0:tok0 + ntok], reads=[R_scr], writes=[R_zf[zb]])
                    S.op("act", lambda e, zb=zb: e.activation(out=sz[:, 0:ntok], in_=zf[zb][:, 0:ntok], func=AF.Silu),
                         reads=[R_zf[zb]], writes=[R_sz])
                    for tb in range(nb_):
                        pi = nps()

                        def mm3(e, pi=pi, tb=tb, dc=dc):
                            ins = None
                            k = 0
                            for cc in range(2):
                                for ri in range(2):
                                    ins = e.matmul(ps4[pi][:, 0:nn], lhsT=CWt[:, g, cc, ri, dc * 128:(dc + 1) * 128],
                                                   rhs=PTsrc(cc, ri, tb * 512, nn), start=(k == 0), stop=(k == 3))
                                    k += 1
                            return ins
                        S.op("pe", mm3, reads=[R_CW, R_PT, R_PTp], writes=[R_ps4[pi]])
                        fb = p4q[0] % 2
                        S.op("dve", lambda e, pi=pi, fb=fb, tb=tb: e.tensor_tensor(
                            out=fo[fb][:, 0:nn], in0=ps4[pi][:, 0:nn], in1=sz[:, tb * 512:tb * 512 + nn], op=ALU.mult),
                            reads=[R_ps4[pi], R_sz], writes=[R_fo[fb]])
                        S.dma("sp", mixT_d[2048 + r0:2048 + r0 + 128, tok0 + tb * 512:tok0 + tb * 512 + nn],
                              fo[fb][:, 0:nn], reads=[R_fo[fb]], swrites=[R_mixT])

            Ef_v = [Ef_d[ri].rearrange("(i p) c -> p i c", p=128) for ri in range(2)]
            def load_Zg(g):
                gb = g % 2
                first = True
                for ri in range(2):
                    for i4 in range(4):
                        S.dma("sp", Zgb[gb][:, i4 * 8:(i4 + 1) * 8, ri, :], Ef_v[ri][:, i4 * 8:(i4 + 1) * 8, g * 256:(g + 1) * 256],
                              reads=[R_Ef], writes=[R_Zgb[gb]] if first else [], swrites=[] if first else [R_Zgb[gb]])
                        first = False

            load_Zg(0)
            for g in range(8):
                if g + 1 < 8:
                    load_Zg(g + 1)
                Zg = Zgb[g % 2]
                R_Zg = R_Zgb[g % 2]
                for i in range(32):
                    for cc in range(2):
                        pi = nps()

                        def mm2(e, i=i, cc=cc, pi=pi, Zg=Zg):
                            e.matmul(ps4[pi][:, 0:256], lhsT=Zg[:, i, 0, cc * 128:(cc + 1) * 128],
                                     rhs=bd_b[:, 0:2, :], start=True, stop=False)
                            return e.matmul(ps4[pi][:, 0:256], lhsT=Zg[:, i, 1, cc * 128:(cc + 1) * 128],
                                            rhs=bd_b[:, 2:4, :], start=False, stop=True)
                        S.op("pe", mm2, reads=[R_c4, R_Zg], writes=[R_ps4[pi]])
                        eng = "act" if (i + cc) % 2 else "dve"
                        if eng == "act":
                            S.op("act", lambda e, i=i, cc=cc, pi=pi: e.activation(
                                out=PT[:, cc, :, i * 128:(i + 1) * 128],
                                in_=ps4[pi][:, 0:256].rearrange("p (r t) -> p r t", r=2), func=AF.Copy),
                                reads=[R_ps4[pi]], swrites=[R_PT])
                        else:
                            S.op("dve", lambda e, i=i, cc=cc, pi=pi: e.tensor_copy(
                                out=PT[:, cc, :, i * 128:(i + 1) * 128],
                                in_=ps4[pi][:, 0:256].rearrange("p (r t) -> p r t", r=2)),
                                reads=[R_ps4[pi]], swrites=[R_PT])
                stage3(g, TS, 0, lambda cc, ri, t0_, n_: PT[:, cc, ri, t0_:t0_ + n_])
                S.op("dve", lambda e: e.memset(PT[:, 0, 0, 0:1], 0.0), writes=[R_PT])
            S.barrier()
            p4s.close()
            Zp = sb4("Zp", [128, 2, 2048], BF16)
            PTp = sb4("PTp", [128, 16, 2, 256], BF16)
            for si in (1, 2):
                tok0 = seqs[si][0]
                S.dma("sp", Zp[:], u_d[tok0:tok0 + 256, :].rearrange("(i p) c -> p i c", p=128), reads=[R_scr], writes=[R_Zp])
                for cc in range(16):
                    pi = nps()

                    def mmp(e, cc=cc, pi=pi):
                        e.matmul(ps4[pi][:], lhsT=Zp[:, 0, cc * 128:(cc + 1) * 128], rhs=rp_b[:, 0, :], start=True, stop=False)
                        return e.matmul(ps4[pi][:], lhsT=Zp[:, 1, cc * 128:(cc + 1) * 128], rhs=rp_b[:, 1, :], start=False, stop=True)
                    S.op("pe", mmp, reads=[R_c4, R_Zp], writes=[R_ps4[pi]])
                    S.op("dve", lambda e, cc=cc, pi=pi: e.tensor_copy(
                        out=PTp[:, cc, :, :], in_=ps4[pi][:].rearrange("p (r t) -> p r t", r=2)),
                        reads=[R_ps4[pi]], swrites=[R_PTp])
                for g in range(8):
                    stage3(g, 256, tok0, lambda cc, ri, t0_, n_, g=g: PTp[:, 2 * g + cc, ri, 0:256])
                S.op("dve", lambda e: e.memset(PTp[:, 0, 0, 0:1], 0.0), writes=[R_PTp])
            S.barrier()
        if stop_after == 4:
            S.barrier(("sp",))
            return nc

        with ExitStack() as p5:
            def sb5(name, shape, dt):
                return p5.enter_context(nc.sbuf_tensor(name, list(shape), dt))
            mixTb = sb5("mixTb", [128, KC, 512], BF16)
            yblk = sb5("yblk", [128, 4, D], F32)
            wob = [sb5("wob%d" % i, [128, KC, 512], BF16) for i in range(2)]
            gate_rep = sb5("gate_rep", [128, D], F32)
            fnw_rep = sb5("fnw_rep", [128, D], F32)
            tmp5 = [sb5("tmp5_%d" % i, [128, 512], F32) for i in range(2)]
            junk5 = sb5("junk5", [128, D], BF16)
            ss5 = sb5("ss5", [128, 4], F32)
            psO = [p5.enter_context(nc.psum_tensor("psO%d" % i, [128, 512], F32)) for i in range(4)]
            R_mixTb, R_gate, R_fnw, R_junk5, R_ss5 = Res(), Res(), Res(), Res(), Res()
            R_y = [Res() for _ in range(4)]; R_yx = [Res() for _ in range(4)]
            R_wob = [Res(), Res()]; R_tmp5 = [Res(), Res()]; R_psO = [Res() for _ in range(4)]
            R_yout = Res("y out")
            wout_v = wout_d.rearrange("(kc p) n -> p kc n", p=128)
            mixT_v5 = mixT_d.rearrange("(kc p) t -> p kc t", p=128)
            S.dma("sp", fnw_rep[:], fnw_d.broadcast_to([128, D]), writes=[R_fnw])
            wq = [0]
            o5 = [0]

            woutb_d = dscr("woutb", [8, 128, KC, 512], BF16)
            R_woutb = [Res() for _ in range(8)]

            def load_wo(cb, tb):
                b = wq[0] % 2
                wq[0] += 1
                if tb == 0:
                    for q4 in range(4):
                        S.dma("pool", wob[b][:, q4 * 8:(q4 + 1) * 8, :], wout_v[:, q4 * 8:(q4 + 1) * 8, cb * 512:(cb + 1) * 512],
                              swrites=[R_wob[b]])
                    S.dma("pool", woutb_d[cb], wob[b][:], reads=[R_wob[b]], writes=[R_woutb[cb]])
                else:
                    for q4 in range(4):
                        S.dma("pool", wob[b][:, q4 * 8:(q4 + 1) * 8, :], woutb_d[cb][:, q4 * 8:(q4 + 1) * 8, :],
                              reads=[R_woutb[cb]], swrites=[R_wob[b]])
                return b
            NB5 = NTOK // 512
            for tb in range(NB5):
                tok0 = tb * 512
                v = 0 if tok0 < TS else 1
                if tb == 0 or tb == TS // 512:
                    S.dma("sp", gate_rep[:], modrow_d[v:v + 1, 2 * D:3 * D].broadcast_to([128, D]),
                          reads=[R_modrow_d], writes=[R_gate])
                for tt in range(4):
                    S.dma("sp", yblk[:, tt, :], xrows(tok0 + tt * 128, 128), writes=[R_y[tt], R_yx[tt]])
                S.dma("sp", mixTb[:], mixT_v5[:, :, tok0:tok0 + 512], reads=[R_mixT], writes=[R_mixTb])
                nxt = load_wo(0, tb)
                for cb in range(8):
                    b = nxt
                    if cb + 1 < 8:
                        nxt = load_wo(cb + 1, tb)
                    for tt in range(4):
                        o5[0] += 1
                        pi = o5[0] % 4
                        tb_ = o5[0] % 2

                        def mmo(e, b=b, tt=tt, pi=pi):
                            ins = None
                            for kc in range(KC):
                                ins = e.matmul(psO[pi][:, :], lhsT=mixTb[:, kc, tt * 128:(tt + 1) * 128],
                                               rhs=wob[b][:, kc, :], start=(kc == 0), stop=(kc == KC - 1))
                            return ins
                        S.op("pe", mmo, reads=[R_mixTb, R_wob[b]], writes=[R_psO[pi]])
                        S.op("dve", lambda e, pi=pi, tb_=tb_, cb=cb: e.tensor_tensor(
                            out=tmp5[tb_][:], in0=psO[pi][:, :], in1=gate_rep[:, cb * 512:(cb + 1) * 512], op=ALU.mult),
                            reads=[R_psO[pi], R_gate], writes=[R_tmp5[tb_]])
                        S.op("pool", lambda e, tb_=tb_, tt=tt, cb=cb: e.tensor_tensor(
                            out=yblk[:, tt, cb * 512:(cb + 1) * 512], in0=yblk[:, tt, cb * 512:(cb + 1) * 512],
                            in1=tmp5[tb_][:], op=ALU.add),
                            reads=[R_tmp5[tb_], R_yx[tt]], swrites=[R_y[tt]])
                for tt in range(4):
                    S.op("act", lambda e, tt=tt: e.activation(out=junk5[:], in_=yblk[:, tt, :], func=AF.Square,
                                                              accum_out=ss5[:, tt:tt + 1]),
                         reads=[R_y[tt]], writes=[R_junk5, R_ss5])
                    S.op("dve", lambda e, tt=tt: e.tensor_scalar(
                        out=ss5[:, tt:tt + 1], in0=ss5[:, tt:tt + 1], scalar1=1.0 / D, scalar2=EPS,
                        op0=ALU.mult, op1=ALU.add), reads=[R_ss5], writes=[R_ss5])
                    S.op("act", lambda e, tt=tt: e.activation(out=ss5[:, tt:tt + 1], in_=ss5[:, tt:tt + 1], func=AF.Sqrt),
                         reads=[R_ss5], writes=[R_ss5])
                    S.op("dve", lambda e, tt=tt: e.reciprocal(out=ss5[:, tt:tt + 1], in_=ss5[:, tt:tt + 1]),
                         reads=[R_ss5], writes=[R_ss5])
                    S.op("dve", lambda e, tt=tt: e.scalar_tensor_tensor(
                        out=yblk[:, tt, :], in0=yblk[:, tt, :], scalar=ss5[:, tt:tt + 1], in1=fnw_rep[:],
                        op0=ALU.mult, op1=ALU.mult),
                        reads=[R_ss5, R_fnw, R_y[tt]], writes=[R_y[tt]])
                    t0 = tok0 + tt * 128
                    dst = ys_d[t0:t0 + 128, :] if t0 < TS else yp_d[t0 - TS:t0 - TS + 128, :]
                    S.dma("sp", dst, yblk[:, tt, :], reads=[R_y[tt]], swrites=[R_yout])
            S.barrier()

        S.barrier(("sp",))
    return nc


def make_consts():
    f = np.float32
    c = {}
    c["ident"] = np.eye(128, dtype=f)
    sel = np.zeros((4, 4, 128), f)
    for h in range(4):
        sel[h, h, :] = 1.0
    c["sel"] = sel
    s_idx = np.arange(128)[:, None]
    t_idx = np.arange(128)[None, :]
    mk = np.zeros((2, 128, 128), f)
    mk[0][s_idx > t_idx] = -30000.0
    mk[1][s_idx < t_idx] = -30000.0
    c["maskb"] = mk
    a = np.arange(64, dtype=np.float64)
    th = 2 * np.pi * np.outer(a, a) / 64.0
    C64, S64 = np.cos(th) / 8.0, np.sin(th) / 8.0
    z = np.zeros((64, 64))
    BDC = np.block([[C64, z], [z, C64]])
    BDS = np.block([[S64, z], [z, S64]])
    c["bd64"] = np.ascontiguousarray(np.stack([BDC, -BDS, BDS, BDC], axis=1).astype(f))
    t = np.arange(256, dtype=np.float64)
    th = 2 * np.pi * np.outer(t, t) / 256.0
    Cp, Sp = np.cos(th) / 16.0, np.sin(th) / 16.0
    rp = np.concatenate([Cp, -Sp], axis=1).reshape(2, 128, 512).transpose(1, 0, 2)
    c["rp256"] = np.ascontiguousarray(rp.astype(f))
    cs = np.stack([Cp, Sp], axis=1).reshape(2, 128, 2, 256).transpose(1, 0, 2, 3)
    c["cs256"] = np.ascontiguousarray(cs.astype(f))
    return c


_NC_CACHE = {}


def _core_inputs(inp, b, consts):
    f = np.float32
    cvec = np.stack([np.asarray(inp["c"])[b].reshape(32, 128).T,
                     np.asarray(inp["c_ctx"]).reshape(32, 128).T], axis=-1)
    m = {
        "xs": np.ascontiguousarray(inp["x_sample"][b], dtype=f),
        "xp": np.ascontiguousarray(np.asarray(inp["x_prompt"])[2 * b:2 * b + 2].reshape(512, D), dtype=f),
        "cvec": np.ascontiguousarray(cvec, dtype=f),
        "w_ada": np.asarray(inp["w_ada"])[0], "b_ada": np.asarray(inp["b_ada"])[0].reshape(1, -1),
        "norm_w": np.ascontiguousarray(np.asarray(inp["norm_w"])[0].reshape(32, 128).T),
        "w_in": np.asarray(inp["w_in"])[0], "b_gates": np.asarray(inp["b_gates"])[0].reshape(16, 1),
        "hnorm_wT": np.ascontiguousarray(np.asarray(inp["hnorm_w"])[0].reshape(16, 128).T), "w_four": np.asarray(inp["w_four"])[0],
        "w_out": np.asarray(inp["w_out"])[0], "final_norm_w": np.asarray(inp["final_norm_w"]).reshape(1, -1),
        "state_C": np.ascontiguousarray(np.asarray(inp["state_C"])[b, 0]),
        "state_n": np.ascontiguousarray(np.asarray(inp["state_n"])[b, 0]),
        "state_m": np.ascontiguousarray(np.asarray(inp["state_m"])[b, 0]),
    }
    m.update(consts)
    return m


def kernel(**inputs):
    if "nc" not in _NC_CACHE:
        _NC_CACHE["nc"] = build_nc()
    nc = _NC_CACHE["nc"]
    consts = make_consts()
    in_maps = [_core_inputs(inputs, b, consts) for b in range(8)]
    res = run_bass_kernel_spmd(nc, in_maps, core_ids=list(range(8)))
    r = res.results
    y_sample = np.stack([r[b]["ys"] for b in range(8)], axis=0).astype(np.float32)
    y_prompt = np.concatenate([r[b]["yp"].reshape(2, TP, D) for b in range(8)], axis=0).astype(np.float32)
    new_C = np.concatenate([r[b]["new_C"] for b in range(8)], axis=0)[:, None].astype(np.float32)
    new_n = np.concatenate([r[b]["new_n"] for b in range(8)], axis=0)[:, None].astype(np.float32)
    new_m = np.concatenate([r[b]["new_m"] for b in range(8)], axis=0)[:, None].astype(np.float32)
    return (y_prompt, y_sample, new_C, new_n, new_m)
```

```python
import numpy as np
from contextlib import ExitStack
import concourse.bass as bass
import concourse.mybir as mybir
from concourse.bass_utils import run_bass_kernel_spmd

F32 = mybir.dt.float32
BF16 = mybir.dt.bfloat16
AF = mybir.ActivationFunctionType
ALU = mybir.AluOpType
AX = mybir.AxisListType

D = 4096
NIN = 12304
TS = 4096
TP = 256
NTOK = TS + 2 * TP
NT = NTOK // 128
KC = D // 128
EPS = 1e-6
H = 4
DK = 256
DV = 512
C_Q, C_K, C_V, C_O, C_ZM, C_G, C_U, C_ZF = 0, 1024, 2048, 4096, 6144, 8192, 8208, 10256


class Res:
    __slots__ = ("w", "wx", "r", "name")

    def __init__(self, name=""):
        self.w = {}
        self.wx = {}
        self.r = {}
        self.name = name


def _merge(dst, src):
    for k, v in src.items():
        if dst.get(k, 0) < v:
            dst[k] = v


class Sched:
    def __init__(self, nc, es, kslots=None):
        self.nc = nc
        self.eng = {"pe": nc.tensor, "dve": nc.vector, "act": nc.scalar,
                    "pool": nc.gpsimd, "sp": nc.sync}
        self.semh = {}
        self.cnt = {}
        self.waited = {e: {} for e in self.eng}
        for e in ("pe", "dve", "act", "pool"):
            self.semh[e] = es.enter_context(nc.semaphore("s_" + e))
            self.cnt[e] = 0
        self.kslots = kslots or {"sp": 8, "pool": 6, "act": 4}
        self.dma_i = {q: 0 for q in self.kslots}
        for q, k in self.kslots.items():
            for s in range(k):
                key = "d_%s_%d" % (q, s)
                self.semh[key] = es.enter_context(nc.semaphore(key))
                self.cnt[key] = 0
        self.nwaits = 0
        self.nops = 0

    def _wait(self, e, ev):
        w = self.waited[e]
        for k, v in ev.items():
            if w.get(k, 0) < v:
                self.eng[e].wait_ge(self.semh[k], v)
                w[k] = v
                self.nwaits += 1

    def _deps(self, reads, writes, swrites):
        ev = {}
        for r in reads:
            _merge(ev, r.w)
        for w_ in writes:
            _merge(ev, w_.w)
            _merge(ev, w_.r)
        for w_ in swrites:
            _merge(ev, w_.r)
            _merge(ev, w_.wx)
        return ev

    def _record(self, key, val, reads, writes, swrites):
        for r in reads:
            if r.r.get(key, 0) < val:
                r.r[key] = val
        for w_ in writes:
            w_.w = {key: val}
            w_.wx = {key: val}
            w_.r = {}
        for w_ in swrites:
            if w_.w.get(key, 0) < val:
                w_.w[key] = val

    def op(self, e, fn, reads=(), writes=(), swrites=()):
        ev = self._deps(reads, writes, swrites)
        self._wait(e, ev)
        ins = fn(self.eng[e])
        self.cnt[e] += 1
        ins.then_inc(self.semh[e], 1)
        self._record(e, self.cnt[e], reads, writes, swrites)
        self.nops += 1
        return ins

    def dma(self, q, out, in_, reads=(), writes=(), swrites=(), **kw):
        k = self.kslots[q]
        slot = self.dma_i[q] % k
        self.dma_i[q] += 1
        key = "d_%s_%d" % (q, slot)
        ev = self._deps(reads, writes, swrites)
        if self.cnt[key] > 0:
            if ev.get(key, 0) < self.cnt[key]:
                ev[key] = self.cnt[key]
        self._wait(q, ev)
        ins = self.eng[q].dma_start(out=out, in_=in_, **kw)
        self.cnt[key] += 16
        ins.then_inc(self.semh[key], 16)
        self._record(key, self.cnt[key], reads, writes, swrites)
        self.nops += 1
        return ins

    def barrier(self, engines=("pe", "dve", "act", "pool", "sp")):
        ev = dict(self.cnt)
        ev = {k: v for k, v in ev.items() if v > 0}
        for e in engines:
            self._wait(e, ev)


def build_nc(debug_out=(), stop_after=None):
    nc = bass.Bass("TRN2", target_bir_lowering=False)

    def din(name, shape, dt=F32):
        return nc.dram_tensor(name, list(shape), dt, kind="ExternalInput").ap()

    def dout(name, shape, dt=F32):
        return nc.dram_tensor(name, list(shape), dt, kind="ExternalOutput").ap()

    def dscr(name, shape, dt):
        kind = "ExternalOutput" if name in debug_out else "Internal"
        return nc.dram_tensor(name, list(shape), dt, kind=kind).ap()

    xs_d = din("xs", [TS, D])
    xp_d = din("xp", [2 * TP, D])
    cvec_d = din("cvec", [128, KC, 2])
    wada_d = din("w_ada", [D, 3 * D])
    bada_d = din("b_ada", [1, 3 * D])
    normw_d = din("norm_w", [128, KC])
    win_d = din("w_in", [D, NIN])
    bg_d = din("b_gates", [16, 1])
    hnwT_d = din("hnorm_wT", [128, 16])
    wfour_d = din("w_four", [8, 256, 256])
    wout_d = din("w_out", [D, D])
    fnw_d = din("final_norm_w", [1, D])
    sC_d = din("state_C", [2, H, DK, DV])
    sn_d = din("state_n", [2, H, DK])
    sm_d = din("state_m", [2, H])
    ident_d = din("ident", [128, 128])
    sel_d = din("sel", [4, 4, 128])
    mask_d = din("maskb", [2, 128, 128])
    bd_d = din("bd64", [128, 4, 128])
    rp_d = din("rp256", [128, 2, 512])
    cs_d = din("cs256", [128, 2, 2, 256])
    ys_d = dout("ys", [TS, D])
    yp_d = dout("yp", [2 * TP, D])
    nC_d = dout("new_C", [2, 2, H, DK, DV])
    nn_d = dout("new_n", [2, 2, H, DK])
    nm_d = dout("new_m", [2, 2, H])
    modrow_d = dscr("modrow", [2, 3 * D], F32)
    qT_d = dscr("qT", [1024, NTOK], BF16)
    kT_d = dscr("kT", [1024, NTOK], BF16)
    ktok_d = dscr("ktok", [NTOK, 1024], BF16)
    v_d = dscr("v", [NTOK, 2048], BF16)
    o_d = dscr("o", [NTOK, 2048], BF16)
    zm_d = dscr("zm", [NTOK, 2048], BF16)
    u_d = dscr("u", [NTOK, 2048], BF16)
    zfT_d = dscr("zfT", [2048, NTOK], BF16)
    gT_d = dscr("gT", [16, NTOK], F32)

    def xrows(t0, n):
        if t0 < TS:
            return xs_d[t0:t0 + n, :]
        return xp_d[t0 - TS:t0 - TS + n, :]

    with ExitStack() as es:
        S = Sched(nc, es)

        def sb(name, shape, dt):
            return es.enter_context(nc.sbuf_tensor(name, list(shape), dt))

        ident_f = sb("ident_f", [128, 128], F32)
        ident_b = sb("ident_b", [128, 128], BF16)
        gfeat = sb("gfeat", [128, 2, KC], F32)
        sfeat = sb("sfeat", [128, 2, KC], F32)
        R_ident = Res("ident")
        R_gs = Res("gs")
        sc = sb("sc", [128, KC, 2], BF16)
        S.dma("sp", ident_f[:], ident_d, writes=[R_ident])
        S.op("dve", lambda e: e.tensor_copy(out=ident_b[:], in_=ident_f[:]),
             reads=[R_ident], writes=[R_ident])

        with ExitStack() as p0:
            def sb0(name, shape, dt):
                return p0.enter_context(nc.sbuf_tensor(name, list(shape), dt))
            cv = sb0("cv", [128, KC, 2], F32)
            nw = sb0("nw", [128, KC], F32)
            bada = sb0("bada", [2, 3 * D], F32)
            modrow = sb0("modrow_sb", [2, 3 * D], F32)
            wa = [sb0("wa%d" % i, [128, KC, 512], BF16) for i in range(2)]
            ps_m = [p0.enter_context(nc.psum_tensor("ps_m%d" % i, [128, 512], F32)) for i in range(2)]
            ps_t = p0.enter_context(nc.psum_tensor("ps_t", [128, 2, KC, 2], F32))
            R_cv, R_sc, R_nw, R_bada = Res(), Res(), Res(), Res()
            R_wa = [Res(), Res()]
            R_psm = [Res(), Res()]
            R_pst = Res()
            R_modrow = Res()
            R_modrow_d = Res()
            S.dma("sp", cv[:], cvec_d, writes=[R_cv])
            S.dma("sp", nw[:], normw_d, writes=[R_nw])
            S.dma("sp", bada[0:1, :], bada_d, swrites=[R_bada])
            S.dma("sp", bada[1:2, :], bada_d, swrites=[R_bada])
            S.op("act", lambda e: e.activation(out=sc[:], in_=cv[:], func=AF.Silu),
                 reads=[R_cv], writes=[R_sc])
            wada_v = wada_d.rearrange("(kc p) n -> p kc n", p=128)
            NB0 = 2 * D // 512
            for nb in range(NB0):
                b = nb % 2
                for q4 in range(4):
                    S.dma("pool", wa[b][:, q4 * 8:(q4 + 1) * 8, :], wada_v[:, q4 * 8:(q4 + 1) * 8, nb * 512:(nb + 1) * 512],
                          swrites=[R_wa[b]])

                def mm(e, b=b):
                    ins = None
                    for kc in range(KC):
                        ins = e.matmul(ps_m[b][0:2, :], lhsT=sc[:, kc, :], rhs=wa[b][:, kc, :],
                                       start=(kc == 0), stop=(kc == KC - 1))
                    return ins
                S.op("pe", mm, reads=[R_sc, R_wa[b]], writes=[R_psm[b]])
                S.op("dve", lambda e, b=b, nb=nb: e.tensor_tensor(
                    out=modrow[:, nb * 512:(nb + 1) * 512], in0=ps_m[b][0:2, :],
                    in1=bada[:, nb * 512:(nb + 1) * 512], op=ALU.add),
                    reads=[R_psm[b], R_bada], swrites=[R_modrow])
            S.dma("sp", modrow_d[:, 0:2 * D], modrow[:, 0:2 * D], reads=[R_modrow], writes=[R_modrow_d])

            def tr(e):
                ins = None
                for j in range(2):
                    for kc in range(KC):
                        ins = e.transpose(out=ps_t[:, j, kc, :],
                                          in_=modrow[0:2, j * D + kc * 128:j * D + (kc + 1) * 128],
                                          identity=ident_f[0:2, 0:2])
                return ins
            S.op("pe", tr, reads=[R_modrow, R_ident], writes=[R_pst])
            for v in range(2):
                S.op("dve", lambda e, v=v: e.tensor_copy(out=sfeat[:, v, :], in_=ps_t[:, 0, :, v]),
                     reads=[R_pst], swrites=[R_gs])
                S.op("dve", lambda e, v=v: e.scalar_tensor_tensor(
                    out=gfeat[:, v, :], in0=ps_t[:, 1, :, v], scalar=1.0, in1=nw[:],
                    op0=ALU.add, op1=ALU.mult),
                    reads=[R_pst, R_nw], swrites=[R_gs])
            S.barrier()
        if stop_after == 0:
            S.barrier(("sp",))
            return nc

        import os
        TB = 384 if os.environ.get('K_DBG_SMALL') else 1152
        NTB = NTOK // TB
        TPB = TB // 128
        with ExitStack() as p1:
            def sb1(name, shape, dt):
                return p1.enter_context(nc.sbuf_tensor(name, list(shape), dt))
            hT = sb1("hT", [128, KC, TB], BF16)
            wb = [sb1("wb%d" % i, [128, KC, 512], BF16) for i in range(2)]
            xbuf = [sb1("xbuf%d" % i, [128, D], F32) for i in range(2)]
            xsb = sb1("xsb", [128, D], BF16)
            ss = sb1("ss", [128, 2], F32)
            rstd = sb1("rstd", [128, 2], F32)
            stg = [sb1("stg%d" % i, [128, 512], BF16) for i in range(4)]
            stg32 = sb1("stg32", [16, 384], F32)
            stgK = [sb1("stgK%d" % i, [128, 3, 128], BF16) for i in range(2)]
            R_stgK = [Res(), Res()]
            kq = [0]
            bg = sb1("bg", [16, 1], F32)
            ps = [p1.enter_context(nc.psum_tensor("ps%d" % i, [128, 512], F32)) for i in range(6)]
            pst = [p1.enter_context(nc.psum_tensor("pst%d" % i, [128, 8, 128], BF16)) for i in range(2)]
            R_ps = [Res() for _ in range(6)]
            R_pst = [Res() for _ in range(2)]
            R_hT = [Res() for _ in range(TPB)]
            R_wb = [Res(), Res()]
            R_x = [Res(), Res()]
            R_xsb, R_ss, R_rstd = Res(), Res(), Res()
            R_stg = [Res() for _ in range(4)]
            R_stg32 = Res()
            R_bg = Res()
            R_scr = Res("proj scratch")
            S.dma("sp", bg[:], bg_d, writes=[R_bg])
            win_v = win_d.rearrange("(kc p) n -> p kc n", p=128)

            ablocks = []
            for (dst, c0, n) in ((v_d, C_V, 2048), (o_d, C_O, 2048),
                                 (zm_d, C_ZM, 2048), (u_d, C_U, 2048)):
                for j in range(n // 512):
                    ablocks.append(("A", dst, j * 512, c0 + j * 512, 512))
            bblocks = []
            for (dst, c0, n) in ((qT_d, C_Q, 1024), (kT_d, C_K, 1024), (zfT_d, C_ZF, 2048)):
                for j in range(n // 512):
                    bblocks.append(("B", dst, j * 512, c0 + j * 512, 512))
            bblocks.append(("G", gT_d, 0, C_G, 16))
            blocks = ablocks + bblocks
            mblocks = [("M", None, 0, 2 * D + j * 512, 512) for j in range(D // 512)]
            badag = sb1("badag", [2, 512], F32)
            mg = sb1("mg", [2, 512], F32)
            R_badag, R_mg = Res(), Res()
            if os.environ.get('K_DBG_SMALL'):
                blocks = ablocks[:2] + bblocks[:1] + bblocks[-1:]
                NTB = int(os.environ['K_DBG_SMALL'])
            evq = [0]
            psq = [0]
            stq = [0]

            def evac_engine():
                evq[0] += 1
                return "act" if evq[0] % 2 else "dve"

            winb_d = dscr("winb", [len(blocks), 128, KC, 512], BF16)
            R_winb = [Res() for _ in blocks]

            def load_w(i, blk, tb):
                b = i % 2
                kind, dst, dc0, wc0, n = blk
                if kind == "M":
                    for q4 in range(4):
                        S.dma("pool", wb[b][:, q4 * 8:(q4 + 1) * 8, :], wada_v[:, q4 * 8:(q4 + 1) * 8, wc0:wc0 + 512],
                              swrites=[R_wb[b]])
                    return
                if tb == 0 or n < 512:
                    for q4 in range(4):
                        S.dma("pool", wb[b][:, q4 * 8:(q4 + 1) * 8, 0:n], win_v[:, q4 * 8:(q4 + 1) * 8, wc0:wc0 + n],
                              swrites=[R_wb[b]])
                    if n == 512:
                        S.dma("pool", winb_d[i], wb[b][:], reads=[R_wb[b]], writes=[R_winb[i]])
                else:
                    for q4 in range(4):
                        S.dma("pool", wb[b][:, q4 * 8:(q4 + 1) * 8, :], winb_d[i][:, q4 * 8:(q4 + 1) * 8, :],
                              reads=[R_winb[i]], swrites=[R_wb[b]])

            for tb in range(NTB):
                tok0 = tb * TB
                for tt in range(TPB):
                    if os.environ.get('K_DBG_CUT') == '3':
                        break
                    t0 = tok0 + tt * 128
                    v = 0 if t0 < TS else 1
                    xb = (tb * TPB + tt) % 2
                    S.dma("sp", xbuf[xb][:], xrows(t0, 128), writes=[R_x[xb]])
                    S.op("act", lambda e, xb=xb, xc=xb: e.activation(
                        out=xsb[:], in_=xbuf[xb][:], func=AF.Square, accum_out=ss[:, xc:xc + 1]),
                        reads=[R_x[xb]], writes=[R_xsb, R_ss])
                    PL = int(os.environ.get('K_DBG_PREP', '9'))
                    if PL < 2:
                        continue
                    S.op("dve", lambda e, xc=xb: e.tensor_scalar(
                        out=rstd[:, xc:xc + 1], in0=ss[:, xc:xc + 1], scalar1=1.0 / D, scalar2=EPS,
                        op0=ALU.mult, op1=ALU.add), reads=[R_ss], writes=[R_rstd])
                    S.op("act", lambda e, xc=xb: e.activation(
                        out=rstd[:, xc:xc + 1], in_=rstd[:, xc:xc + 1], func=AF.Sqrt),
                        reads=[R_rstd], writes=[R_rstd])
                    S.op("dve", lambda e, xc=xb: e.reciprocal(
                        out=rstd[:, xc:xc + 1], in_=rstd[:, xc:xc + 1]),
                        reads=[R_rstd], writes=[R_rstd])
                    if PL < 3:
                        continue
                    S.op("dve", lambda e, xb=xb, xc=xb: e.tensor_scalar(
                        out=xsb[:], in0=xbuf[xb][:], scalar1=rstd[:, xc:xc + 1], scalar2=None,
                        op0=ALU.mult), reads=[R_x[xb], R_rstd], writes=[R_xsb])
                    if PL < 4:
                        continue
                    for grp in range(4):
                        pb = grp % 2

                        def trp(e, grp=grp, pb=pb):
                            ins = None
                            for j in range(8):
                                kc = grp * 8 + j
                                ins = e.transpose(out=pst[pb][:, j, :], in_=xsb[:, kc * 128:(kc + 1) * 128],
                                                  identity=ident_b[:])
                            return ins
                        S.op("pe", trp, reads=[R_xsb, R_ident], writes=[R_pst[pb]])
                        if PL < 5:
                            continue
                        for j in range(8):
                            kc = grp * 8 + j
                            eng = "dve"
                            if eng == "act":
                                S.op("act", lambda e, j=j, kc=kc, pb=pb, tt=tt, v=v: e.activation(
                                    out=hT[:, kc, tt * 128:(tt + 1) * 128], in_=pst[pb][:, j, :],
                                    func=AF.Identity, scale=gfeat[:, v, kc:kc + 1], bias=sfeat[:, v, kc:kc + 1]),
                                    reads=[R_pst[pb], R_gs], swrites=[R_hT[tt]])
                            else:
                                S.op("dve", lambda e, j=j, kc=kc, pb=pb, tt=tt, v=v: e.tensor_scalar(
                                    out=hT[:, kc, tt * 128:(tt + 1) * 128], in0=pst[pb][:, j, :],
                                    scalar1=gfeat[:, v, kc:kc + 1], scalar2=sfeat[:, v, kc:kc + 1],
                                    op0=ALU.mult, op1=ALU.add),
                                    reads=[R_pst[pb], R_gs], swrites=[R_hT[tt]])
                if os.environ.get('K_DBG_CUT') == '2':
                    continue
                blks = blocks + mblocks if tb == 0 else blocks
                load_w(0, blks[0], tb)
                for bi, blk in enumerate(blks):
                    if bi + 1 < len(blks):
                        load_w(bi + 1, blks[bi + 1], tb)
                    b = bi % 2
                    kind, dst, dc0, wc0, n = blk
                    if kind == "M":
                        pi = psq[0] % 6
                        psq[0] += 1

                        def mmm(e, b=b, pi=pi):
                            ins = None
                            for kc in range(KC):
                                ins = e.matmul(ps[pi][0:2, :], lhsT=sc[:, kc, :], rhs=wb[b][:, kc, :],
                                               start=(kc == 0), stop=(kc == KC - 1))
                            return ins
                        S.op("pe", mmm, reads=[R_sc, R_wb[b]], writes=[R_ps[pi]])
                        S.dma("sp", badag[:], bada_d[0:1, wc0:wc0 + 512].broadcast_to([2, 512]), writes=[R_badag])
                        S.op("dve", lambda e, pi=pi: e.tensor_tensor(out=mg[:], in0=ps[pi][0:2, :], in1=badag[:], op=ALU.add),
                             reads=[R_ps[pi], R_badag], writes=[R_mg])
                        S.dma("sp", modrow_d[:, wc0:wc0 + 512], mg[:], reads=[R_mg], swrites=[R_modrow_d])
                        continue
                    if kind == "A":
                        for tt in range(TPB):
                            pi = psq[0] % 6
                            psq[0] += 1

                            def mm(e, b=b, tt=tt, pi=pi):
                                ins = None
                                for kc in range(KC):
                                    ins = e.matmul(ps[pi][:, :], lhsT=hT[:, kc, tt * 128:(tt + 1) * 128],
                                                   rhs=wb[b][:, kc, :], start=(kc == 0), stop=(kc == KC - 1))
                                return ins
                            S.op("pe", mm, reads=[R_hT[tt], R_wb[b]], writes=[R_ps[pi]])
                            si = stq[0] % 4
                            stq[0] += 1
                            eng = evac_engine()
                            if eng == "act":
                                S.op("act", lambda e, si=si, pi=pi: e.activation(
                                    out=stg[si][:], in_=ps[pi][:], func=AF.Copy),
                                    reads=[R_ps[pi]], writes=[R_stg[si]])
                            else:
                                S.op("dve", lambda e, si=si, pi=pi: e.tensor_copy(
                                    out=stg[si][:], in_=ps[pi][:]),
                                    reads=[R_ps[pi]], writes=[R_stg[si]])
                            t0 = tok0 + tt * 128
                            S.dma("sp", dst[t0:t0 + 128, dc0:dc0 + 512], stg[si][:],
                                  reads=[R_stg[si]], swrites=[R_scr])
                    else:
                        nch = (n + 127) // 128
                        for ch in range(nch):
                            m = 128
                            for tg in range(TB // 384):
                                pi = psq[0] % 6
                                psq[0] += 1

                                def mm(e, b=b, ch=ch, m=m, tg=tg, pi=pi):
                                    ins = None
                                    for kc in range(KC):
                                        ins = e.matmul(ps[pi][0:m, 0:384], lhsT=wb[b][:, kc, ch * 128:ch * 128 + m],
                                                       rhs=hT[:, kc, tg * 384:(tg + 1) * 384],
                                                       start=(kc == 0), stop=(kc == KC - 1))
                                    return ins
                                S.op("pe", mm, reads=[R_hT[3 * tg], R_hT[3 * tg + 1], R_hT[3 * tg + 2], R_wb[b]],
                                     writes=[R_ps[pi]])
                                t0 = tok0 + tg * 384
                                if kind == "G":
                                    S.op("act", lambda e, pi=pi: e.activation(
                                        out=stg32[:], in_=ps[pi][0:16, 0:384], func=AF.Identity,
                                        bias=bg[:, 0:1], scale=1.0),
                                        reads=[R_ps[pi], R_bg], writes=[R_stg32])
                                    S.dma("sp", dst[:, t0:t0 + 384], stg32[:], reads=[R_stg32], swrites=[R_scr])
                                    continue
                                si = stq[0] % 4
                                stq[0] += 1
                                eng = evac_engine()
                                if eng == "act":
                                    S.op("act", lambda e, si=si, pi=pi: e.activation(
                                        out=stg[si][:, 0:384], in_=ps[pi][:, 0:384], func=AF.Copy),
                                        reads=[R_ps[pi]], writes=[R_stg[si]])
                                else:
                                    S.op("dve", lambda e, si=si, pi=pi: e.tensor_copy(
                                        out=stg[si][:, 0:384], in_=ps[pi][:, 0:384]),
                                        reads=[R_ps[pi]], writes=[R_stg[si]])
                                r0 = dc0 + ch * 128
                                S.dma("sp", dst[r0:r0 + 128, t0:t0 + 384], stg[si][:, 0:384],
                                      reads=[R_stg[si]], swrites=[R_scr])
                                if dst is kT_d:
                                    kq[0] += 1
                                    pb = kq[0] % 2

                                    def trk(e, si=si, pb=pb):
                                        ins = None
                                        for j in range(3):
                                            ins = e.transpose(out=pst[pb][:, j, :], in_=stg[si][:, j * 128:(j + 1) * 128],
                                                              identity=ident_b[:])
                                        return ins
                                    S.op("pe", trk, reads=[R_stg[si], R_ident], writes=[R_pst[pb]])
                                    S.op("dve", lambda e, pb=pb: e.tensor_copy(out=stgK[pb][:], in_=pst[pb][:, 0:3, :]),
                                         reads=[R_pst[pb]], writes=[R_stgK[pb]])
                                    S.dma("sp", ktok_d[t0:t0 + 384, r0:r0 + 128].rearrange("(j p) f -> p j f", p=128),
                                          stgK[pb][:], reads=[R_stgK[pb]], swrites=[R_scr])
            S.barrier()
        if stop_after == 1:
            S.barrier(("sp",))
            return nc


        seqs = [(0, TS), (TS, TP), (TS + TP, TP)]
        p23 = ExitStack()

        def sb23(name, shape, dt):
            return p23.enter_context(nc.sbuf_tensor(name, list(shape), dt))
        ncm = [sb23("ncm%d" % d, [128, NTOK], F32) for d in range(2)]
        ucol = sb23("ucol", [128, NT, 2, 4], F32)
        emtcol = sb23("emtcol", [128, NT, 2, 4], F32)
        R_ncm = [Res(), Res()]
        R_ucol, R_emtcol = Res(), Res()
        R_nm = Res()
        with ExitStack() as p2:
            def sb2(name, shape, dt):
                return p2.enter_context(nc.sbuf_tensor(name, list(shape), dt))
            gi = sb2("gi", [4, NTOK], F32)
            gf = sb2("gf", [4, NTOK], F32)
            Bp = sb2("Bp", [4, NTOK], F32)
            cmr = sb2("cmr", [4, NTOK], F32)
            ones4 = sb2("ones4", [4, NTOK], F32)
            m0 = sb2("m0", [4, 2, 2], F32)
            nmt = sb2("nmt", [4, 2, 2], F32)
            R_nmt = Res()
            ps_u_t = p2.enter_context(nc.psum_tensor("ps_u", [128, 512], F32))
            ps_e_t = p2.enter_context(nc.psum_tensor("ps_e", [128, 512], F32))
            ps_u = ps_u_t[:, 0:NT * 4].rearrange("p (t h) -> p t h", h=4)
            ps_e = ps_e_t[:, 0:NT * 4].rearrange("p (t h) -> p t h", h=4)
            R_gi, R_gf, R_Bp, R_cmr, R_ones, R_m0 = Res(), Res(), Res(), Res(), Res(), Res()
            R_psu, R_pse = Res(), Res()
            S.op("dve", lambda e: e.memset(ones4[:], 1.0), writes=[R_ones])
            for d_ in range(2):
                S.op("dve", lambda e, d_=d_: e.memset(ncm[d_][:], 0.0), writes=[R_ncm[d_]])
            S.op("dve", lambda e: e.memset(m0[:], 0.0), writes=[R_m0])
            for d in range(2):
                S.dma("sp", m0[:, d, 0:1], sm_d[d:d + 1, :].rearrange("o h -> h o"), reads=[], swrites=[R_m0])
            for d in range(2):
                S.dma("sp", gi[:], gT_d[8 * d:8 * d + 4, :], reads=[R_scr], writes=[R_gi])
                S.dma("sp", gf[:], gT_d[8 * d + 4:8 * d + 8, :], reads=[R_scr], writes=[R_gf])
                S.op("act", lambda e: e.activation(out=gf[:], in_=gf[:], func=AF.Exp, scale=-1.0),
                     reads=[R_gf], writes=[R_gf])
                S.op("dve", lambda e: e.tensor_scalar_add(out=gf[:], in0=gf[:], scalar1=1.0),
                     reads=[R_gf], writes=[R_gf])
                S.op("act", lambda e: e.activation(out=gf[:], in_=gf[:], func=AF.Ln),
                     reads=[R_gf], writes=[R_gf])

                def dirv(ap_):
                    return ap_ if d == 0 else ap_[:, ::-1]
                for (t0, T) in seqs:
                    S.op("dve", lambda e, t0=t0, T=T: e.tensor_tensor_scan(
                        out=dirv(Bp[:, t0:t0 + T]), data0=dirv(ones4[:, t0:t0 + T]), data1=dirv(gf[:, t0:t0 + T]),
                        initial=0.0, op0=ALU.mult, op1=ALU.add),
                        reads=[R_gf, R_ones], swrites=[R_Bp])
                S.op("dve", lambda e: e.tensor_tensor(out=gi[:], in0=gi[:], in1=Bp[:], op=ALU.add),
                     reads=[R_Bp, R_gi], writes=[R_gi])
                for si, (t0, T) in enumerate(seqs):
                    mi = 0 if si == 0 else 1
                    S.op("dve", lambda e, t0=t0, T=T, mi=mi: e.tensor_tensor_scan(
                        out=dirv(cmr[:, t0:t0 + T]), data0=dirv(gi[:, t0:t0 + T]), data1=dirv(gi[:, t0:t0 + T]),
                        initial=m0[:, d, mi:mi + 1], op0=ALU.max, op1=ALU.max),
                        reads=[R_gi, R_m0], swrites=[R_cmr])
                S.op("dve", lambda e: e.tensor_scalar_mul(out=ncm[d][0:4, :], in0=cmr[:], scalar1=-1.0),
                     reads=[R_cmr], writes=[R_ncm[d]])
                S.op("dve", lambda e: e.tensor_tensor(out=Bp[:], in0=cmr[:], in1=Bp[:], op=ALU.subtract),
                     reads=[R_cmr, R_Bp], writes=[R_Bp])
                for si in (1, 2):
                    t0, T = seqs[si]
                    idx = t0 + T - 1 if d == 0 else t0
                    S.op("dve", lambda e, si=si, idx=idx: e.tensor_copy(out=nmt[:, si - 1, d:d + 1], in_=Bp[:, idx:idx + 1]),
                         reads=[R_Bp], swrites=[R_nmt])
                S.op("act", lambda e: e.activation(out=cmr[:], in_=Bp[:], func=AF.Exp, scale=-1.0),
                     reads=[R_Bp], writes=[R_cmr])

                def tru(e):
                    ins = None
                    for tt in range(NT):
                        ins = e.transpose(out=ps_u[:, tt, :], in_=gi[0:4, tt * 128:(tt + 1) * 128],
                                          identity=ident_f[0:4, 0:4])
                    return ins
                S.op("pe", tru, reads=[R_gi, R_ident], writes=[R_psu])

                def tre(e):
                    ins = None
                    for tt in range(NT):
                        ins = e.transpose(out=ps_e[:, tt, :], in_=cmr[0:4, tt * 128:(tt + 1) * 128],
                                          identity=ident_f[0:4, 0:4])
                    return ins
                S.op("pe", tre, reads=[R_cmr, R_ident], writes=[R_pse])
                S.op("dve", lambda e: e.tensor_copy(out=ucol[:, :, d, :], in_=ps_u),
                     reads=[R_psu], swrites=[R_ucol])
                S.op("dve", lambda e: e.tensor_copy(out=emtcol[:, :, d, :], in_=ps_e),
                     reads=[R_pse], swrites=[R_emtcol])
            with nc.allow_non_contiguous_dma(reason="tiny new_m output"):
                S.dma("sp", nm_d.rearrange("s d h -> h s d"), nmt[:], reads=[R_nmt], swrites=[R_nm])
            S.barrier()
        if stop_after == 2:
            S.barrier(("sp",))
            return nc

        with ExitStack() as p3:
            def sb3(name, shape, dt):
                return p3.enter_context(nc.sbuf_tensor(name, list(shape), dt))
            sel = sb3("sel_sb", [128, 4, 128], F32)
            maskf = sb3("maskf", [128, 2, 128], F32)
            maskb = sb3("maskb_sb", [128, 2, 128], BF16)
            ones_b = sb3("ones_b", [128, 1], BF16)
            hnwT = sb3("hnwT_sb", [128, 16], F32)
            qTc = [sb3("qTc%d" % i, [128, 8, 128], BF16) for i in range(2)]
            kTc = [sb3("kTc%d" % i, [128, 8, 128], BF16) for i in range(2)]
            ktc = [sb3("ktc%d" % i, [128, 1024], BF16) for i in range(2)]
            vc = [sb3("vc%d" % i, [128, 2048], BF16) for i in range(2)]
            oc = [sb3("oc%d" % i, [128, 2048], BF16) for i in range(2)]
            zc = [sb3("zc%d" % i, [128, 2048], BF16) for i in range(2)]
            hfc = [sb3("hfc%d" % i, [128, 2048], F32) for i in range(2)]
            gmul = sb3("gmul", [128, 2048], F32)
            gsil = sb3("gsil", [128, 2048], F32)
            hfo = [sb3("hfo%d" % i, [128, 2048], F32) for i in range(2)]
            Cf = sb3("Cf", [128, H, 2, DV], F32)
            Cb = sb3("Cb", [128, H, 2, DV], BF16)
            nf = sb3("nf", [128, H, 2], F32)
            nb = sb3("nb", [128, H, 2], BF16)
            cmprev = sb3("cmprev", [128, H], F32)
            DT = [sb3("DT%d" % i, [128, 128], F32) for i in range(2)]
            inter = [sb3("inter%d" % i, [128, 128], F32) for i in range(2)]
            sTm = [sb3("sTm%d" % i, [128, 128], BF16) for i in range(2)]
            qs = [sb3("qs%d" % i, [128, 2, 128], BF16) for i in range(2)]
            kw = [sb3("kw%d" % i, [128, 256], BF16) for i in range(2)]
            kws = [sb3("kws%d" % i, [128, 1], F32) for i in range(2)]
            rd = [sb3("rd%d" % i, [128, 1], F32) for i in range(2)]
            hm = [sb3("hm%d" % i, [128, 512], F32) for i in range(2)]
            sq = [sb3("sq%d" % i, [128, 1], F32) for i in range(2)]
            mtok = [sb3("mtok%d" % i, [128, 512], BF16) for i in range(2)]
            mTs = [sb3("mTs%d" % i, [128, 16, 128], BF16) for i in range(2)]
            psX = [p3.enter_context(nc.psum_tensor("psX%d" % i, [128, 512], F32)) for i in range(2)]
            psD = [p3.enter_context(nc.psum_tensor("psD%d" % i, [128, 512], F32)) for i in range(2)]
            psF = p3.enter_context(nc.psum_tensor("psF", [128, 2, 512], F32))
            psEG = p3.enter_context(nc.psum_tensor("psEG", [128, 512], F32))
            psT = p3.enter_context(nc.psum_tensor("psT", [128, 4, 128], BF16))
            R_sel, R_mask, R_onesb, R_hnw = Res(), Res(), Res(), Res()
            R_q = [Res(), Res()]; R_k = [Res(), Res()]; R_kt = [Res(), Res()]; R_v = [Res(), Res()]
            R_o = [Res(), Res()]; R_z = [Res(), Res()]; R_hfc = [Res(), Res()]
            R_gmul, R_gsil = Res(), Res()
            R_hfo = [Res(), Res()]
            R_Cf = [Res() for _ in range(H)]; R_Cb = [Res() for _ in range(H)]
            R_nf = [Res() for _ in range(H)]; R_nb = [Res() for _ in range(H)]
            R_cmp = [Res() for _ in range(H)]
            R_DT = [Res(), Res()]; R_inter = [Res(), Res()]; R_sTm = [Res(), Res()]; R_qs = [Res(), Res()]
            R_kw = [Res(), Res()]; R_kws = [Res(), Res()]; R_rd = [Res(), Res()]; R_hm = [Res(), Res()]
            R_sq = [Res(), Res()]; R_mtok = [Res(), Res()]; R_mTs = [Res(), Res()]
            R_A = [Res(), Res()]; R_B = R_A; R_C = R_A
            R_EG = Res(); R_E = [R_EG, R_EG]; R_G = R_E
            R_D = [Res(), Res()]; R_F = Res(); R_T = Res()
            R_hfd = Res("hf dram"); R_mixT = Res("mixT dram"); R_state = Res("state out")
            hf_d = dscr("hf", [NTOK, 2048], F32)
            mixT_d = dscr("mixT", [D, NTOK], BF16)
            S.op("dve", lambda e: e.memset(sel[:], 0.0), writes=[R_sel])
            S.dma("sp", sel[0:4, :, :], sel_d, writes=[R_sel])
            S.dma("sp", maskf[:], mask_d.rearrange("d s t -> s d t"), writes=[R_mask])
            S.op("dve", lambda e: e.tensor_copy(out=maskb[:], in_=maskf[:]), reads=[R_mask], writes=[R_mask])
            S.op("dve", lambda e: e.memset(ones_b[:], 1.0), writes=[R_onesb])
            S.dma("sp", hnwT[:], hnwT_d, writes=[R_hnw])
            qT_v = qT_d.rearrange("(j p) t -> p j t", p=128)
            kT_v = kT_d.rearrange("(j p) t -> p j t", p=128)
            mixT_v = mixT_d.rearrange("(j p) t -> p j t", p=128)
            gmulb = [gmul, p3.enter_context(nc.sbuf_tensor("gmul1", [128, 2048], F32))]
            nrow = sb3("nrow", [8, 128], F32)
            R_nrow = Res()
            R_gm = [R_gmul, Res()]
            hix = [0]

            def issue_loads(t0, cb, d):
                S.dma("sp", qTc[cb][:], qT_v[:, :, t0:t0 + 128], reads=[R_scr], writes=[R_q[cb]])
                S.dma("sp", kTc[cb][:], kT_v[:, :, t0:t0 + 128], reads=[R_scr], writes=[R_k[cb]])
                S.dma("sp", ktc[cb][:], ktok_d[t0:t0 + 128, :], reads=[R_scr], writes=[R_kt[cb]])
                S.dma("sp", vc[cb][:], v_d[t0:t0 + 128, :], reads=[R_scr], writes=[R_v[cb]])
                if d == 1:
                    S.dma("sp", oc[cb][:], o_d[t0:t0 + 128, :], reads=[R_scr], writes=[R_o[cb]])
                    S.dma("sp", zc[cb][:], zm_d[t0:t0 + 128, :], reads=[R_scr], writes=[R_z[cb]])
                    S.dma("sp", hfc[cb][:], hf_d[t0:t0 + 128, :], reads=[R_hfd], writes=[R_hfc[cb]])

            def gate_prep(cb):
                S.op("act", lambda e: e.activation(out=gmulb[cb][:], in_=oc[cb][:], func=AF.Sigmoid),
                     reads=[R_o[cb]], writes=[R_gm[cb]])
                S.op("act", lambda e: e.activation(out=gsil[:], in_=zc[cb][:], func=AF.Silu),
                     reads=[R_z[cb]], writes=[R_gsil])
                S.op("dve", lambda e: e.tensor_tensor(out=gmulb[cb][:], in0=gmulb[cb][:], in1=gsil[:], op=ALU.mult),
                     reads=[R_gsil, R_gm[cb]], writes=[R_gm[cb]])

            def stage1(cx):
                d, t0, tt, cb, h, sl, eidx = cx["d"], cx["t0"], cx["tt"], cx["cb"], cx["h"], cx["sl"], cx["eidx"]
                A = psX[sl][:, 0:128]
                Bm = psX[sl][:, 128:256]
                Cs = psX[sl][:, 256:384]
                ncm_c = ncm[d][:, t0:t0 + 128]
                S.op("pe", lambda e: e.matmul(A, lhsT=sel[:, h, :], rhs=ncm_c, start=True, stop=True),
                     reads=[R_sel, R_ncm[d]], swrites=[R_A[sl]])

                def mmB(e):
                    e.matmul(Bm, lhsT=sel[:, h, :], rhs=ncm_c, start=True, stop=False)
                    return e.matmul(Bm, lhsT=ident_b[:], rhs=maskb[:, d, :], start=False, stop=True)
                S.op("pe", mmB, reads=[R_sel, R_ncm[d], R_ident, R_mask], swrites=[R_B[sl]])

                def mmC(e):
                    e.matmul(Cs, lhsT=kTc[cb][:, 2 * h, :], rhs=qTc[cb][:, 2 * h, :], start=True, stop=False)
                    return e.matmul(Cs, lhsT=kTc[cb][:, 2 * h + 1, :], rhs=qTc[cb][:, 2 * h + 1, :],
                                    start=False, stop=True)
                S.op("pe", mmC, reads=[R_q[cb], R_k[cb]], swrites=[R_C[sl]])
                uc = ucol[:, tt, d, h:h + 1]
                S.op("act", lambda e: e.activation(out=DT[sl][:], in_=Bm, func=AF.Exp, bias=uc, scale=1.0),
                     reads=[R_B[sl], R_ucol], writes=[R_DT[sl]])
                S.op("act", lambda e: e.activation(out=inter[sl][:], in_=A, func=AF.Exp, bias=cmprev[:, h:h + 1], scale=1.0),
                     reads=[R_A[sl], R_cmp[h]], writes=[R_inter[sl]])
                S.op("act", lambda e: e.activation(out=kws[sl][:], in_=A[:, eidx:eidx + 1], func=AF.Exp, bias=uc, scale=1.0),
                     reads=[R_A[sl], R_ucol], writes=[R_kws[sl]])
                S.op("dve", lambda e: e.tensor_scalar_mul(out=cmprev[:, h:h + 1], in0=A[:, eidx:eidx + 1], scalar1=-1.0),
                     reads=[R_A[sl]], writes=[R_cmp[h]])
                S.op("dve", lambda e: e.scalar_tensor_tensor(
                    out=sTm[sl][:], in0=Cs, scalar=0.0625, in1=DT[sl][:], op0=ALU.mult, op1=ALU.mult),
                    reads=[R_C[sl], R_DT[sl]], writes=[R_sTm[sl]])
                S.op("dve", lambda e: e.tensor_tensor(
                    out=qs[sl][:], in0=qTc[cb][:, 2 * h:2 * h + 2, :],
                    in1=inter[sl][:].rearrange("p (o t) -> p o t", o=1).broadcast_to([128, 2, 128]), op=ALU.mult),
                    reads=[R_q[cb], R_inter[sl]], writes=[R_qs[sl]])
                S.op("dve", lambda e: e.tensor_scalar(
                    out=kw[sl][:], in0=ktc[cb][:, h * 256:(h + 1) * 256], scalar1=kws[sl][:, 0:1],
                    scalar2=0.0625, op0=ALU.mult, op1=ALU.mult),
                    reads=[R_kt[cb], R_kws[sl]], writes=[R_kw[sl]])

            def stage2(cx):
                d, t0, tt, cb, h, sl, eidx = cx["d"], cx["t0"], cx["tt"], cx["cb"], cx["h"], cx["sl"], cx["eidx"]
                Ed = psEG[:, 8 * sl:8 * sl + 1]
                Gn = psEG[:, 8 * sl + 2:8 * sl + 4]

                def mmD(e):
                    e.matmul(psD[sl][:], lhsT=sTm[sl][:], rhs=vc[cb][:, h * 512:(h + 1) * 512], start=True, stop=False)
                    e.matmul(psD[sl][:], lhsT=qs[sl][:, 0, :], rhs=Cb[:, h, 0, :], start=False, stop=False)
                    return e.matmul(psD[sl][:], lhsT=qs[sl][:, 1, :], rhs=Cb[:, h, 1, :], start=False, stop=True)
                S.op("pe", mmD, reads=[R_sTm[sl], R_v[cb], R_qs[sl], R_Cb[h]], writes=[R_D[sl]])

                def mmE(e):
                    e.matmul(Ed, lhsT=sTm[sl][:], rhs=ones_b[:], start=True, stop=False)
                    e.matmul(Ed, lhsT=qs[sl][:, 0, :], rhs=nb[:, h, 0:1], start=False, stop=False)
                    return e.matmul(Ed, lhsT=qs[sl][:, 1, :], rhs=nb[:, h, 1:2], start=False, stop=True)
                S.op("pe", mmE, reads=[R_sTm[sl], R_onesb, R_qs[sl], R_nb[h]], swrites=[R_E[sl]])

                def mmF(e):
                    e.matmul(psF[:, 0, :], lhsT=kw[sl][:, 0:128], rhs=vc[cb][:, h * 512:(h + 1) * 512], start=True, stop=True)
                    return e.matmul(psF[:, 1, :], lhsT=kw[sl][:, 128:256], rhs=vc[cb][:, h * 512:(h + 1) * 512],
                                    start=True, stop=True)
                S.op("pe", mmF, reads=[R_kw[sl], R_v[cb]], writes=[R_F])

                def mmG(e):
                    e.matmul(Gn[:, 0:1], lhsT=kw[sl][:, 0:128], rhs=ones_b[:], start=True, stop=True)
                    return e.matmul(Gn[:, 1:2], lhsT=kw[sl][:, 128:256], rhs=ones_b[:], start=True, stop=True)
                S.op("pe", mmG, reads=[R_kw[sl], R_onesb], swrites=[R_G[sl]])
                dec = inter[sl][:, eidx:eidx + 1]
                S.op("dve", lambda e: e.tensor_scalar_mul(out=rd[sl][:], in0=Ed, scalar1=-1.0),
                     reads=[R_E[sl]], writes=[R_rd[sl]])
                S.op("dve", lambda e: e.scalar_tensor_tensor(
                    out=rd[sl][:], in0=Ed, scalar=emtcol[:, tt, d, h:h + 1], in1=rd[sl][:], op0=ALU.max, op1=ALU.max),
                    reads=[R_E[sl], R_emtcol, R_rd[sl]], writes=[R_rd[sl]])
                S.op("dve", lambda e: e.scalar_tensor_tensor(
                    out=nf[:, h, :], in0=nf[:, h, :], scalar=dec, in1=Gn, op0=ALU.mult, op1=ALU.add),
                    reads=[R_G[sl], R_inter[sl], R_nf[h]], writes=[R_nf[h]])
                S.op("dve", lambda e: e.reciprocal(out=rd[sl][:], in_=rd[sl][:]),
                     reads=[R_rd[sl]], writes=[R_rd[sl]])
                S.op("dve", lambda e: e.tensor_copy(out=nb[:, h, :], in_=nf[:, h, :]),
                     reads=[R_nf[h]], writes=[R_nb[h]])
                S.op("dve", lambda e: e.scalar_tensor_tensor(
                    out=Cf[:, h, :, :], in0=Cf[:, h, :, :], scalar=dec, in1=psF[:, :, :], op0=ALU.mult, op1=ALU.add),
                    reads=[R_F, R_inter[sl], R_Cf[h]], writes=[R_Cf[h]])

            def stage2b(cx):
                d, t0, tt, cb, h, sl, eidx = cx["d"], cx["t0"], cx["tt"], cx["cb"], cx["h"], cx["sl"], cx["eidx"]
                S.op("act", lambda e: e.activation(out=Cb[:, h, :, :], in_=Cf[:, h, :, :], func=AF.Copy),
                     reads=[R_Cf[h]], writes=[R_Cb[h]])
                if d == 0:
                    S.op("act", lambda e: e.activation(
                        out=hfo[cb][:, h * 512:(h + 1) * 512], in_=psD[sl][:], func=AF.Copy, scale=rd[sl][:, 0:1]),
                        reads=[R_D[sl], R_rd[sl]], swrites=[R_hfo[cb]])
                    return
                S.op("dve", lambda e: e.scalar_tensor_tensor(
                    out=hm[sl][:], in0=psD[sl][:], scalar=rd[sl][:, 0:1],
                    in1=hfc[cb][:, h * 512:(h + 1) * 512], op0=ALU.mult, op1=ALU.add),
                    reads=[R_D[sl], R_rd[sl], R_hfc[cb]], writes=[R_hm[sl]])
                S.op("act", lambda e: e.activation(out=mtok[sl][:], in_=hm[sl][:], func=AF.Square, accum_out=sq[sl][:]),
                     reads=[R_hm[sl]], writes=[R_mtok[sl], R_sq[sl]])
                S.op("dve", lambda e: e.tensor_scalar(
                    out=sq[sl][:], in0=sq[sl][:], scalar1=1.0 / DV, scalar2=EPS, op0=ALU.mult, op1=ALU.add),
                    reads=[R_sq[sl]], writes=[R_sq[sl]])
                S.op("act", lambda e: e.activation(out=sq[sl][:], in_=sq[sl][:], func=AF.Ln),
                     reads=[R_sq[sl]], writes=[R_sq[sl]])
                S.op("act", lambda e: e.activation(out=sq[sl][:], in_=sq[sl][:], func=AF.Exp, scale=-0.5),
                     reads=[R_sq[sl]], writes=[R_sq[sl]])
                S.op("dve", lambda e: e.scalar_tensor_tensor(
                    out=mtok[sl][:], in0=hm[sl][:], scalar=sq[sl][:, 0:1],
                    in1=gmulb[cb][:, h * 512:(h + 1) * 512], op0=ALU.mult, op1=ALU.mult),
                    reads=[R_hm[sl], R_sq[sl], R_gm[cb]], writes=[R_mtok[sl]])

            def stage2c(cx):
                cb, h, sl = cx["cb"], cx["h"], cx["sl"]

                def trm(e):
                    ins = None
                    for j in range(4):
                        ins = e.transpose(out=psT[:, j, :], in_=mtok[sl][:, j * 128:(j + 1) * 128], identity=ident_b[:])
                    return ins
                S.op("pe", trm, reads=[R_mtok[sl], R_ident], writes=[R_T])
                S.op("dve", lambda e: e.tensor_tensor(
                    out=mTs[cb][:, 4 * h:4 * h + 4, :], in0=psT[:],
                    in1=hnwT[:, 4 * h:4 * h + 4].rearrange("p (j o) -> p j o", o=1).broadcast_to([128, 4, 128]),
                    op=ALU.mult),
                    reads=[R_T, R_hnw], swrites=[R_mTs[cb]])

            cbq = [0]
            for si, (seq0, T) in enumerate(seqs):
                nchunk = T // 128
                for d in range(2):
                    if si == 0:
                        S.dma("sp", Cf[:], sC_d[d].rearrange("h (kc p) e -> p h kc e", p=128), writes=R_Cf)
                        with nc.allow_non_contiguous_dma(reason="tiny state vectors"):
                            S.dma("sp", nf[:], sn_d[d].rearrange("h (kc p) -> p h kc", p=128), writes=R_nf)
                        S.dma("sp", cmprev[:], sm_d[d:d + 1, :].broadcast_to([128, H]), writes=R_cmp)
                    else:
                        S.op("dve", lambda e: e.memset(Cf[:], 0.0), writes=R_Cf)
                        S.op("dve", lambda e: e.memset(nf[:], 0.0), writes=R_nf)
                        S.op("dve", lambda e: e.memset(cmprev[:], 0.0), writes=R_cmp)
                    S.op("pool", lambda e: e.tensor_copy(out=Cb[:], in_=Cf[:]), reads=R_Cf, writes=R_Cb)
                    S.op("pool", lambda e: e.tensor_copy(out=nb[:], in_=nf[:]), reads=R_nf, writes=R_nb)
                    order = list(range(nchunk)) if d == 0 else list(range(nchunk - 1, -1, -1))
                    eidx = 127 if d == 0 else 0
                    ctxs = []
                    for c in order:
                        cb = cbq[0] % 2
                        cbq[0] += 1
                        for h in range(H):
                            t0 = seq0 + c * 128
                            ctxs.append({"d": d, "t0": t0, "tt": t0 // 128, "cb": cb, "h": h,
                                         "sl": hix[0] % 2, "eidx": eidx})
                            hix[0] += 1
                    issue_loads(ctxs[0]["t0"], ctxs[0]["cb"], d)
                    if d == 1:
                        gate_prep(ctxs[0]["cb"])
                    stage1(ctxs[0])
                    def finish(cx):
                        stage2b(cx)
                        if cx["h"] == 3 and d == 0:
                            t0, cb = cx["t0"], cx["cb"]
                            S.dma("pool", hf_d[t0:t0 + 128, :], hfo[cb][:], reads=[R_hfo[cb]], swrites=[R_hfd])

                    def finish2(cx):
                        if d == 0:
                            return
                        stage2c(cx)
                        if cx["h"] == 3:
                            t0, cb = cx["t0"], cx["cb"]
                            for h4 in range(4):
                                S.dma("pool", mixT_v[:, 4 * h4:4 * h4 + 4, t0:t0 + 128], mTs[cb][:, 4 * h4:4 * h4 + 4, :],
                                      reads=[R_mTs[cb]], swrites=[R_mixT])

                    for i, cx in enumerate(ctxs):
                        nxt = ctxs[i + 4] if i + 4 < len(ctxs) else None
                        if i + 1 < len(ctxs):
                            stage1(ctxs[i + 1])
                        stage2(cx)
                        if i >= 1:
                            finish(ctxs[i - 1])
                        if i >= 2:
                            finish2(ctxs[i - 2])
                        if cx["h"] == 0 and nxt is not None:
                            issue_loads(nxt["t0"], nxt["cb"], d)
                        if cx["h"] == 1 and d == 1 and nxt is not None:
                            gate_prep(nxt["cb"])
                    finish(ctxs[-1])
                    if len(ctxs) >= 2:
                        finish2(ctxs[-2])
                    finish2(ctxs[-1])
                    if si > 0:
                        S.dma("sp", nC_d[si - 1, d].rearrange("h (kc p) e -> p h kc e", p=128), Cf[:],
                              reads=R_Cf, swrites=[R_state])
                        S.op("pe", lambda e: e.transpose(out=psX[0][0:8, 0:128], in_=nf[:].rearrange("p h k -> p (h k)"),
                                                         identity=ident_f[:]),
                             reads=R_nf + [R_ident], writes=[R_A[0]])
                        S.op("dve", lambda e: e.tensor_copy(out=nrow[:], in_=psX[0][0:8, 0:128]),
                             reads=[R_A[0]], writes=[R_nrow])
                        S.dma("sp", nn_d[si - 1, d].rearrange("h (kc p) -> (h kc) p", p=128), nrow[:],
                              reads=[R_nrow], swrites=[R_state])
            S.barrier()
        p23.close()
        if stop_after == 3:
            S.barrier(("sp",))
            return nc


        Ef_d = dscr("Efft", [2, TS, 2048], BF16)
        with ExitStack() as p4:
            def sb4(name, shape, dt):
                return p4.enter_context(nc.sbuf_tensor(name, list(shape), dt))
            bd_f = sb4("bd_f", [128, 4, 128], F32)
            bd_b = sb4("bd_b", [128, 4, 128], BF16)
            rp_f = sb4("rp_f", [128, 2, 512], F32)
            rp_b = sb4("rp_b", [128, 2, 512], BF16)
            cs_f = sb4("cs_f", [128, 2, 2, 256], F32)
            cs_b = sb4("cs_b", [128, 2, 2, 256], BF16)
            w4 = sb4("w4", [128, 8, 2, 256], BF16)
            CWt = sb4("CWt", [128, 8, 2, 2, 256], BF16)
            U = [sb4("U%d" % i, [128, 2048], BF16) for i in range(2)]
            E1 = [[sb4("E1_%d_%d" % (i, r), [128, 2048], BF16) for r in range(2)] for i in range(2)]
            Zg = sb4("Zg", [128, 32, 2, 256], BF16)
            PT = sb4("PT", [128, 2, 2, TS], BF16)
            zf = [sb4("zf%d" % i, [128, TS], BF16) for i in range(2)]
            sz = sb4("sz", [128, TS], F32)
            fo = [sb4("fo%d" % i, [128, 512], BF16) for i in range(2)]
            Zp = sb4("Zp", [128, 2, 2048], BF16)
            PTp = sb4("PTp", [128, 16, 2, 256], BF16)
            ps4 = [p4.enter_context(nc.psum_tensor("ps4_%d" % i, [128, 512], F32)) for i in range(6)]
            R_c4 = Res(); R_w4 = Res(); R_CW = Res()
            R_U = [Res(), Res()]; R_E1 = [[Res(), Res()], [Res(), Res()]]
            R_Zg, R_PT, R_sz, R_Zp, R_PTp = Res(), Res(), Res(), Res(), Res()
            R_zf = [Res(), Res()]; R_fo = [Res(), Res()]
            R_ps4 = [Res() for _ in range(6)]
            R_Ef = Res("Ef dram")
            p4q = [0]

            def nps():
                p4q[0] += 1
                return p4q[0] % 6
            S.dma("sp", bd_f[:], bd_d, writes=[R_c4])
            S.dma("sp", rp_f[:], rp_d, swrites=[R_c4])
            S.dma("sp", cs_f[:], cs_d, swrites=[R_c4])
            S.op("dve", lambda e: e.tensor_copy(out=bd_b[:], in_=bd_f[:]), reads=[R_c4], swrites=[R_c4])
            S.op("dve", lambda e: e.tensor_copy(out=rp_b[:], in_=rp_f[:]), reads=[R_c4], swrites=[R_c4])
            S.op("dve", lambda e: e.tensor_copy(out=cs_b[:], in_=cs_f[:]), reads=[R_c4], swrites=[R_c4])
            w4_v = wfour_d.rearrange("g (j p) d -> p g j d", p=128)
            for g4 in range(4):
                S.dma("pool", w4[:, 2 * g4:2 * g4 + 2, :, :], w4_v[:, 2 * g4:2 * g4 + 2, :, :], swrites=[R_w4])
            for g in range(8):
                for cch in range(2):
                    for ri in range(2):
                        pi = nps()

                        def mmcw(e, g=g, cch=cch, ri=ri, pi=pi):
                            e.matmul(ps4[pi][:, 0:256], lhsT=cs_b[:, 0, ri, cch * 128:(cch + 1) * 128],
                                     rhs=w4[:, g, 0, :], start=True, stop=False)
                            return e.matmul(ps4[pi][:, 0:256], lhsT=cs_b[:, 1, ri, cch * 128:(cch + 1) * 128],
                                            rhs=w4[:, g, 1, :], start=False, stop=True)
                        S.op("pe", mmcw, reads=[R_c4, R_w4], writes=[R_ps4[pi]])
                        S.op("dve", lambda e, g=g, cch=cch, ri=ri, pi=pi: e.tensor_copy(
                            out=CWt[:, g, cch, ri, :], in_=ps4[pi][:, 0:256]),
                            reads=[R_ps4[pi]], swrites=[R_CW])
            for j in range(32):
                ub = j % 2
                S.dma("sp", U[ub][0:64, :], u_d[2 * j:TS:64, :], reads=[R_scr], writes=[R_U[ub]])
                S.dma("sp", U[ub][64:128, :], u_d[2 * j + 1:TS:64, :], reads=[R_scr], swrites=[R_U[ub]])
                for ri in range(2):
                    for chb in range(4):
                        pi = nps()
                        S.op("pe", lambda e, ub=ub, ri=ri, chb=chb, pi=pi: e.matmul(
                            ps4[pi][:], lhsT=bd_b[:, ri, :], rhs=U[ub][:, chb * 512:(chb + 1) * 512],
                            start=True, stop=True),
                            reads=[R_c4, R_U[ub]], writes=[R_ps4[pi]])
                        if chb % 2:
                            S.op("act", lambda e, ub=ub, ri=ri, chb=chb, pi=pi: e.activation(
                                out=E1[ub][ri][:, chb * 512:(chb + 1) * 512], in_=ps4[pi][:], func=AF.Copy),
                                reads=[R_ps4[pi]], swrites=[R_E1[ub][ri]])
                        else:
                            S.op("dve", lambda e, ub=ub, ri=ri, chb=chb, pi=pi: e.tensor_copy(
                                out=E1[ub][ri][:, chb * 512:(chb + 1) * 512], in_=ps4[pi][:]),
                                reads=[R_ps4[pi]], swrites=[R_E1[ub][ri]])
                    S.dma("sp", Ef_d[ri, 2 * j:TS:64, :], E1[ub][ri][0:64, :], reads=[R_E1[ub][ri]], swrites=[R_Ef])
                    S.dma("sp", Ef_d[ri, 2 * j + 1:TS:64, :], E1[ub][ri][64:128, :], reads=[R_E1[ub][ri]], swrites=[R_Ef])
            def stage3(g, ntok, tok0, PTsrc):
                nb_ = max(1, ntok // 512)
                nn = min(512, ntok)
                for dc in range(2):
                    zb = (2 * g + dc) % 2
                    r0 = g * 256 + dc * 128
                    S.dma("sp", zf[zb][:, 0:ntok], zfT_d[r0:r0 + 128, tok0:tok0 + ntok], reads=[R_scr], writes=[R_zf[zb]])
                    S.op("act", lambda e, zb=zb: e.activation(out=sz[:, 0:ntok], in_=zf[zb][:, 0:ntok], func=AF.Silu),
                         reads=[R_zf[zb]], writes=[R_sz])
                    for tb in range(nb_):
                        pi = nps()

                        def mm3(e, pi=pi, tb=tb, dc=dc):
                            ins = None
                            k = 0
                            for cc in range(2):
                                for ri in range(2):
                                    ins = e.matmul(ps4[pi][:, 0:nn], lhsT=CWt[:, g, cc, ri, dc * 128:(dc + 1) * 128],
                                                   rhs=PTsrc(cc, ri, tb * 512, nn), start=(k == 0), stop=(k == 3))
                                    k += 1
                            return ins
                        S.op("pe", mm3, reads=[R_CW, R_PT, R_PTp], writes=[R_ps4[pi]])
                        fb = p4q[0] % 2
                        S.op("dve", lambda e, pi=pi, fb=fb, tb=tb: e.tensor_tensor(
                            out=fo[fb][:, 0:nn], in0=ps4[pi][:, 0:nn], in1=sz[:, tb * 512:tb * 512 + nn], op=ALU.mult),
                            reads=[R_ps4[pi], R_sz], writes=[R_fo[fb]])
                        S.dma("sp", mixT_d[2048 + r0:2048 + r0 + 128, tok0 + tb * 512:tok0 + tb * 512 + nn],
                              fo[fb][:, 0:nn], reads=[R_fo[fb]], swrites=[R_mixT])

            Ef_v = [Ef_d[ri].rearrange("(i p) c -> p i c", p=128) for ri in range(2)]
            for g in range(8):
                first = True
                for ri in range(2):
                    for i4 in range(4):
                        S.dma("sp", Zg[:, i4 * 8:(i4 + 1) * 8, ri, :], Ef_v[ri][:, i4 * 8:(i4 + 1) * 8, g * 256:(g + 1) * 256],
                              reads=[R_Ef], writes=[R_Zg] if first else [], swrites=[] if first else [R_Zg])
                        first = False
                for i in range(32):
                    for cc in range(2):
                        pi = nps()

                        def mm2(e, i=i, cc=cc, pi=pi):
                            e.matmul(ps4[pi][:, 0:256], lhsT=Zg[:, i, 0, cc * 128:(cc + 1) * 128],
                                     rhs=bd_b[:, 0:2, :], start=True, stop=False)
                            return e.matmul(ps4[pi][:, 0:256], lhsT=Zg[:, i, 1, cc * 128:(cc + 1) * 128],
                                            rhs=bd_b[:, 2:4, :], start=False, stop=True)
                        S.op("pe", mm2, reads=[R_c4, R_Zg], writes=[R_ps4[pi]])
                        eng = "act" if (i + cc) % 2 else "dve"
                        if eng == "act":
                            S.op("act", lambda e, i=i, cc=cc, pi=pi: e.activation(
                                out=PT[:, cc, :, i * 128:(i + 1) * 128],
                                in_=ps4[pi][:, 0:256].rearrange("p (r t) -> p r t", r=2), func=AF.Copy),
                                reads=[R_ps4[pi]], swrites=[R_PT])
                        else:
                            S.op("dve", lambda e, i=i, cc=cc, pi=pi: e.tensor_copy(
                                out=PT[:, cc, :, i * 128:(i + 1) * 128],
                                in_=ps4[pi][:, 0:256].rearrange("p (r t) -> p r t", r=2)),
                                reads=[R_ps4[pi]], swrites=[R_PT])
                stage3(g, TS, 0, lambda cc, ri, t0_, n_: PT[:, cc, ri, t0_:t0_ + n_])
                S.op("dve", lambda e: e.memset(PT[:, 0, 0, 0:1], 0.0), writes=[R_PT])
            for si in (1, 2):
                tok0 = seqs[si][0]
                S.dma("sp", Zp[:], u_d[tok0:tok0 + 256, :].rearrange("(i p) c -> p i c", p=128), reads=[R_scr], writes=[R_Zp])
                for cc in range(16):
                    pi = nps()

                    def mmp(e, cc=cc, pi=pi):
                        e.matmul(ps4[pi][:], lhsT=Zp[:, 0, cc * 128:(cc + 1) * 128], rhs=rp_b[:, 0, :], start=True, stop=False)
                        return e.matmul(ps4[pi][:], lhsT=Zp[:, 1, cc * 128:(cc + 1) * 128], rhs=rp_b[:, 1, :], start=False, stop=True)
                    S.op("pe", mmp, reads=[R_c4, R_Zp], writes=[R_ps4[pi]])
                    S.op("dve", lambda e, cc=cc, pi=pi: e.tensor_copy(
                        out=PTp[:, cc, :, :], in_=ps4[pi][:].rearrange("p (r t) -> p r t", r=2)),
                        reads=[R_ps4[pi]], swrites=[R_PTp])
                for g in range(8):
                    stage3(g, 256, tok0, lambda cc, ri, t0_, n_, g=g: PTp[:, 2 * g + cc, ri, 0:256])
                S.op("dve", lambda e: e.memset(PTp[:, 0, 0, 0:1], 0.0), writes=[R_PTp])
            S.barrier()
        if stop_after == 4:
            S.barrier(("sp",))
            return nc

        with ExitStack() as p5:
            def sb5(name, shape, dt):
                return p5.enter_context(nc.sbuf_tensor(name, list(shape), dt))
            mixTb = sb5("mixTb", [128, KC, 512], BF16)
            yblk = sb5("yblk", [128, 4, D], F32)
            wob = [sb5("wob%d" % i, [128, KC, 512], BF16) for i in range(2)]
            gate_rep = sb5("gate_rep", [128, D], F32)
            fnw_rep = sb5("fnw_rep", [128, D], F32)
            tmp5 = [sb5("tmp5_%d" % i, [128, 512], F32) for i in range(2)]
            junk5 = sb5("junk5", [128, D], BF16)
            ss5 = sb5("ss5", [128, 4], F32)
            psO = [p5.enter_context(nc.psum_tensor("psO%d" % i, [128, 512], F32)) for i in range(4)]
            R_mixTb, R_gate, R_fnw, R_junk5, R_ss5 = Res(), Res(), Res(), Res(), Res()
            R_y = [Res() for _ in range(4)]; R_yx = [Res() for _ in range(4)]
            R_wob = [Res(), Res()]; R_tmp5 = [Res(), Res()]; R_psO = [Res() for _ in range(4)]
            R_yout = Res("y out")
            wout_v = wout_d.rearrange("(kc p) n -> p kc n", p=128)
            mixT_v5 = mixT_d.rearrange("(kc p) t -> p kc t", p=128)
            S.dma("sp", fnw_rep[:], fnw_d.broadcast_to([128, D]), writes=[R_fnw])
            wq = [0]
            o5 = [0]

            woutb_d = dscr("woutb", [8, 128, KC, 512], BF16)
            R_woutb = [Res() for _ in range(8)]

            def load_wo(cb, tb):
                b = wq[0] % 2
                wq[0] += 1
                if tb == 0:
                    for q4 in range(4):
                        S.dma("pool", wob[b][:, q4 * 8:(q4 + 1) * 8, :], wout_v[:, q4 * 8:(q4 + 1) * 8, cb * 512:(cb + 1) * 512],
                              swrites=[R_wob[b]])
                    S.dma("pool", woutb_d[cb], wob[b][:], reads=[R_wob[b]], writes=[R_woutb[cb]])
                else:
                    for q4 in range(4):
                        S.dma("pool", wob[b][:, q4 * 8:(q4 + 1) * 8, :], woutb_d[cb][:, q4 * 8:(q4 + 1) * 8, :],
                              reads=[R_woutb[cb]], swrites=[R_wob[b]])
                return b
            NB5 = NTOK // 512
            for tb in range(NB5):
                tok0 = tb * 512
                v = 0 if tok0 < TS else 1
                if tb == 0 or tb == TS // 512:
                    S.dma("sp", gate_rep[:], modrow_d[v:v + 1, 2 * D:3 * D].broadcast_to([128, D]),
                          reads=[R_modrow_d], writes=[R_gate])
                for tt in range(4):
                    S.dma("sp", yblk[:, tt, :], xrows(tok0 + tt * 128, 128), writes=[R_y[tt], R_yx[tt]])
                S.dma("sp", mixTb[:], mixT_v5[:, :, tok0:tok0 + 512], reads=[R_mixT], writes=[R_mixTb])
                nxt = load_wo(0, tb)
                for cb in range(8):
                    b = nxt
                    if cb + 1 < 8:
                        nxt = load_wo(cb + 1, tb)
                    for tt in range(4):
                        o5[0] += 1
                        pi = o5[0] % 4
                        tb_ = o5[0] % 2

                        def mmo(e, b=b, tt=tt, pi=pi):
                            ins = None
                            for kc in range(KC):
                                ins = e.matmul(psO[pi][:, :], lhsT=mixTb[:, kc, tt * 128:(tt + 1) * 128],
                                               rhs=wob[b][:, kc, :], start=(kc == 0), stop=(kc == KC - 1))
                            return ins
                        S.op("pe", mmo, reads=[R_mixTb, R_wob[b]], writes=[R_psO[pi]])
                        S.op("dve", lambda e, pi=pi, tb_=tb_, cb=cb: e.tensor_tensor(
                            out=tmp5[tb_][:], in0=psO[pi][:, :], in1=gate_rep[:, cb * 512:(cb + 1) * 512], op=ALU.mult),
                            reads=[R_psO[pi], R_gate], writes=[R_tmp5[tb_]])
                        S.op("pool", lambda e, tb_=tb_, tt=tt, cb=cb: e.tensor_tensor(
                            out=yblk[:, tt, cb * 512:(cb + 1) * 512], in0=yblk[:, tt, cb * 512:(cb + 1) * 512],
                            in1=tmp5[tb_][:], op=ALU.add),
                            reads=[R_tmp5[tb_], R_yx[tt]], swrites=[R_y[tt]])
                for tt in range(4):
                    S.op("act", lambda e, tt=tt: e.activation(out=junk5[:], in_=yblk[:, tt, :], func=AF.Square,
                                                              accum_out=ss5[:, tt:tt + 1]),
                         reads=[R_y[tt]], writes=[R_junk5, R_ss5])
                    S.op("dve", lambda e, tt=tt: e.tensor_scalar(
                        out=ss5[:, tt:tt + 1], in0=ss5[:, tt:tt + 1], scalar1=1.0 / D, scalar2=EPS,
                        op0=ALU.mult, op1=ALU.add), reads=[R_ss5], writes=[R_ss5])
                    S.op("act", lambda e, tt=tt: e.activation(out=ss5[:, tt:tt + 1], in_=ss5[:, tt:tt + 1], func=AF.Sqrt),
                         reads=[R_ss5], writes=[R_ss5])
                    S.op("dve", lambda e, tt=tt: e.reciprocal(out=ss5[:, tt:tt + 1], in_=ss5[:, tt:tt + 1]),
                         reads=[R_ss5], writes=[R_ss5])
                    S.op("dve", lambda e, tt=tt: e.scalar_tensor_tensor(
                        out=yblk[:, tt, :], in0=yblk[:, tt, :], scalar=ss5[:, tt:tt + 1], in1=fnw_rep[:],
                        op0=ALU.mult, op1=ALU.mult),
                        reads=[R_ss5, R_fnw, R_y[tt]], writes=[R_y[tt]])
                    t0 = tok0 + tt * 128
                    dst = ys_d[t0:t0 + 128, :] if t0 < TS else yp_d[t0 - TS:t0 - TS + 128, :]
                    S.dma("sp", dst, yblk[:, tt, :], reads=[R_y[tt]], swrites=[R_yout])
            S.barrier()

        S.barrier(("sp",))
    return nc


def make_consts():
    f = np.float32
    c = {}
    c["ident"] = np.eye(128, dtype=f)
    sel = np.zeros((4, 4, 128), f)
    for h in range(4):
        sel[h, h, :] = 1.0
    c["sel"] = sel
    s_idx = np.arange(128)[:, None]
    t_idx = np.arange(128)[None, :]
    mk = np.zeros((2, 128, 128), f)
    mk[0][s_idx > t_idx] = -30000.0
    mk[1][s_idx < t_idx] = -30000.0
    c["maskb"] = mk
    a = np.arange(64, dtype=np.float64)
    th = 2 * np.pi * np.outer(a, a) / 64.0
    C64, S64 = np.cos(th) / 8.0, np.sin(th) / 8.0
    z = np.zeros((64, 64))
    BDC = np.block([[C64, z], [z, C64]])
    BDS = np.block([[S64, z], [z, S64]])
    c["bd64"] = np.ascontiguousarray(np.stack([BDC, -BDS, BDS, BDC], axis=1).astype(f))
    t = np.arange(256, dtype=np.float64)
    th = 2 * np.pi * np.outer(t, t) / 256.0
    Cp, Sp = np.cos(th) / 16.0, np.sin(th) / 16.0
    rp = np.concatenate([Cp, -Sp], axis=1).reshape(2, 128, 512).transpose(1, 0, 2)
    c["rp256"] = np.ascontiguousarray(rp.astype(f))
    cs = np.stack([Cp, Sp], axis=1).reshape(2, 128, 2, 256).transpose(1, 0, 2, 3)
    c["cs256"] = np.ascontiguousarray(cs.astype(f))
    return c


_NC_CACHE = {}


def _core_inputs(inp, b, consts):
    f = np.float32
    cvec = np.stack([np.asarray(inp["c"])[b].reshape(32, 128).T,
                     np.asarray(inp["c_ctx"]).reshape(32, 128).T], axis=-1)
    m = {
        "xs": np.ascontiguousarray(inp["x_sample"][b], dtype=f),
        "xp": np.ascontiguousarray(np.asarray(inp["x_prompt"])[2 * b:2 * b + 2].reshape(512, D), dtype=f),
        "cvec": np.ascontiguousarray(cvec, dtype=f),
        "w_ada": np.asarray(inp["w_ada"])[0], "b_ada": np.asarray(inp["b_ada"])[0].reshape(1, -1),
        "norm_w": np.ascontiguousarray(np.asarray(inp["norm_w"])[0].reshape(32, 128).T),
        "w_in": np.asarray(inp["w_in"])[0], "b_gates": np.asarray(inp["b_gates"])[0].reshape(16, 1),
        "hnorm_wT": np.ascontiguousarray(np.asarray(inp["hnorm_w"])[0].reshape(16, 128).T), "w_four": np.asarray(inp["w_four"])[0],
        "w_out": np.asarray(inp["w_out"])[0], "final_norm_w": np.asarray(inp["final_norm_w"]).reshape(1, -1),
        "state_C": np.ascontiguousarray(np.asarray(inp["state_C"])[b, 0]),
        "state_n": np.ascontiguousarray(np.asarray(inp["state_n"])[b, 0]),
        "state_m": np.ascontiguousarray(np.asarray(inp["state_m"])[b, 0]),
    }
    m.update(consts)
    return m


def kernel(**inputs):
    if "nc" not in _NC_CACHE:
        _NC_CACHE["nc"] = build_nc()
    nc = _NC_CACHE["nc"]
    consts = make_consts()
    in_maps = [_core_inputs(inputs, b, consts) for b in range(8)]
    res = run_bass_kernel_spmd(nc, in_maps, core_ids=list(range(8)))
    r = res.results
    y_sample = np.stack([r[b]["ys"] for b in range(8)], axis=0).astype(np.float32)
    y_prompt = np.concatenate([r[b]["yp"].reshape(2, TP, D) for b in range(8)], axis=0).astype(np.float32)
    new_C = np.concatenate([r[b]["new_C"] for b in range(8)], axis=0)[:, None].astype(np.float32)
    new_n = np.concatenate([r[b]["new_n"] for b in range(8)], axis=0)[:, None].astype(np.float32)
    new_m = np.concatenate([r[b]["new_m"] for b in range(8)], axis=0)[:, None].astype(np.float32)
    return (y_prompt, y_sample, new_C, new_n, new_m)
```
